# Optimizing a Trainium2 kernel written in Bass

```python
import math
import jax, jax.numpy as jnp
from jax import lax
import numpy as np


D_MODEL = 2048
BATCH = 4
SEQ = 2048
DEPTH = 2
DEC_BATCH = 128
DEC_SEQ = 4
PAST_LEN = 16384
PAGE_SIZE = 128

LRU_WIDTH = D_MODEL // 2
LRU_BLOCKS = 8
LRU_BS = LRU_WIDTH // LRU_BLOCKS
LRU_C = 8.0
CONV_W = 4
GLA_HEADS = 4
GLA_DK = D_MODEL // 4 // GLA_HEADS
GLA_DV = D_MODEL // 2 // GLA_HEADS
GLA_RANK = 16
GLA_TAU = 16.0
GLA_CHUNK = 64
MLSTM_HEADS = 4
MLSTM_WIDTH = D_MODEL // 2
MLSTM_DH = MLSTM_WIDTH // MLSTM_HEADS
MLSTM_QKV_BS = 4
MLSTM_NB = MLSTM_WIDTH // MLSTM_QKV_BS
MLSTM_CHUNK = 64
D_FF = 4 * D_MODEL
N_BRANCH = 3
EPS = 1e-6
SPLIT_SIZES = (LRU_WIDTH, GLA_HEADS * GLA_DK, GLA_HEADS * GLA_DK, GLA_HEADS * GLA_DV, GLA_RANK,
               GLA_HEADS * GLA_DV, MLSTM_WIDTH, MLSTM_WIDTH, 2 * MLSTM_HEADS, N_BRANCH * D_MODEL)
D_IN = (LRU_WIDTH + 2 * GLA_HEADS * GLA_DK + 2 * GLA_HEADS * GLA_DV + GLA_RANK
        + 2 * MLSTM_WIDTH + 2 * MLSTM_HEADS + N_BRANCH * D_MODEL)

kernel_name = 'hybrid_rglru_gla_mlstm_decoder_step'


def _split_points():
    return [int(v) for v in np.cumsum(SPLIT_SIZES)[:-1]]


def rmsnorm(x, g):
    xf = x.astype(jnp.float32)
    y = xf * lax.rsqrt(jnp.mean(xf * xf, axis=-1, keepdims=True) + EPS)
    return (y * g.astype(jnp.float32)).astype(x.dtype)


def blockdiag(x, w):
    nb, bi, bo = w.shape
    y = jnp.einsum('btnc,ncd->btnd', x.reshape(*x.shape[:-1], nb, bi), w)
    return y.reshape(*x.shape[:-1], nb * bo)


def causal_conv(x, buf, w, b):
    T = x.shape[1]
    xp = jnp.concatenate([buf.astype(x.dtype), x], axis=1)
    y = b + xp[:, 0:T] * w[0]
    for j in range(1, CONV_W):
        y = y + xp[:, j:j + T] * w[j]
    return y, xp[:, -(CONV_W - 1):]


def rglru(x, h0, w_a, b_a, w_x, b_x, lam):
    f32 = jnp.float32
    r = jax.nn.sigmoid((blockdiag(x, w_a) + b_a).astype(f32))
    i = jax.nn.sigmoid((blockdiag(x, w_x) + b_x).astype(f32))
    log_a = LRU_C * r * jax.nn.log_sigmoid(lam.astype(f32))
    a = jnp.exp(log_a)
    u = jnp.sqrt(-jnp.expm1(2.0 * log_a)) * (i * x.astype(f32))

    def step(h, au):
        a_t, u_t = au
        h = a_t * h + u_t
        return h, h

    hT, hs = lax.scan(step, h0.astype(f32), (a.swapaxes(0, 1), u.swapaxes(0, 1)))
    return hs.swapaxes(0, 1), hT


def gla_chunked(q, k, v, g, S0):
    f32 = jnp.float32
    B, T, H, DK = q.shape
    DV = v.shape[-1]
    L = math.gcd(T, GLA_CHUNK)
    nc = T // L

    def chunks(a):
        return a.astype(f32).reshape(B, nc, L, H, a.shape[-1]).transpose(1, 0, 3, 2, 4)

    qc, kc, vc, gc = chunks(q * (DK ** -0.5)), chunks(k), chunks(v), chunks(g)
    mask = jnp.tril(jnp.ones((L, L), dtype=bool))

    def step(S, inp):
        q_, k_, v_, g_ = inp
        b = jnp.cumsum(g_, axis=-2)
        qe = q_ * jnp.exp(b)
        ke = k_ * jnp.exp(-b)
        A = jnp.where(mask, jnp.einsum('bhid,bhjd->bhij', qe, ke), 0.0)
        o = jnp.einsum('bhid,bhde->bhie', qe, S) + jnp.einsum('bhij,bhje->bhie', A, v_)
        bL = b[..., -1:, :]
        S = jnp.exp(bL[..., 0, :])[..., None] * S + jnp.einsum('bhjd,bhje->bhde', k_ * jnp.exp(bL - b), v_)
        return S, o

    ST, o = lax.scan(step, S0.astype(f32), (qc, kc, vc, gc))
    return o.transpose(1, 0, 3, 2, 4).reshape(B, T, H, DV), ST


def mlstm_chunked(q, k, v, ig, fg, C0, n0, m0):
    f32 = jnp.float32
    B, T, H, DH = q.shape
    L = math.gcd(T, MLSTM_CHUNK)
    nc = T // L

    def chunks(a):
        return a.astype(f32).reshape(B, nc, L, H, a.shape[-1]).transpose(1, 0, 3, 2, 4)

    def gchunks(a):
        return a.astype(f32).reshape(B, nc, L, H).transpose(1, 0, 3, 2)

    qc, kc, vc = chunks(q), chunks(k), chunks(v)
    ic, lfc = gchunks(ig), gchunks(jax.nn.log_sigmoid(fg.astype(f32)))
    mask = jnp.tril(jnp.ones((L, L), dtype=bool))

    def step(carry, inp):
        C, n, m = carry
        q_, k_, v_, i_, lf_ = inp
        F = jnp.cumsum(lf_, axis=-1)
        Dm = jnp.where(mask, F[..., :, None] - F[..., None, :] + i_[..., None, :], -jnp.inf)
        inter = F + m[..., None]
        mt = jnp.maximum(inter, jnp.max(Dm, axis=-1))
        W = jnp.exp(Dm - mt[..., None])
        ci = jnp.exp(inter - mt)
        S = jnp.einsum('bhid,bhjd->bhij', q_, k_) * W
        num = ci[..., None] * jnp.einsum('bhid,bhde->bhie', q_, C) + jnp.einsum('bhij,bhje->bhie', S, v_)
        den = ci * jnp.einsum('bhid,bhd->bhi', q_, n) + jnp.sum(S, axis=-1)
        h = num / jnp.maximum(jnp.abs(den), jnp.exp(-mt))[..., None]
        FL = F[..., -1]
        dj = FL[..., None] - F + i_
        m_new = jnp.maximum(FL + m, jnp.max(dj, axis=-1))
        cs = jnp.exp(FL + m - m_new)
        wj = jnp.exp(dj - m_new[..., None])
        C = cs[..., None, None] * C + jnp.einsum('bhj,bhjd,bhje->bhde', wj, k_, v_)
        n = cs[..., None] * n + jnp.einsum('bhj,bhjd->bhd', wj, k_)
        return (C, n, m_new), h

    (CT, nT, mT), h = lax.scan(step, (C0.astype(f32), n0.astype(f32), m0.astype(f32)), (qc, kc, vc, ic, lfc))
    return h.transpose(1, 0, 3, 2, 4).reshape(B, T, H, DH), CT, nT, mT


def mixer(xn, st, P):
    lru_h, lru_conv, gla_S, ml_C, ml_n, ml_m, ml_conv = st
    B, T, _ = xn.shape
    dt = xn.dtype
    proj = xn @ P['w_in']
    (lru_x, g_q, g_k, g_v, g_lr, g_gate, m_x, m_o, m_if, mg) = jnp.split(proj, _split_points(), axis=-1)
    u, lru_conv_new = causal_conv(lru_x, lru_conv, P['lru_conv_w'], P['lru_conv_b'])
    y_lru, lru_h_new = rglru(u, lru_h, P['lru_w_a'], P['lru_b_a'], P['lru_w_x'], P['lru_b_x'], P['lru_lam'])
    y_lru = y_lru.astype(dt)
    q = g_q.reshape(B, T, GLA_HEADS, GLA_DK)
    k = g_k.reshape(B, T, GLA_HEADS, GLA_DK)
    v = g_v.reshape(B, T, GLA_HEADS, GLA_DV)
    logdec = jax.nn.log_sigmoid((g_lr @ P['gla_w_g2'] + P['gla_b_g']).astype(jnp.float32)) / GLA_TAU
    o, gla_S_new = gla_chunked(q, k, v, logdec.reshape(B, T, GLA_HEADS, GLA_DK), gla_S)
    y_gla = rmsnorm(o.astype(dt), P['gla_g_norm']).reshape(B, T, GLA_HEADS * GLA_DV) * jax.nn.silu(g_gate)
    mc, ml_conv_new = causal_conv(m_x, ml_conv, P['ml_conv_w'], P['ml_conv_b'])
    mc = jax.nn.silu(mc)
    mq = blockdiag(mc, P['ml_w_q']).reshape(B, T, MLSTM_HEADS, MLSTM_DH)
    mk = blockdiag(mc, P['ml_w_k']).reshape(B, T, MLSTM_HEADS, MLSTM_DH) * (MLSTM_DH ** -0.5)
    mv = blockdiag(m_x, P['ml_w_v']).reshape(B, T, MLSTM_HEADS, MLSTM_DH)
    gates = (m_if + P['ml_b_if']).astype(jnp.float32)
    h, C_new, n_new, m_new = mlstm_chunked(mq, mk, mv, gates[..., :MLSTM_HEADS], gates[..., MLSTM_HEADS:],
                                           ml_C, ml_n, ml_m)
    y_ml = jax.nn.sigmoid(m_o) * rmsnorm(h.astype(dt), P['ml_g_norm']).reshape(B, T, MLSTM_WIDTH)
    gate = jax.nn.sigmoid(mg).reshape(B, T, N_BRANCH, D_MODEL)
    merged = (gate[:, :, 0] * (y_lru @ P['w_br_lru'])
              + gate[:, :, 1] * (y_gla @ P['w_br_gla'])
              + gate[:, :, 2] * (y_ml @ P['w_br_ml']))
    new_st = (lru_h_new.astype(dt), lru_conv_new.astype(dt), gla_S_new.astype(dt), C_new.astype(dt),
              n_new.astype(dt), m_new.astype(dt), ml_conv_new.astype(dt))
    return merged @ P['w_out'], new_st


def run_trunk(x, c, states, layer_params, g_final):
    new_states = []
    for l in range(DEPTH):
        P = layer_params[l]
        mod = jax.nn.silu(c) @ P['w_ada'] + P['b_ada']
        sh1, sc1, gt1, sh2, sc2, gt2 = [m[:, None, :] for m in jnp.split(mod, 6, axis=-1)]
        xn = rmsnorm(x, P['g_norm1']) * (1.0 + sc1) + sh1
        mix, st = mixer(xn, states[l], P)
        x = x + gt1 * mix
        xn = rmsnorm(x, P['g_norm2']) * (1.0 + sc2) + sh2
        x = x + gt2 * (jnp.square(jax.nn.relu(xn @ P['w_ff1'])) @ P['w_ff2'])
        new_states.append(st)
    y = rmsnorm(x, g_final)
    stacked = [jnp.stack([s[i] for s in new_states]) for i in range(7)]
    return y, stacked


def setup_inputs(seed: int = 0) -> dict:
    key = jax.random.key(seed)
    ks = jax.random.split(key, 48)
    f32 = jnp.float32

    def nrm(i, shape, s):
        return jax.random.normal(ks[i], shape, f32) * s

    a0 = jax.random.uniform(ks[47], (DEPTH, LRU_WIDTH), f32, minval=0.9, maxval=0.999)
    a1 = a0 ** (1.0 / LRU_C)
    b_if = jnp.concatenate([nrm(44, (DEPTH, MLSTM_HEADS), 0.1) - 1.0,
                            3.0 + nrm(45, (DEPTH, MLSTM_HEADS), 0.5)], axis=-1)
    return {
        'x_prompt': nrm(0, (BATCH, SEQ, D_MODEL), 1.0),
        'x_sample': nrm(1, (DEC_BATCH, DEC_SEQ, D_MODEL), 1.0),
        'c_prompt': nrm(2, (BATCH, D_MODEL), 1.0),
        'c_sample': nrm(3, (DEC_BATCH, D_MODEL), 1.0),
        'state_lru_h': nrm(4, (DEPTH, DEC_BATCH, LRU_WIDTH), 0.5),
        'state_lru_conv': nrm(5, (DEPTH, DEC_BATCH, CONV_W - 1, LRU_WIDTH), 1.0),
        'state_gla': nrm(6, (DEPTH, DEC_BATCH, GLA_HEADS, GLA_DK, GLA_DV), 1.0),
        'state_mlstm_C': nrm(7, (DEPTH, DEC_BATCH, MLSTM_HEADS, MLSTM_DH, MLSTM_DH), 0.5),
        'state_mlstm_n': nrm(8, (DEPTH, DEC_BATCH, MLSTM_HEADS, MLSTM_DH), 0.5),
        'state_mlstm_m': nrm(9, (DEPTH, DEC_BATCH, MLSTM_HEADS), 0.5),
        'state_mlstm_conv': nrm(10, (DEPTH, DEC_BATCH, CONV_W - 1, MLSTM_WIDTH), 1.0),
        'w_ada': nrm(11, (DEPTH, D_MODEL, 6 * D_MODEL), 0.2 * D_MODEL ** -0.5),
        'b_ada': nrm(12, (DEPTH, 6 * D_MODEL), 0.02),
        'g_norm1': 1.0 + nrm(13, (DEPTH, D_MODEL), 0.02),
        'g_norm2': 1.0 + nrm(14, (DEPTH, D_MODEL), 0.02),
        'w_in': nrm(15, (DEPTH, D_MODEL, D_IN), D_MODEL ** -0.5),
        'lru_conv_w': nrm(16, (DEPTH, CONV_W, LRU_WIDTH), CONV_W ** -0.5),
        'lru_conv_b': nrm(17, (DEPTH, LRU_WIDTH), 0.02),
        'lru_w_a': nrm(18, (DEPTH, LRU_BLOCKS, LRU_BS, LRU_BS), LRU_BS ** -0.5),
        'lru_b_a': nrm(19, (DEPTH, LRU_WIDTH), 0.02),
        'lru_w_x': nrm(20, (DEPTH, LRU_BLOCKS, LRU_BS, LRU_BS), LRU_BS ** -0.5),
        'lru_b_x': nrm(21, (DEPTH, LRU_WIDTH), 0.02),
        'lru_lam': jnp.log(a1) - jnp.log1p(-a1),
        'gla_w_g2': nrm(22, (DEPTH, GLA_RANK, GLA_HEADS * GLA_DK), GLA_RANK ** -0.5),
        'gla_b_g': nrm(23, (DEPTH, GLA_HEADS * GLA_DK), 0.5),
        'gla_g_norm': 1.0 + nrm(24, (DEPTH, GLA_DV), 0.02),
        'ml_conv_w': nrm(25, (DEPTH, CONV_W, MLSTM_WIDTH), CONV_W ** -0.5),
        'ml_conv_b': nrm(26, (DEPTH, MLSTM_WIDTH), 0.02),
        'ml_w_q': nrm(27, (DEPTH, MLSTM_NB, MLSTM_QKV_BS, MLSTM_QKV_BS), MLSTM_QKV_BS ** -0.5),
        'ml_w_k': nrm(28, (DEPTH, MLSTM_NB, MLSTM_QKV_BS, MLSTM_QKV_BS), MLSTM_QKV_BS ** -0.5),
        'ml_w_v': nrm(29, (DEPTH, MLSTM_NB, MLSTM_QKV_BS, MLSTM_QKV_BS), MLSTM_QKV_BS ** -0.5),
        'ml_b_if': b_if,
        'ml_g_norm': 1.0 + nrm(30, (DEPTH, MLSTM_DH), 0.02),
        'w_br_lru': nrm(31, (DEPTH, LRU_WIDTH, D_MODEL), LRU_WIDTH ** -0.5),
        'w_br_gla': nrm(32, (DEPTH, GLA_HEADS * GLA_DV, D_MODEL), (GLA_HEADS * GLA_DV) ** -0.5),
        'w_br_ml': nrm(33, (DEPTH, MLSTM_WIDTH, D_MODEL), MLSTM_WIDTH ** -0.5),
        'w_out': nrm(34, (DEPTH, D_MODEL, D_MODEL), D_MODEL ** -0.5),
        'w_ff1': nrm(35, (DEPTH, D_MODEL, D_FF), D_MODEL ** -0.5),
        'w_ff2': nrm(36, (DEPTH, D_FF, D_MODEL), D_FF ** -0.5),
        'g_final': 1.0 + nrm(37, (D_MODEL,), 0.02),
    }


def reference(x_prompt, x_sample, c_prompt, c_sample, state_lru_h, state_lru_conv, state_gla,
              state_mlstm_C, state_mlstm_n, state_mlstm_m, state_mlstm_conv,
              w_ada, b_ada, g_norm1, g_norm2, w_in, lru_conv_w, lru_conv_b, lru_w_a, lru_b_a,
              lru_w_x, lru_b_x, lru_lam, gla_w_g2, gla_b_g, gla_g_norm, ml_conv_w, ml_conv_b,
              ml_w_q, ml_w_k, ml_w_v, ml_b_if, ml_g_norm, w_br_lru, w_br_gla, w_br_ml, w_out,
              w_ff1, w_ff2, g_final):
    layer_params = [dict(w_ada=w_ada[l], b_ada=b_ada[l], g_norm1=g_norm1[l], g_norm2=g_norm2[l],
                         w_in=w_in[l], lru_conv_w=lru_conv_w[l], lru_conv_b=lru_conv_b[l],
                         lru_w_a=lru_w_a[l], lru_b_a=lru_b_a[l], lru_w_x=lru_w_x[l], lru_b_x=lru_b_x[l],
                         lru_lam=lru_lam[l], gla_w_g2=gla_w_g2[l], gla_b_g=gla_b_g[l],
                         gla_g_norm=gla_g_norm[l], ml_conv_w=ml_conv_w[l], ml_conv_b=ml_conv_b[l],
                         ml_w_q=ml_w_q[l], ml_w_k=ml_w_k[l], ml_w_v=ml_w_v[l], ml_b_if=ml_b_if[l],
                         ml_g_norm=ml_g_norm[l], w_br_lru=w_br_lru[l], w_br_gla=w_br_gla[l],
                         w_br_ml=w_br_ml[l], w_out=w_out[l], w_ff1=w_ff1[l], w_ff2=w_ff2[l])
                    for l in range(DEPTH)]
    dt = x_prompt.dtype
    zero = (jnp.zeros((BATCH, LRU_WIDTH), dt), jnp.zeros((BATCH, CONV_W - 1, LRU_WIDTH), dt),
            jnp.zeros((BATCH, GLA_HEADS, GLA_DK, GLA_DV), dt),
            jnp.zeros((BATCH, MLSTM_HEADS, MLSTM_DH, MLSTM_DH), dt),
            jnp.zeros((BATCH, MLSTM_HEADS, MLSTM_DH), dt), jnp.zeros((BATCH, MLSTM_HEADS), dt),
            jnp.zeros((BATCH, CONV_W - 1, MLSTM_WIDTH), dt))
    prompt_states = [zero for _ in range(DEPTH)]
    sample_states = [(state_lru_h[l], state_lru_conv[l], state_gla[l], state_mlstm_C[l], state_mlstm_n[l],
                      state_mlstm_m[l], state_mlstm_conv[l]) for l in range(DEPTH)]
    y_prompt, ps = run_trunk(x_prompt, c_prompt, prompt_states, layer_params, g_final)
    y_sample, ss = run_trunk(x_sample, c_sample, sample_states, layer_params, g_final)
    p_lru_h, p_lru_conv, p_gla, p_C, p_n, p_m, p_ml_conv = ps
    s_lru_h, s_lru_conv, s_gla, s_C, s_n, s_m, s_ml_conv = ss
    return (y_prompt, y_sample, p_lru_h, p_lru_conv, p_gla, p_C, p_n, p_m, p_ml_conv,
            s_lru_h, s_lru_conv, s_gla, s_C, s_n, s_m, s_ml_conv)
```

```python
import types
import numpy as np
from contextlib import ExitStack
import concourse.bass as bass
import concourse.mybir as mybir
from concourse.bass_utils import run_bass_kernel_spmd

F32 = mybir.dt.float32
BF16 = mybir.dt.bfloat16
AF = mybir.ActivationFunctionType
ALU = mybir.AluOpType

D = 2048
KC = 16
DIN = 12312
NL = 2
EPS = 1e-6
NCORE = 8
GS = 16
TS = 4
O_LRUX, O_GQ, O_GK, O_GV, O_GLR, O_GG, O_MX, O_MO, O_MIF, O_MG = 0, 1024, 1536, 2048, 3072, 3088, 4112, 5136, 6160, 6168

N_PROMPT_TILES = 4
RUN_SAMPLE = True
NT_MAX = 460


def prm_layout():
    off = {}
    n = 0
    for l in range(NL):
        for name, w in (("g1", 16), ("g2", 16), ("bada", 96), ("lcw", 32), ("lcb", 8), ("lba", 8), ("lbx", 8),
                        ("lam", 8), ("gbg", 4), ("ggn", 2), ("mcw", 32), ("mcb", 8), ("mbif", 2), ("mgn", 2)):
            off[(name, l)] = n
            n += w
    off[("gf", 0)] = n
    n += 16
    return off, n


CONST_COLS = 128 + 128 + 64 + 16 + 4 * 128 + 4 + 128


def _freeze(fn):
    if fn.__closure__ is None:
        return fn
    cells = []
    for c in fn.__closure__:
        try:
            cells.append(types.CellType(c.cell_contents))
        except ValueError:
            cells.append(c)
    return types.FunctionType(fn.__code__, fn.__globals__, fn.__name__, fn.__defaults__, tuple(cells))


class Res:
    __slots__ = ("name", "w", "r", "sem", "cnt")

    def __init__(self, name):
        self.name = name
        self.w = None
        self.r = {}
        self.sem = None
        self.cnt = 0


class Prog:
    ENG = ("pe", "act", "dve", "pool", "sp")

    def __init__(self, nc, stack):
        self.nc = nc
        self.stack = stack
        self.ins = {e: [] for e in self.ENG}
        self.sems = {e: stack.enter_context(nc.semaphore("s_" + e)) for e in self.ENG}
        self.store_toks = []
        self.nsem = 0

    def _deps(self, rd, wr):
        deps = set()
        for r in rd:
            if r.w is not None:
                deps.add(r.w)
        for w in wr:
            if w.w is not None:
                deps.add(w.w)
            deps.update(w.r.values())
        return deps

    def op(self, eng, fn, rd=(), wr=()):
        i = len(self.ins[eng])
        tok = ("c", eng, i)
        deps = self._deps(rd, wr)
        if eng == "pe":
            deps = {d for d in deps if not (d[0] == "c" and d[1] == "pe")}
        self.ins[eng].append((_freeze(fn), deps, None))
        for r in rd:
            r.r[eng] = tok
        for w in wr:
            w.w = tok
            w.r = {}
        return tok

    def dma(self, q, fn, rd=(), wr=(), own=None, store=False):
        if own.sem is None:
            own.sem = self.stack.enter_context(self.nc.semaphore("d%d" % self.nsem))
            self.nsem += 1
        deps = self._deps(rd, wr)
        own.cnt += 1
        tok = ("d", own, own.cnt * 16)
        self.ins[q].append((_freeze(fn), deps, own))
        for r in rd:
            r.r["dma" + own.name] = tok
        for w in wr:
            w.w = tok
            w.r = {}
        if store:
            self.store_toks.append(tok)
        return tok

    def alias_after(self, new, old):
        acc = {}
        for o in old:
            toks = list(o.r.items())
            if o.w is not None:
                toks.append(("w" + o.name, o.w))
            for k, t in toks:
                if t[0] == "c":
                    key = t[1]
                    if key not in acc or acc[key][2] < t[2]:
                        acc[key] = t
                else:
                    key = "d" + t[1].name
                    if key not in acc or acc[key][2] < t[2]:
                        acc[key] = t
        for n_ in new:
            n_.w = None
            n_.r = dict(acc)

    def emit(self):
        nc = self.nc
        needed = {e: set() for e in self.ENG}
        for e in self.ENG:
            for (_, deps, _) in self.ins[e]:
                for d in deps:
                    if d[0] == "c":
                        needed[d[1]].add(d[2])
        mval = {}
        for e in self.ENG:
            for rank, i in enumerate(sorted(needed[e])):
                mval[(e, i)] = rank + 1
        final = list(self.store_toks)

        def run(e, eng):
            seen = {}

            def wait(sem, val):
                k = id(sem)
                if seen.get(k, 0) >= val:
                    return
                seen[k] = val
                eng.wait_ge(sem, val)

            def wait_all(deps):
                best = {}
                for d in deps:
                    sem, val = (self.sems[d[1]], mval[(d[1], d[2])]) if d[0] == "c" else (d[1].sem, d[2])
                    k = id(sem)
                    if k not in best or best[k][1] < val:
                        best[k] = (sem, val)
                for sem, val in best.values():
                    wait(sem, val)

            for i, (fn, deps, own) in enumerate(self.ins[e]):
                wait_all(deps)
                inst = fn(eng)
                if own is not None:
                    inst.then_inc(own.sem, 16)
                elif (e, i) in mval:
                    inst.then_inc(self.sems[e], 1)
            if e == "sp":
                wait_all(final)

        with nc.Block() as block:
            @block.tensor
            def _(eng):
                run("pe", eng)

            @block.scalar
            def _(eng):
                run("act", eng)

            @block.vector
            def _(eng):
                run("dve", eng)

            @block.gpsimd
            def _(eng):
                run("pool", eng)

            @block.sync
            def _(eng):
                run("sp", eng)


class Ring:
    def __init__(self, views):
        self.v = views
        self.i = 0

    def get(self):
        v = self.v[self.i % len(self.v)]
        self.i += 1
        return v


def build_program():
    nc = bass.Bass("TRN2", target_bir_lowering=False)
    poff, NPRM = prm_layout()

    def din(name, shape):
        return nc.dram_tensor(name, list(shape), F32, kind="ExternalInput").ap()

    def dout(name, shape):
        return nc.dram_tensor(name, list(shape), F32, kind="ExternalOutput").ap()

    NPT = 2048
    i_xp = din("xp", [128, KC, NPT])
    i_xs = din("xs", [128, KC, GS * TS])
    i_ct = din("ct", [128, KC, 1 + GS])
    i_prm = din("prm", [128, NPRM])
    i_cst = din("cst", [128, CONST_COLS])
    i_hst = din("s_hst", [128, NL, 8, GS])
    i_hist = din("s_hist", [128, NL, 2, 8, GS, 3])
    i_mc = din("s_m", [128, NL, GS])
    i_sn = din("s_n", [128, NL, GS, 4, 2])
    i_sgla = din("s_gla", [NL, GS, 4, 128, 256])
    i_sC = din("s_C", [NL, GS, 4, 256, 256])
    w_all = din("w_all", [NT_MAX, 128, 4096])
    w_la = din("lru_w_a", [NL, 8, 128, 128])
    w_lx = din("lru_w_x", [NL, 8, 128, 128])
    w_g2 = din("gla_w_g2", [NL, 16, 512])
    w_mq = din("ml_wq", [NL, 8, 128, 128])
    w_mk = din("ml_wk", [NL, 8, 128, 128])
    w_mv = din("ml_wv", [NL, 8, 128, 128])

    o_yp = dout("o_yp", [128, KC, NPT])
    o_ys = dout("o_ys", [128, KC, GS * TS])
    o_hst = [dout("o_hst%d" % i, [128, NL, 8, g]) for i, g in enumerate((1, GS))]
    o_hist = [dout("o_hist%d" % i, [128, NL, 2, 8, g, 3]) for i, g in enumerate((1, GS))]
    o_m = [dout("o_m%d" % i, [128, NL, g]) for i, g in enumerate((1, GS))]
    o_n = [dout("o_n%d" % i, [128, NL, g, 4, 2]) for i, g in enumerate((1, GS))]
    o_gla = [dout("o_gla%d" % i, [NL, g, 4, 128, 256]) for i, g in enumerate((1, GS))]
    o_C = [dout("o_C%d" % i, [NL, g, 4, 256, 256]) for i, g in enumerate((1, GS))]

    with ExitStack() as st:
        P = Prog(nc, st)
        ARENA_W = 52480
        arena = st.enter_context(nc.sbuf_tensor("arena", [128, ARENA_W], F32))
        ps_t = [st.enter_context(nc.psum_tensor("ps%d" % i, [128, 512], F32)) for i in range(8)]
        ps_res = [Res("ps%d" % i) for i in range(8)]
        ps_i = [0]

        def psum():
            i = ps_i[0] % 8
            ps_i[0] += 1
            return ps_t[i], ps_res[i]

        top = [0]
        uid = [0]

        def carve(words):
            a = top[0]
            top[0] += (words + 7) // 8 * 8
            assert top[0] <= ARENA_W, ("SBUF arena overflow", top[0])
            return a

        def view(shape, dt, name=None):
            n = int(np.prod(shape))
            words = n if dt == F32 else (n + 1) // 2
            a = carve(words)
            ap = arena[:, a:a + words]
            if dt != F32:
                ap = ap.bitcast(dt)[:, 0:n]
            if len(shape) >= 2:
                names = ["d%d" % i for i in range(len(shape))]
                pat = "p (" + " ".join(names) + ") -> p " + " ".join(names)
                ap = ap.rearrange(pat, **{names[i]: int(shape[i]) for i in range(1, len(shape))})
            uid[0] += 1
            return ap, Res(name or "v%d" % uid[0])

        def ring(cnt, shape, dt, name):
            return Ring([view(shape, dt, "%s%d" % (name, i)) for i in range(cnt)])

        cst, Rcst = view([CONST_COLS], F32, "cst")
        prm, Rprm = view([NPRM], F32, "prm")
        drv, Rdrv = view([NL, 40], F32, "drv")
        modT0, Rmod0 = view([NL, 96], F32, "modT0")
        modA0, RmodA0 = view([NL, 2, 16], F32, "modA0")
        c0_ = 0
        mask128 = cst[:, c0_:c0_ + 128]; c0_ += 128
        ones_f = cst[:, c0_:c0_ + 128]; c0_ += 128
        mask64 = cst[:, c0_:c0_ + 64]; c0_ += 64
        gmask = cst[:, c0_:c0_ + 16]; c0_ += 16
        sel = [cst[:, c0_ + 128 * h:c0_ + 128 * (h + 1)] for h in range(4)]; c0_ += 512
        E4 = cst[:, c0_:c0_ + 4]; c0_ += 4
        ident = cst[:, c0_:c0_ + 128]; c0_ += 128
        ones_b, Rob = view([128], BF16, "ones_b")
        wbuf = [view([KC, 256], BF16, "wbuf%d" % i) for i in range(3)]
        wring = Ring(wbuf)
        sring = ring(6, [128], BF16, "smallw")
        wg2, Rwg2 = view([512], BF16, "wg2")

        def pc(name, l, j=0, n=1):
            o = poff[(name, l)] + j
            return prm[:, o:o + n]

        P.dma("sp", lambda e: e.dma_start(out=cst, in_=i_cst), wr=[Rcst], own=Rcst)
        P.dma("sp", lambda e: e.dma_start(out=prm, in_=i_prm), wr=[Rprm], own=Rprm)
        P.op("dve", lambda e: e.memset(ones_b, 1.0), wr=[Rob])
        P.op("dve", lambda e: e.memset(drv, 0.0), wr=[Rdrv])
        for l in range(NL):
            P.op("act", lambda e, l=l: e.activation(out=drv[:, l, 0:8], in_=pc("lam", l, 0, 8), func=AF.Exp, scale=-1.0), rd=[Rprm, Rdrv], wr=[Rdrv])
            P.op("act", lambda e, l=l: e.activation(out=drv[:, l, 0:8], in_=drv[:, l, 0:8], func=AF.Ln, bias=1.0), rd=[Rdrv], wr=[Rdrv])
            P.op("dve", lambda e, l=l: e.tensor_scalar(out=drv[:, l, 8:16], in0=drv[:, l, 0:8], scalar1=-16.0, scalar2=None, op0=ALU.mult), rd=[Rdrv], wr=[Rdrv])
            P.op("dve", lambda e, l=l: e.tensor_scalar(out=drv[:, l, 0:8], in0=drv[:, l, 0:8], scalar1=-8.0, scalar2=None, op0=ALU.mult), rd=[Rdrv], wr=[Rdrv])
            P.op("dve", lambda e, l=l: e.tensor_scalar(out=drv[:, l, 16:20], in0=pc("gbg", l, 0, 4), scalar1=-1.0, scalar2=None, op0=ALU.mult), rd=[Rprm, Rdrv], wr=[Rdrv])
            P.op("dve", lambda e, l=l: e.tensor_scalar(out=drv[:, l, 20:21], in0=pc("mbif", l, 1, 1), scalar1=-1.0, scalar2=None, op0=ALU.mult), rd=[Rprm, Rdrv], wr=[Rdrv])

        WT = {}
        WT_LIST = []

        def wload(name, l, r0, nr, c0, ncn):
            key = (name, l, r0, nr, c0, ncn)
            if key not in WT:
                WT[key] = len(WT_LIST)
                WT_LIST.append(key)
            idx = WT[key]
            assert idx < NT_MAX
            kcn = nr // 128
            buf, R = wring.get()
            flat = buf.rearrange("p k n -> p (k n)")
            P.dma("pool", lambda e: e.dma_start(out=flat[:, 0:kcn * ncn], in_=w_all[idx][:, 0:kcn * ncn], max_dma_last_dim=8192), wr=[R], own=R)
            return flat[:, 0:kcn * ncn].rearrange("p (k n) -> p k n", n=ncn), R

        def sload(src2):
            buf, R = sring.get()
            P.dma("pool", lambda e: e.dma_start(out=buf, in_=src2), wr=[R], own=R)
            return buf, R

        top_setup = top[0]
        modT, Rmod = view([NL, 96, 1 + GS], F32, "modT")
        modA, RmodA = view([NL, 2, 16, 1 + GS], F32, "modA")
        top_mods = top[0]
        ctf, Rctf = view([KC, 1 + GS], F32, "ctf")
        ctb, Rctb = view([KC, 1 + GS], BF16, "ctb")
        P.dma("sp", lambda e: e.dma_start(out=ctf, in_=i_ct), wr=[Rctf], own=Rctf)
        P.op("act", lambda e: e.activation(out=ctb, in_=ctf, func=AF.Silu), rd=[Rctf], wr=[Rctb])
        NC_ = 1 + GS
        for l in range(NL):
            for t4 in range(48):
                buf, Rw = wload("w_ada", l, 0, D, t4 * 256, 256)
                for j in range(2):
                    oc = t4 * 2 + j
                    pt, Rp = psum()
                    for kc in range(KC):
                        P.op("pe", lambda e, kc=kc, j=j, pt=pt, buf=buf: e.matmul(pt[:, 0:NC_], lhsT=buf[:, kc, j * 128:(j + 1) * 128], rhs=ctb[:, kc, :],
                                                                                    start=(kc == 0), stop=(kc == KC - 1)), rd=[Rw, Rctb], wr=[Rp])
                    P.op("act", lambda e, oc=oc, l=l, pt=pt: e.activation(out=modT[:, l, oc, :], in_=pt[:, 0:NC_], func=AF.Identity, bias=pc("bada", l, oc, 1), scale=1.0),
                         rd=[Rp, Rprm], wr=[Rmod])
            for nrm, (gname, sck) in enumerate((("g1", 1), ("g2", 4))):
                P.op("dve", lambda e, l=l, nrm=nrm, sck=sck: e.tensor_scalar(out=modA[:, l, nrm], in0=modT[:, l, sck * 16:(sck + 1) * 16, :], scalar1=1.0, scalar2=None, op0=ALU.add),
                     rd=[Rmod], wr=[RmodA])
                P.op("dve", lambda e, l=l, nrm=nrm, gname=gname: e.tensor_tensor(out=modA[:, l, nrm], in0=modA[:, l, nrm],
                                                                                in1=pc(gname, l, 0, 16).unsqueeze(2).to_broadcast([128, 16, NC_]), op=ALU.mult),
                     rd=[RmodA, Rprm], wr=[RmodA])
        P.op("dve", lambda e: e.tensor_copy(out=modT0, in_=modT[:, :, :, 0]), rd=[Rmod], wr=[Rmod0])
        P.op("dve", lambda e: e.tensor_copy(out=modA0, in_=modA[:, :, :, :, 0]), rd=[RmodA], wr=[RmodA0])
        setup_res = [Rctf, Rctb]

        def run_phase(is_sample, prev_res):
            top[0] = top_mods if is_sample else top_setup
            allres = []
            if is_sample:
                N, G, T = GS * TS, GS, TS
                chunks = [(0, GS, TS)]
                ntiles = 1
                mcol0 = 1
                oi = 1
            else:
                N, G, T = 512, 1, 512
                chunks = [(128 * c, 1, 128) for c in range(4)]
                ntiles = N_PROMPT_TILES
                mcol0 = 0
                oi = 0
            NCH = len(chunks)
            CN = chunks[0][1] * chunks[0][2]

            def V(shape, dt, name):
                ap, R = view(shape, dt, name)
                allres.append(R)
                return ap, R

            def RING(cnt, shape, dt, name):
                r = ring(cnt, shape, dt, name)
                allres.extend(R for _, R in r.v)
                return r

            X = [V([N], F32, "x%d" % k) for k in range(KC)]
            XN = [V([N], BF16, "xn%d" % k) for k in range(KC)]
            ov0 = top[0]
            YB = [[V([N], BF16, "yb%d_%d" % (b, k)) for k in range(8)] for b in range(3)]
            MG = [V([N], BF16, "mg%d" % k) for k in range(KC)]
            ov1 = top[0]
            top[0] = ov0
            HG = [[V([N], BF16, "hg%d_%d" % (i, k)) for k in range(8)] for i in range(2)]
            top[0] = max(ov1, top[0])
            mix_res = [R for b in YB for _, R in b] + [R for _, R in MG]
            hg_res = [R for hb in HG for _, R in hb]
            hst, Rhst = V([NL, 8, G], F32, "hst")
            hist, Rhist = V([NL, 2, 8, G, 3], F32, "hist")
            Fc, RFc = V([NL, G], F32, "Fc")
            Mc, RMc = V([NL, G], F32, "Mc")
            mout, Rmout = V([NL, G], F32, "mout")
            nout, Rnout = V([NL, G, 4, 2], F32, "nout")
            if is_sample:
                Sst = [V([G, 256], F32, "Sst")]
                Cst = [V([G, 2, 257], F32, "Cst")]
                nin, Rnin = V([NL, G, 4, 2], F32, "nin")
            else:
                Sst = [V([4, 256], F32, "S%d" % l) for l in range(NL)]
                Cst = [V([4, 2, 257], F32, "C%d" % l) for l in range(NL)]
            Sbf, RSbf = V([G, 256], BF16, "Sbf")
            Cbf, RCbf = V([G, 2, 257], BF16, "Cbf")
            nbc, Rnbc = V([G, 2, 128], BF16, "nbc")
            rF = RING(6, [N], F32, "rF")
            rB = RING(12, [N], BF16, "rB")
            rf = RING(8, [CN], F32, "rf")
            rb = RING(4, [128], BF16, "rb")
            rcv = RING(2, [G, T + 3], F32, "rcv")
            vtm = [V([257], BF16, "vtm%d" % c) for c in range(NCH)]
            ktm = [V([256], BF16, "ktm%d" % c) for c in range(NCH)]
            mvt = [V([257], BF16, "mvt%d" % c) for c in range(NCH)]
            rbc = RING(2, [3, CN], F32, "rbc")
            rkm = RING(2, [chunks[0][1], 256], BF16, "rkm")
            rtc = RING(2, [257], F32, "rtc")
            rk256 = RING(2, [256], BF16, "rk256")
            rcol_r = RING(2, [4], F32, "rcol")
            rRs = [V([3, CN], F32, "Rrow%d" % c) for c in range(NCH)]
            rcolv = [V([4], F32, "rcolv%d" % c) for c in range(NCH)]
            rstdN = V([N], F32, "rstdN")
            glrv = V([N], BF16, "glr")
            P.alias_after(allres, prev_res)

            if is_sample:
                P.dma("sp", lambda e: e.dma_start(out=hst, in_=i_hst), wr=[Rhst], own=Rhst)
                P.dma("sp", lambda e: e.dma_start(out=hist, in_=i_hist), wr=[Rhist], own=Rhist)
                P.dma("sp", lambda e: e.dma_start(out=Mc, in_=i_mc), wr=[RMc], own=RMc)
                P.dma("sp", lambda e: e.dma_start(out=nin, in_=i_sn), wr=[Rnin], own=Rnin)
            else:
                P.op("dve", lambda e: e.memset(hst, 0.0), wr=[Rhst])
                P.op("dve", lambda e: e.memset(hist, 0.0), wr=[Rhist])
                P.op("dve", lambda e: e.memset(Mc, 0.0), wr=[RMc])
                for l in range(NL):
                    P.op("dve", lambda e, l=l: e.memset(Sst[l][0], 0.0), wr=[Sst[l][1]])
                    P.op("dve", lambda e, l=l: e.memset(Cst[l][0], 0.0), wr=[Cst[l][1]])
            P.op("dve", lambda e: e.memset(Fc, 0.0), wr=[RFc])
            for c in range(NCH):
                P.op("dve", lambda e, c=c: e.memset(mvt[c][0][:, 256:257], 1.0), wr=[mvt[c][1]])

            def gt3(ap2):
                return ap2.rearrange("p (g t) -> p g t", t=T)

            def rmsnorm_rstd(srcs, dst=None):
                n = srcs[0][0].shape[-1]
                pt, Rp = psum()
                for i, (ap, R) in enumerate(srcs):
                    sq, Rsq = (rF.get() if n == N else rf.get())
                    P.op("act", lambda e, ap=ap, sq=sq: e.activation(out=sq[:, 0:n], in_=ap, func=AF.Square), rd=[R], wr=[Rsq])
                    P.op("pe", lambda e, sq=sq, i=i, pt=pt: e.matmul(pt[:, 0:n], lhsT=ones_f, rhs=sq[:, 0:n], start=(i == 0), stop=(i == len(srcs) - 1)),
                         rd=[Rsq, Rcst], wr=[Rp])
                rs, Rrs = dst if dst is not None else (rF.get() if n == N else rf.get())
                dim = 128.0 * len(srcs)
                P.op("act", lambda e, pt=pt, rs=rs: e.activation(out=rs[:, 0:n], in_=pt[:, 0:n], func=AF.Sqrt, scale=1.0 / dim, bias=EPS), rd=[Rp], wr=[Rrs])
                P.op("dve", lambda e, rs=rs: e.reciprocal(out=rs[:, 0:n], in_=rs[:, 0:n]), rd=[Rrs], wr=[Rrs])
                return rs[:, 0:n], Rrs

            def norm_mod(l, nrm):
                rs, Rrs = rmsnorm_rstd(X, rstdN)
                shk = 0 if nrm == 0 else 3
                for kc in range(KC):
                    t1, Rt1 = rF.get()
                    P.op("dve", lambda e, kc=kc, t1=t1: e.tensor_tensor(out=t1, in0=X[kc][0], in1=rs, op=ALU.mult), rd=[X[kc][1], Rrs], wr=[Rt1])
                    if G == 1:
                        P.op("act", lambda e, kc=kc, t1=t1: e.activation(out=XN[kc][0], in_=t1, func=AF.Identity, scale=modA0[:, l, nrm, kc:kc + 1],
                                                                          bias=modT0[:, l, shk * 16 + kc:shk * 16 + kc + 1]),
                             rd=[Rt1, RmodA0, Rmod0], wr=[XN[kc][1]])
                    else:
                        P.op("dve", lambda e, kc=kc, t1=t1: e.tensor_tensor(out=gt3(t1), in0=gt3(t1),
                                                                             in1=modA[:, l, nrm, kc, mcol0:mcol0 + G].unsqueeze(2).to_broadcast([128, G, T]), op=ALU.mult),
                             rd=[Rt1, RmodA], wr=[Rt1])
                        P.op("dve", lambda e, kc=kc, t1=t1: e.tensor_tensor(out=gt3(XN[kc][0]), in0=gt3(t1),
                                                                             in1=modT[:, l, shk * 16 + kc, mcol0:mcol0 + G].unsqueeze(2).to_broadcast([128, G, T]), op=ALU.add),
                             rd=[Rt1, Rmod], wr=[XN[kc][1]])

            def proj_fm(pt, Rp, buf, Rw, wc0, ncol, acts, n=None):
                n = n or N
                K = len(acts)
                for kc in range(K):
                    P.op("pe", lambda e, kc=kc: e.matmul(pt[0:ncol, 0:n], lhsT=buf[:, kc, wc0:wc0 + ncol], rhs=acts[kc][0], start=(kc == 0), stop=(kc == K - 1)),
                         rd=[Rw, acts[kc][1]], wr=[Rp])

            def proj_tm(pt, Rp, buf, Rw, wc0, ncol, acts, c0, n, pcol0=0):
                K = len(acts)
                for kc in range(K):
                    P.op("pe", lambda e, kc=kc: e.matmul(pt[0:n, pcol0:pcol0 + ncol], lhsT=acts[kc][0][:, c0:c0 + n], rhs=buf[:, kc, wc0:wc0 + ncol],
                                                         start=(kc == 0), stop=(kc == K - 1)), rd=[Rw, acts[kc][1]], wr=[Rp])

            def resid_add(l, kind, oc, pt, Rp):
                col = kind * 16 + oc
                if G == 1:
                    P.op("dve", lambda e: e.scalar_tensor_tensor(out=X[oc][0], in0=pt[:, 0:N], scalar=modT0[:, l, col:col + 1], in1=X[oc][0], op0=ALU.mult, op1=ALU.add),
                         rd=[Rp, Rmod0, X[oc][1]], wr=[X[oc][1]])
                else:
                    t1, Rt1 = rF.get()
                    P.op("dve", lambda e: e.tensor_tensor(out=gt3(t1), in0=gt3(pt[:, 0:N]), in1=modT[:, l, col, mcol0:mcol0 + G].unsqueeze(2).to_broadcast([128, G, T]), op=ALU.mult),
                         rd=[Rp, Rmod], wr=[Rt1])
                    P.op("dve", lambda e: e.tensor_tensor(out=X[oc][0], in0=X[oc][0], in1=t1, op=ALU.add), rd=[Rt1, X[oc][1]], wr=[X[oc][1]])

            def conv(l, cname, bname, ch, cb, Rcb, out, Rout):
                o3 = gt3(out)
                P.op("dve", lambda e: e.tensor_scalar(out=o3, in0=cb[:, :, 3:3 + T], scalar1=pc(cname, l, ch * 4 + 3, 1), scalar2=pc(bname, l, ch, 1), op0=ALU.mult, op1=ALU.add),
                     rd=[Rcb, Rprm], wr=[Rout])
                for j in (2, 1, 0):
                    P.op("dve", lambda e, j=j: e.scalar_tensor_tensor(out=o3, in0=cb[:, :, j:j + T], scalar=pc(cname, l, ch * 4 + j, 1), in1=o3, op0=ALU.mult, op1=ALU.add),
                         rd=[Rcb, Rprm, Rout], wr=[Rout])

            def conv_in(l, cv, ch, pt, Rp):
                cb, Rcb = rcv.get()
                P.op("act", lambda e: e.activation(out=cb[:, :, 0:3], in_=hist[:, l, cv, ch], func=AF.Copy), rd=[Rhist], wr=[Rcb])
                P.op("act", lambda e: e.activation(out=cb[:, :, 3:3 + T], in_=gt3(pt[:, 0:N]), func=AF.Copy), rd=[Rp], wr=[Rcb])
                P.op("act", lambda e: e.activation(out=hist[:, l, cv, ch], in_=cb[:, :, T:T + 3], func=AF.Copy), rd=[Rcb], wr=[Rhist])
                return cb, Rcb

            def lru_branch(l):
                for ch in range(8):
                    if ch % 2 == 0:
                        buf, Rw = wload("w_in", l, 0, D, O_LRUX + ch * 128, 256)
                    wa, Rwa = sload(w_la[l, ch])
                    wx, Rwx = sload(w_lx[l, ch])
                    pt, Rp = psum()
                    proj_fm(pt, Rp, buf, Rw, (ch % 2) * 128, 128, XN)
                    cb, Rcb = conv_in(l, 0, ch, pt, Rp)
                    u, Ru = rF.get()
                    conv(l, "lcw", "lcb", ch, cb, Rcb, u, Ru)
                    ub, Rub = rB.get()
                    P.op("act", lambda e: e.activation(out=ub, in_=u, func=AF.Copy), rd=[Ru], wr=[Rub])
                    pa, Rpa = psum()
                    P.op("pe", lambda e: e.matmul(pa[:, 0:N], lhsT=wa, rhs=ub, start=True, stop=True), rd=[Rwa, Rub], wr=[Rpa])
                    px, Rpx = psum()
                    P.op("pe", lambda e: e.matmul(px[:, 0:N], lhsT=wx, rhs=ub, start=True, stop=True), rd=[Rwx, Rub], wr=[Rpx])
                    r, Rr = rF.get()
                    P.op("act", lambda e: e.activation(out=r, in_=pa[:, 0:N], func=AF.Sigmoid, bias=pc("lba", l, ch, 1), scale=1.0), rd=[Rpa, Rprm], wr=[Rr])
                    ig, Rig = rF.get()
                    P.op("act", lambda e: e.activation(out=ig, in_=px[:, 0:N], func=AF.Sigmoid, bias=pc("lbx", l, ch, 1), scale=1.0), rd=[Rpx, Rprm], wr=[Rig])
                    a, Ra = rF.get()
                    P.op("act", lambda e: e.activation(out=a, in_=r, func=AF.Exp, scale=drv[:, l, ch:ch + 1]), rd=[Rr, Rdrv], wr=[Ra])
                    P.op("act", lambda e: e.activation(out=r, in_=r, func=AF.Exp, scale=drv[:, l, 8 + ch:9 + ch]), rd=[Rr, Rdrv], wr=[Rr])
                    P.op("dve", lambda e: e.tensor_scalar(out=r, in0=r, scalar1=1.0, scalar2=-1.0, op0=ALU.min, op1=ALU.mult), rd=[Rr], wr=[Rr])
                    P.op("act", lambda e: e.activation(out=r, in_=r, func=AF.Sqrt, bias=1.0, scale=1.0), rd=[Rr], wr=[Rr])
                    P.op("dve", lambda e: e.tensor_tensor(out=ig, in0=ig, in1=u, op=ALU.mult), rd=[Rig, Ru], wr=[Rig])
                    P.op("dve", lambda e: e.tensor_tensor(out=ig, in0=ig, in1=r, op=ALU.mult), rd=[Rig, Rr], wr=[Rig])
                    if G == 1:
                        P.op("dve", lambda e: e.tensor_tensor_scan(out=u, data0=a, data1=ig, initial=hst[:, l, ch, 0:1], op0=ALU.mult, op1=ALU.add),
                             rd=[Ra, Rig, Rhst], wr=[Ru])
                    else:
                        a3, g3, u3 = gt3(a), gt3(ig), gt3(u)
                        for t in range(T):
                            prev = hst[:, l, ch, :] if t == 0 else u3[:, :, t - 1]
                            P.op("dve", lambda e, t=t, prev=prev: e.tensor_tensor(out=u3[:, :, t], in0=a3[:, :, t], in1=prev, op=ALU.mult), rd=[Ra, Ru, Rhst], wr=[Ru])
                            P.op("dve", lambda e, t=t: e.tensor_tensor(out=u3[:, :, t], in0=u3[:, :, t], in1=g3[:, :, t], op=ALU.add), rd=[Ru, Rig], wr=[Ru])
                    P.op("act", lambda e: e.activation(out=hst[:, l, ch, :], in_=gt3(u)[:, :, T - 1], func=AF.Copy), rd=[Ru], wr=[Rhst])
                    P.op("act", lambda e: e.activation(out=YB[0][ch][0], in_=u, func=AF.Copy), rd=[Ru], wr=[YB[0][ch][1]])

            def gla_branch(l, Sfull, RS):
                buf, Rw = wload("w_in", l, 0, D, O_GLR, 16)
                pt, Rp = psum()
                proj_fm(pt, Rp, buf, Rw, 0, 16, XN)
                glr, Rglr = glrv
                P.op("act", lambda e: e.activation(out=glr[0:16, :], in_=pt[0:16, 0:N], func=AF.Copy), rd=[Rp], wr=[Rglr])
                P.dma("pool", lambda e: e.dma_start(out=wg2[0:16, :], in_=w_g2[l]), wr=[Rwg2], own=Rwg2)
                for h in range(4):
                    if is_sample:
                        S, RSh = Sst[0]
                        P.dma("sp", lambda e, h=h: e.dma_start(out=S, in_=i_sgla[l, :, h].rearrange("g p e -> p g e")), wr=[RSh], own=RSh)
                    else:
                        S, RSh = Sfull[:, h:h + 1, :], RS
                    P.op("act", lambda e, S=S: e.activation(out=Sbf, in_=S, func=AF.Copy), rd=[RSh], wr=[RSbf])
                    pz, Rpz = psum()
                    P.op("pe", lambda e, h=h: e.matmul(pz[:, 0:N], lhsT=wg2[0:16, h * 128:(h + 1) * 128], rhs=glr[0:16, :], start=True, stop=True), rd=[Rwg2, Rglr], wr=[Rpz])
                    sp_, Rsp = rF.get()
                    P.op("act", lambda e, h=h: e.activation(out=sp_, in_=pz[:, 0:N], func=AF.Exp, scale=-1.0, bias=drv[:, l, 16 + h:17 + h]), rd=[Rpz, Rdrv], wr=[Rsp])
                    P.op("act", lambda e: e.activation(out=sp_, in_=sp_, func=AF.Ln, bias=1.0), rd=[Rsp], wr=[Rsp])
                    cs, Rcs = rF.get()
                    if G == 1:
                        for (c0, ng, gl) in chunks:
                            P.op("dve", lambda e, c0=c0, gl=gl: e.tensor_tensor_scan(out=cs[:, c0:c0 + gl], data0=ones_f[:, 0:gl], data1=sp_[:, c0:c0 + gl], initial=0.0,
                                                                                     op0=ALU.mult, op1=ALU.add), rd=[Rsp, Rcst], wr=[Rcs])
                    else:
                        s3, c3 = gt3(sp_), gt3(cs)
                        P.op("dve", lambda e: e.tensor_copy(out=c3[:, :, 0], in_=s3[:, :, 0]), rd=[Rsp], wr=[Rcs])
                        for t in range(1, T):
                            P.op("dve", lambda e, t=t: e.tensor_tensor(out=c3[:, :, t], in0=c3[:, :, t - 1], in1=s3[:, :, t], op=ALU.add), rd=[Rsp, Rcs], wr=[Rcs])
                    eb, Reb = rF.get()
                    P.op("act", lambda e: e.activation(out=eb, in_=cs, func=AF.Exp, scale=-1.0 / 16.0), rd=[Rcs], wr=[Reb])
                    P.op("act", lambda e: e.activation(out=cs, in_=cs, func=AF.Exp, scale=1.0 / 16.0), rd=[Rcs], wr=[Rcs])
                    buf, Rw = wload("w_in", l, 0, D, O_GQ + h * 128, 128)
                    pq, Rpq = psum()
                    proj_fm(pq, Rpq, buf, Rw, 0, 128, XN)
                    qe, Rqe = rB.get()
                    P.op("dve", lambda e: e.scalar_tensor_tensor(out=qe, in0=pq[:, 0:N], scalar=128.0 ** -0.5, in1=eb, op0=ALU.mult, op1=ALU.mult), rd=[Rpq, Reb], wr=[Rqe])
                    buf, Rw = wload("w_in", l, 0, D, O_GK + h * 128, 128)
                    pk, Rpk = psum()
                    proj_fm(pk, Rpk, buf, Rw, 0, 128, XN)
                    ke, Rke = rF.get()
                    P.op("dve", lambda e: e.tensor_tensor(out=ke, in0=pk[:, 0:N], in1=cs, op=ALU.mult), rd=[Rpk, Rcs], wr=[Rke])
                    keb, Rkeb = rB.get()
                    P.op("act", lambda e: e.activation(out=keb, in_=ke, func=AF.Copy), rd=[Rke], wr=[Rkeb])
                    buf, Rw = wload("w_in", l, 0, D, O_GV + h * 256, 256)
                    for c, (c0, ng, gl) in enumerate(chunks):
                        n = ng * gl
                        pv, Rpv = psum()
                        proj_tm(pv, Rpv, buf, Rw, 0, 256, XN, c0, n)
                        P.op("act", lambda e, c=c, n=n, pv=pv: e.activation(out=vtm[c][0][0:n, 0:256], in_=pv[0:n, 0:256], func=AF.Copy), rd=[Rpv], wr=[vtm[c][1]])
                    buf, Rw = wload("w_in", l, 0, D, O_GG + h * 256, 256)
                    sg = []
                    for ec in range(2):
                        pg, Rpg = psum()
                        proj_fm(pg, Rpg, buf, Rw, ec * 128, 128, XN)
                        s_, Rs_ = rB.get()
                        P.op("act", lambda e, pg=pg, s_=s_: e.activation(out=s_, in_=pg[:, 0:N], func=AF.Silu), rd=[Rpg], wr=[Rs_])
                        sg.append((s_, Rs_))
                    for c, (c0, ng, gl) in enumerate(chunks):
                        n = ng * gl
                        msk = mask128 if not is_sample else mask64
                        pa, Rpa = psum()
                        P.op("pe", lambda e, c0=c0, n=n, pa=pa: e.matmul(pa[0:n, 0:n], lhsT=keb[:, c0:c0 + n], rhs=qe[:, c0:c0 + n], start=True, stop=True), rd=[Rkeb, Rqe], wr=[Rpa])
                        AT, RAT = rb.get()
                        P.op("dve", lambda e, n=n, pa=pa, AT=AT, msk=msk: e.tensor_tensor(out=AT[0:n, 0:n], in0=pa[0:n, 0:n], in1=msk[0:n, 0:n], op=ALU.mult), rd=[Rpa, Rcst], wr=[RAT])
                        osb = []
                        for ec in range(2):
                            po, Rpo = psum()
                            nmm = 1 + ng
                            P.op("pe", lambda e, c=c, n=n, ec=ec, po=po, AT=AT: e.matmul(po[:, 0:n], lhsT=vtm[c][0][0:n, ec * 128:(ec + 1) * 128], rhs=AT[0:n, 0:n], start=True, stop=False),
                                 rd=[vtm[c][1], RAT], wr=[Rpo])
                            for g in range(ng):
                                P.op("pe", lambda e, g=g, ec=ec, po=po, c0=c0, gl=gl, ng=ng: e.matmul(po[:, g * gl:(g + 1) * gl], lhsT=Sbf[:, g, ec * 128:(ec + 1) * 128],
                                                                                                    rhs=qe[:, c0 + g * gl:c0 + (g + 1) * gl], start=False, stop=(g == ng - 1)),
                                     rd=[RSbf, Rqe], wr=[Rpo])
                            o_, Ro_ = rf.get()
                            P.op("act", lambda e, n=n, po=po, o_=o_: e.activation(out=o_[:, 0:n], in_=po[:, 0:n], func=AF.Copy), rd=[Rpo], wr=[Ro_])
                            osb.append((o_[:, 0:n], Ro_))
                        kd, Rkd = rf.get()
                        kd3 = kd[:, 0:n].rearrange("p (g t) -> p g t", t=gl)
                        ke3 = ke[:, c0:c0 + n].rearrange("p (g t) -> p g t", t=gl)
                        eb3 = eb[:, c0:c0 + n].rearrange("p (g t) -> p g t", t=gl)
                        P.op("dve", lambda e, kd3=kd3, ke3=ke3, eb3=eb3, ng=ng, gl=gl: e.tensor_tensor(out=kd3, in0=ke3, in1=eb3[:, :, gl - 1:gl].to_broadcast([128, ng, gl]), op=ALU.mult),
                             rd=[Rke, Reb], wr=[Rkd])
                        ptr, Rptr = psum()
                        P.op("pe", lambda e, n=n, kd=kd, ptr=ptr: e.transpose(out=ptr[0:n, 0:128], in_=kd[:, 0:n], identity=ident), rd=[Rkd, Rcst], wr=[Rptr])
                        km, Rkm = rkm.get()
                        if ng == 1:
                            P.op("act", lambda e, n=n, ptr=ptr, km=km: e.activation(out=km[0:n, 0, 0:128], in_=ptr[0:n, 0:128], func=AF.Copy), rd=[Rptr], wr=[Rkm])
                        else:
                            kt, Rkt = rb.get()
                            P.op("act", lambda e, n=n, ptr=ptr, kt=kt: e.activation(out=kt[0:n, 0:128], in_=ptr[0:n, 0:128], func=AF.Copy), rd=[Rptr], wr=[Rkt])
                            P.op("dve", lambda e, n=n, kt=kt, km=km, ng=ng: e.tensor_tensor(out=km[0:n, :, 0:128], in0=kt[0:n, 0:128].unsqueeze(1).to_broadcast([n, ng, 128]),
                                                                                           in1=gmask[0:n, 0:ng].unsqueeze(2).to_broadcast([n, ng, 128]), op=ALU.mult),
                                 rd=[Rkt, Rcst], wr=[Rkm])
                        for g in range(ng):
                            pu, Rpu = psum()
                            P.op("pe", lambda e, g=g, n=n, c=c, pu=pu, km=km: e.matmul(pu[:, 0:256], lhsT=km[0:n, g, 0:128], rhs=vtm[c][0][0:n, 0:256], start=True, stop=True),
                                 rd=[Rkm, vtm[c][1]], wr=[Rpu])
                            ebl = eb[:, c0 + (g + 1) * gl - 1:c0 + (g + 1) * gl]
                            P.op("dve", lambda e, g=g, pu=pu, ebl=ebl, S=S: e.scalar_tensor_tensor(out=S[:, g, :], in0=S[:, g, :], scalar=ebl, in1=pu[:, 0:256], op0=ALU.mult, op1=ALU.add),
                                 rd=[RSh, Reb, Rpu], wr=[RSh])
                        if c < NCH - 1:
                            P.op("act", lambda e, S=S: e.activation(out=Sbf, in_=S, func=AF.Copy), rd=[RSh], wr=[RSbf])
                        rs, Rrs = rmsnorm_rstd(osb)
                        for ec in range(2):
                            o_, Ro_ = osb[ec]
                            P.op("dve", lambda e, ec=ec, o_=o_: e.scalar_tensor_tensor(out=o_, in0=o_, scalar=pc("ggn", l, ec, 1), in1=rs, op0=ALU.mult, op1=ALU.mult),
                                 rd=[Ro_, Rprm, Rrs], wr=[Ro_])
                            yb, Ryb = YB[1][2 * h + ec]
                            P.op("dve", lambda e, ec=ec, o_=o_, yb=yb, c0=c0, n=n: e.tensor_tensor(out=yb[:, c0:c0 + n], in0=o_, in1=sg[ec][0][:, c0:c0 + n], op=ALU.mult),
                                 rd=[Ro_, sg[ec][1]], wr=[Ryb])
                    if is_sample:
                        P.dma("sp", lambda e, h=h, S=S: e.dma_start(out=o_gla[1][l, :, h].rearrange("g p e -> p g e"), in_=S), rd=[RSh], own=RSh, store=True)

            def mlstm_branch(l, Cfull, RC):
                buf, Rw = wload("w_if", l, 0, D, 0, 256)
                pi_, Rpi = psum()
                proj_fm(pi_, Rpi, buf, Rw, 0, 128, XN)
                pf_, Rpf = psum()
                proj_fm(pf_, Rpf, buf, Rw, 128, 128, XN)
                lfn, Rlfn = rF.get()
                P.op("act", lambda e: e.activation(out=lfn, in_=pf_[:, 0:N], func=AF.Exp, scale=-1.0, bias=drv[:, l, 20:21]), rd=[Rpf, Rdrv], wr=[Rlfn])
                P.op("act", lambda e: e.activation(out=lfn, in_=lfn, func=AF.Ln, bias=1.0), rd=[Rlfn], wr=[Rlfn])
                Fn, RFn = rF.get()
                gg, Rgg = rF.get()
                Mt, RMt = rF.get()
                if G == 1:
                    for (c0, ng, gl) in chunks:
                        ini = Fc[:, l, 0:1] if c0 == 0 else Fn[:, c0 - 1:c0]
                        P.op("dve", lambda e, c0=c0, gl=gl, ini=ini: e.tensor_tensor_scan(out=Fn[:, c0:c0 + gl], data0=ones_f[:, 0:gl], data1=lfn[:, c0:c0 + gl], initial=ini,
                                                                                          op0=ALU.mult, op1=ALU.add), rd=[Rlfn, RFc, Rcst, RFn], wr=[RFn])
                    P.op("dve", lambda e: e.scalar_tensor_tensor(out=gg, in0=pi_[:, 0:N], scalar=pc("mbif", l, 0, 1), in1=Fn, op0=ALU.add, op1=ALU.add), rd=[Rpi, Rprm, RFn], wr=[Rgg])
                    P.op("dve", lambda e: e.tensor_tensor_scan(out=Mt, data0=gg, data1=gg, initial=Mc[:, l, 0:1], op0=ALU.max, op1=ALU.max), rd=[Rgg, RMc], wr=[RMt])
                else:
                    l3, F3, g3, M3 = gt3(lfn), gt3(Fn), gt3(gg), gt3(Mt)
                    P.op("dve", lambda e: e.tensor_copy(out=F3[:, :, 0], in_=l3[:, :, 0]), rd=[Rlfn], wr=[RFn])
                    for t in range(1, T):
                        P.op("dve", lambda e, t=t: e.tensor_tensor(out=F3[:, :, t], in0=F3[:, :, t - 1], in1=l3[:, :, t], op=ALU.add), rd=[Rlfn, RFn], wr=[RFn])
                    P.op("dve", lambda e: e.scalar_tensor_tensor(out=gg, in0=pi_[:, 0:N], scalar=pc("mbif", l, 0, 1), in1=Fn, op0=ALU.add, op1=ALU.add), rd=[Rpi, Rprm, RFn], wr=[Rgg])
                    for t in range(T):
                        prev = Mc[:, l, :] if t == 0 else M3[:, :, t - 1]
                        P.op("dve", lambda e, t=t, prev=prev: e.tensor_tensor(out=M3[:, :, t], in0=g3[:, :, t], in1=prev, op=ALU.max), rd=[Rgg, RMt, RMc], wr=[RMt])
                Rrows = []
                for c, (c0, ng, gl) in enumerate(chunks):
                    n = ng * gl
                    R_, RR_ = rRs[c]
                    if G == 1:
                        mref = Mc[:, l, 0:1] if c == 0 else Mt[:, c0 - 1:c0]
                        mrd = RMc if c == 0 else RMt
                        P.op("dve", lambda e, c0=c0, n=n, R_=R_, mref=mref: e.tensor_scalar(out=R_[:, 0, 0:n], in0=gg[:, c0:c0 + n], scalar1=mref, scalar2=None, op0=ALU.subtract),
                             rd=[Rgg, mrd], wr=[RR_])
                        P.op("dve", lambda e, c0=c0, n=n, R_=R_, mref=mref: e.tensor_scalar(out=R_[:, 1, 0:n], in0=Mt[:, c0:c0 + n], scalar1=-1.0, scalar2=mref, op0=ALU.mult, op1=ALU.add),
                             rd=[RMt, mrd], wr=[RR_])
                    else:
                        mb = Mc[:, l, :].unsqueeze(2).to_broadcast([128, G, T])
                        P.op("dve", lambda e, R_=R_, mb=mb: e.tensor_tensor(out=gt3(R_[:, 0, 0:N]), in0=gt3(gg), in1=mb, op=ALU.subtract), rd=[Rgg, RMc], wr=[RR_])
                        P.op("dve", lambda e, R_=R_, mb=mb: e.tensor_tensor(out=gt3(R_[:, 1, 0:N]), in0=mb, in1=gt3(Mt), op=ALU.subtract), rd=[RMt, RMc], wr=[RR_])
                    if G == 1:
                        P.op("dve", lambda e, c0=c0, n=n, R_=R_, mref=mref: e.tensor_scalar(out=R_[:, 2, 0:n], in0=Fn[:, c0:c0 + n], scalar1=mref, scalar2=None, op0=ALU.subtract),
                             rd=[RFn, mrd], wr=[RR_])
                    else:
                        P.op("dve", lambda e, R_=R_, mb=mb: e.tensor_tensor(out=gt3(R_[:, 2, 0:N]), in0=gt3(Fn), in1=mb, op=ALU.subtract), rd=[RFn, RMc], wr=[RR_])
                    P.op("act", lambda e, R_=R_: e.activation(out=R_, in_=R_, func=AF.Exp), rd=[RR_], wr=[RR_])
                    Rrows.append((R_, RR_))
                P.op("dve", lambda e: e.tensor_tensor(out=mout[:, l, :], in0=gt3(Mt)[:, :, T - 1], in1=gt3(Fn)[:, :, T - 1], op=ALU.subtract), rd=[RMt, RFn], wr=[Rmout])
                if not is_sample:
                    P.op("dve", lambda e: e.tensor_copy(out=Fc[:, l, :], in_=Fn[:, N - 1:N]), rd=[RFn], wr=[RFc])
                    P.op("dve", lambda e: e.tensor_copy(out=Mc[:, l, :], in_=Mt[:, N - 1:N]), rd=[RMt], wr=[RMc])
                rcols = []
                for c, (c0, ng, gl) in enumerate(chunks):
                    n = ng * gl
                    R_, RR_ = Rrows[c]
                    pr, Rpr = psum()
                    P.op("pe", lambda e, n=n, R_=R_, pr=pr: e.matmul(pr[0:n, 0:4], lhsT=R_[:, 0, 0:n], rhs=E4, start=True, stop=True), rd=[RR_, Rcst], wr=[Rpr])
                    rc_, Rrc = rcolv[c]
                    P.op("act", lambda e, n=n, pr=pr, rc_=rc_: e.activation(out=rc_[0:n, :], in_=pr[0:n, 0:4], func=AF.Copy), rd=[Rpr], wr=[Rrc])
                    rcols.append((rc_, Rrc))
                for h in range(4):
                    if is_sample:
                        C, RCh = Cst[0]
                        for dc in range(2):
                            P.dma("sp", lambda e, h=h, dc=dc, C=C: e.dma_start(out=C[:, :, dc, 0:256], in_=i_sC[l, :, h, dc * 128:(dc + 1) * 128, :].rearrange("g p e -> p g e")), wr=[RCh], own=RCh)
                        P.op("dve", lambda e, h=h, C=C: e.tensor_copy(out=C[:, :, :, 256], in_=nin[:, l, :, h, :]), rd=[Rnin], wr=[RCh])
                    else:
                        C, RCh = Cfull[:, h:h + 1], RC

                    def refresh_shadow(C=C, RCh=RCh):
                        P.op("act", lambda e: e.activation(out=Cbf, in_=C, func=AF.Copy), rd=[RCh], wr=[RCbf])
                        P.op("dve", lambda e: e.tensor_copy(out=nbc, in_=C[:, :, :, 256:257].to_broadcast([128, C.shape[1], 2, 128])), rd=[RCh], wr=[Rnbc])
                    refresh_shadow()
                    buf, Rw = wload("w_in", l, 0, D, O_MX + h * 256, 256)
                    mqT, mkT, mxb_l, mc_l = [], [], [], []
                    for dc in range(2):
                        ch = 2 * h + dc
                        pt, Rp = psum()
                        proj_fm(pt, Rp, buf, Rw, dc * 128, 128, XN)
                        cb, Rcb = conv_in(l, 1, ch, pt, Rp)
                        mxb, Rmxb = rB.get()
                        P.op("act", lambda e, cb=cb, mxb=mxb: e.activation(out=gt3(mxb), in_=cb[:, :, 3:3 + T], func=AF.Copy), rd=[Rcb], wr=[Rmxb])
                        cv, Rcv = rF.get()
                        conv(l, "mcw", "mcb", ch, cb, Rcb, cv, Rcv)
                        mc_, Rmc_ = rB.get()
                        P.op("act", lambda e, cv=cv, mc_=mc_: e.activation(out=mc_, in_=cv, func=AF.Silu), rd=[Rcv], wr=[Rmc_])
                        wq, Rwq = sload(w_mq[l, ch])
                        wk, Rwk = sload(w_mk[l, ch])
                        wv, Rwv = sload(w_mv[l, ch])
                        pq, Rpq = psum()
                        P.op("pe", lambda e, pq=pq, wq=wq, mc_=mc_: e.matmul(pq[:, 0:N], lhsT=wq, rhs=mc_, start=True, stop=True), rd=[Rwq, Rmc_], wr=[Rpq])
                        q_, Rq_ = rB.get()
                        P.op("act", lambda e, pq=pq, q_=q_: e.activation(out=q_, in_=pq[:, 0:N], func=AF.Copy), rd=[Rpq], wr=[Rq_])
                        pk, Rpk = psum()
                        P.op("pe", lambda e, pk=pk, wk=wk, mc_=mc_: e.matmul(pk[:, 0:N], lhsT=wk, rhs=mc_, start=True, stop=True), rd=[Rwk, Rmc_], wr=[Rpk])
                        k_, Rk_ = rB.get()
                        P.op("act", lambda e, pk=pk, k_=k_: e.activation(out=k_, in_=pk[:, 0:N], func=AF.Copy, scale=1.0 / 16.0), rd=[Rpk], wr=[Rk_])
                        mqT.append((q_, Rq_)); mkT.append((k_, Rk_))
                        for c, (c0, ng, gl) in enumerate(chunks):
                            n = ng * gl
                            pkt, Rpkt = psum()
                            P.op("pe", lambda e, c0=c0, n=n, pkt=pkt, mc_=mc_, wk=wk: e.matmul(pkt[0:n, 0:128], lhsT=mc_[:, c0:c0 + n], rhs=wk, start=True, stop=True), rd=[Rmc_, Rwk], wr=[Rpkt])
                            P.op("act", lambda e, c=c, n=n, dc=dc, pkt=pkt: e.activation(out=ktm[c][0][0:n, dc * 128:(dc + 1) * 128], in_=pkt[0:n, 0:128], func=AF.Copy, scale=1.0 / 16.0),
                                 rd=[Rpkt], wr=[ktm[c][1]])
                            pvt, Rpvt = psum()
                            P.op("pe", lambda e, c0=c0, n=n, pvt=pvt, mxb=mxb, wv=wv: e.matmul(pvt[0:n, 0:128], lhsT=mxb[:, c0:c0 + n], rhs=wv, start=True, stop=True), rd=[Rmxb, Rwv], wr=[Rpvt])
                            P.op("act", lambda e, c=c, n=n, dc=dc, pvt=pvt: e.activation(out=mvt[c][0][0:n, dc * 128:(dc + 1) * 128], in_=pvt[0:n, 0:128], func=AF.Copy),
                                 rd=[Rpvt], wr=[mvt[c][1]])
                    buf, Rw = wload("w_in", l, 0, D, O_MO + h * 256, 256)
                    so = []
                    for ec in range(2):
                        pg, Rpg = psum()
                        proj_fm(pg, Rpg, buf, Rw, ec * 128, 128, XN)
                        s_, Rs_ = rB.get()
                        P.op("act", lambda e, pg=pg, s_=s_: e.activation(out=s_, in_=pg[:, 0:N], func=AF.Sigmoid), rd=[Rpg], wr=[Rs_])
                        so.append((s_, Rs_))
                    for c, (c0, ng, gl) in enumerate(chunks):
                        n = ng * gl
                        msk = mask128 if not is_sample else mask64
                        R_, RR_ = Rrows[c]
                        pb, Rpb = psum()
                        P.op("pe", lambda e, h=h, n=n, R_=R_, pb=pb: e.matmul(pb[:, 0:3 * CN], lhsT=sel[h], rhs=R_.rearrange("p a b -> p (a b)"), start=True, stop=True), rd=[RR_, Rcst], wr=[Rpb])
                        bc, Rbc = rbc.get()
                        P.op("act", lambda e, pb=pb, bc=bc: e.activation(out=bc.rearrange("p a b -> p (a b)"), in_=pb[:, 0:3 * CN], func=AF.Copy), rd=[Rpb], wr=[Rbc])
                        pst, Rpst = psum()
                        for dc in range(2):
                            kr, Rkr = rb.get()
                            P.op("dve", lambda e, dc=dc, c0=c0, n=n, kr=kr, bc=bc: e.tensor_tensor(out=kr[:, 0:n], in0=mkT[dc][0][:, c0:c0 + n], in1=bc[:, 0, 0:n], op=ALU.mult),
                                 rd=[mkT[dc][1], Rbc], wr=[Rkr])
                            P.op("pe", lambda e, dc=dc, c0=c0, n=n, kr=kr, pst=pst: e.matmul(pst[0:n, 0:n], lhsT=kr[:, 0:n], rhs=mqT[dc][0][:, c0:c0 + n], start=(dc == 0), stop=(dc == 1)),
                                 rd=[Rkr, mqT[dc][1]], wr=[Rpst])
                        ST, RST = rb.get()
                        P.op("dve", lambda e, n=n, pst=pst, ST=ST, msk=msk: e.tensor_tensor(out=ST[0:n, 0:n], in0=pst[0:n, 0:n], in1=msk[0:n, 0:n], op=ALU.mult), rd=[Rpst, Rcst], wr=[RST])
                        pd, Rpd = psum()
                        P.op("pe", lambda e, n=n, pd=pd, ST=ST: e.matmul(pd[:, 0:n], lhsT=ones_b[0:n, :], rhs=ST[0:n, 0:n], start=True, stop=False), rd=[Rob, RST], wr=[Rpd])
                        for g in range(ng):
                            for dc in range(2):
                                P.op("pe", lambda e, g=g, dc=dc, pd=pd, c0=c0, gl=gl, ng=ng: e.matmul(pd[:, g * gl:(g + 1) * gl], lhsT=nbc[:, g, dc, :], rhs=mqT[dc][0][:, c0 + g * gl:c0 + (g + 1) * gl],
                                                                                                    start=False, stop=(g == ng - 1 and dc == 1)), rd=[Rnbc, mqT[dc][1]], wr=[Rpd])
                        pns = []
                        for ec in range(2):
                            pn, Rpn = psum()
                            P.op("pe", lambda e, c=c, n=n, ec=ec, pn=pn, ST=ST: e.matmul(pn[:, 0:n], lhsT=mvt[c][0][0:n, ec * 128:(ec + 1) * 128], rhs=ST[0:n, 0:n], start=True, stop=False),
                                 rd=[mvt[c][1], RST], wr=[Rpn])
                            for g in range(ng):
                                for dc in range(2):
                                    P.op("pe", lambda e, g=g, dc=dc, ec=ec, pn=pn, c0=c0, gl=gl, ng=ng: e.matmul(pn[:, g * gl:(g + 1) * gl], lhsT=Cbf[:, g, dc, ec * 128:(ec + 1) * 128],
                                                                                                               rhs=mqT[dc][0][:, c0 + g * gl:c0 + (g + 1) * gl], start=False,
                                                                                                               stop=(g == ng - 1 and dc == 1)), rd=[RCbf, mqT[dc][1]], wr=[Rpn])
                            pns.append((pn, Rpn))
                        if ng == 1:
                            km, Rkm = rkm.get()
                            rc_, Rrc = rcols[c]
                            if ng == 1:
                                P.op("dve", lambda e, h=h, c=c, n=n, km=km, rc_=rc_: e.tensor_scalar(out=km[0:n, 0, :], in0=ktm[c][0][0:n, :], scalar1=rc_[0:n, h:h + 1], scalar2=None, op0=ALU.mult),
                                     rd=[ktm[c][1], Rrc], wr=[Rkm])
                            else:
                                kt, Rkt = rk256.get()
                                P.op("dve", lambda e, h=h, c=c, n=n, kt=kt, rc_=rc_: e.tensor_scalar(out=kt[0:n, 0:256], in0=ktm[c][0][0:n, :], scalar1=rc_[0:n, h:h + 1], scalar2=None, op0=ALU.mult),
                                     rd=[ktm[c][1], Rrc], wr=[Rkt])
                                P.op("dve", lambda e, n=n, kt=kt, km=km, ng=ng: e.tensor_tensor(out=km[0:n, :, :], in0=kt[0:n, 0:256].unsqueeze(1).to_broadcast([n, ng, 256]),
                                                                                               in1=gmask[0:n, 0:ng].unsqueeze(2).to_broadcast([n, ng, 256]), op=ALU.mult),
                                     rd=[Rkt, Rcst], wr=[Rkm])
                            for g in range(ng):
                                csc = bc[:, 1, (g + 1) * gl - 1:(g + 1) * gl]
                                for dc in range(2):
                                    pu, Rpu = psum()
                                    P.op("pe", lambda e, g=g, dc=dc, n=n, c=c, pu=pu, km=km: e.matmul(pu[:, 0:257], lhsT=km[0:n, g, dc * 128:(dc + 1) * 128], rhs=mvt[c][0][0:n, 0:257], start=True, stop=True),
                                         rd=[Rkm, mvt[c][1]], wr=[Rpu])
                                    tc_, Rtc = rtc.get()
                                    P.op("dve", lambda e, g=g, dc=dc, pu=pu, tc_=tc_, C=C: e.tensor_tensor(out=tc_, in0=C[:, g, dc, :], in1=pu[:, 0:257], op=ALU.add), rd=[RCh, Rpu], wr=[Rtc])
                                    P.op("act", lambda e, g=g, dc=dc, tc_=tc_, C=C, csc=csc: e.activation(out=C[:, g, dc, :], in_=tc_, func=AF.Copy, scale=csc), rd=[Rtc, Rbc], wr=[RCh])
                            if c < NCH - 1:
                                refresh_shadow()
                        t1, Rt1 = rf.get()
                        P.op("act", lambda e, n=n, pd=pd, t1=t1: e.activation(out=t1[:, 0:n], in_=pd[:, 0:n], func=AF.Abs), rd=[Rpd], wr=[Rt1])
                        P.op("dve", lambda e, n=n, t1=t1, bc=bc: e.tensor_tensor(out=t1[:, 0:n], in0=t1[:, 0:n], in1=bc[:, 2, 0:n], op=ALU.max), rd=[Rt1, Rbc], wr=[Rt1])
                        P.op("dve", lambda e, n=n, t1=t1: e.reciprocal(out=t1[:, 0:n], in_=t1[:, 0:n]), rd=[Rt1], wr=[Rt1])
                        hsb = []
                        for ec in range(2):
                            pn, Rpn = pns[ec]
                            h_, Rh_ = rf.get()
                            P.op("dve", lambda e, n=n, pn=pn, h_=h_, t1=t1: e.tensor_tensor(out=h_[:, 0:n], in0=pn[:, 0:n], in1=t1[:, 0:n], op=ALU.mult), rd=[Rpn, Rt1], wr=[Rh_])
                            hsb.append((h_[:, 0:n], Rh_))
                        rs, Rrs = rmsnorm_rstd(hsb)
                        for ec in range(2):
                            h_, Rh_ = hsb[ec]
                            P.op("dve", lambda e, ec=ec, h_=h_: e.scalar_tensor_tensor(out=h_, in0=h_, scalar=pc("mgn", l, ec, 1), in1=rs, op0=ALU.mult, op1=ALU.mult),
                                 rd=[Rh_, Rprm, Rrs], wr=[Rh_])
                            yb, Ryb = YB[2][2 * h + ec]
                            P.op("dve", lambda e, ec=ec, h_=h_, yb=yb, c0=c0, n=n: e.tensor_tensor(out=yb[:, c0:c0 + n], in0=h_, in1=so[ec][0][:, c0:c0 + n], op=ALU.mult),
                                 rd=[Rh_, so[ec][1]], wr=[Ryb])
                        if ng != 1:
                            km, Rkm = rkm.get()
                            rc_, Rrc = rcols[c]
                            if ng == 1:
                                P.op("dve", lambda e, h=h, c=c, n=n, km=km, rc_=rc_: e.tensor_scalar(out=km[0:n, 0, :], in0=ktm[c][0][0:n, :], scalar1=rc_[0:n, h:h + 1], scalar2=None, op0=ALU.mult),
                                     rd=[ktm[c][1], Rrc], wr=[Rkm])
                            else:
                                kt, Rkt = rk256.get()
                                P.op("dve", lambda e, h=h, c=c, n=n, kt=kt, rc_=rc_: e.tensor_scalar(out=kt[0:n, 0:256], in0=ktm[c][0][0:n, :], scalar1=rc_[0:n, h:h + 1], scalar2=None, op0=ALU.mult),
                                     rd=[ktm[c][1], Rrc], wr=[Rkt])
                                P.op("dve", lambda e, n=n, kt=kt, km=km, ng=ng: e.tensor_tensor(out=km[0:n, :, :], in0=kt[0:n, 0:256].unsqueeze(1).to_broadcast([n, ng, 256]),
                                                                                               in1=gmask[0:n, 0:ng].unsqueeze(2).to_broadcast([n, ng, 256]), op=ALU.mult),
                                     rd=[Rkt, Rcst], wr=[Rkm])
                            for g in range(ng):
                                csc = bc[:, 1, (g + 1) * gl - 1:(g + 1) * gl]
                                for dc in range(2):
                                    pu, Rpu = psum()
                                    P.op("pe", lambda e, g=g, dc=dc, n=n, c=c, pu=pu, km=km: e.matmul(pu[:, 0:257], lhsT=km[0:n, g, dc * 128:(dc + 1) * 128], rhs=mvt[c][0][0:n, 0:257], start=True, stop=True),
                                         rd=[Rkm, mvt[c][1]], wr=[Rpu])
                                    tc_, Rtc = rtc.get()
                                    P.op("dve", lambda e, g=g, dc=dc, pu=pu, tc_=tc_, C=C: e.tensor_tensor(out=tc_, in0=C[:, g, dc, :], in1=pu[:, 0:257], op=ALU.add), rd=[RCh, Rpu], wr=[Rtc])
                                    P.op("act", lambda e, g=g, dc=dc, tc_=tc_, C=C, csc=csc: e.activation(out=C[:, g, dc, :], in_=tc_, func=AF.Copy, scale=csc), rd=[Rtc, Rbc], wr=[RCh])
                            if c < NCH - 1:
                                refresh_shadow()
                    if is_sample:
                        for dc in range(2):
                            P.dma("sp", lambda e, h=h, dc=dc, C=C: e.dma_start(out=o_C[1][l, :, h, dc * 128:(dc + 1) * 128, :].rearrange("g p e -> p g e"), in_=C[:, :, dc, 0:256]), rd=[RCh], own=RCh, store=True)
                        P.op("dve", lambda e, h=h, C=C: e.tensor_copy(out=nout[:, l, :, h, :], in_=C[:, :, :, 256]), rd=[RCh], wr=[Rnout])

            def merge_out(l):
                for og in range(8):
                    gates = []
                    for b in range(3):
                        buf, Rw = wload("w_in", l, 0, D, O_MG + b * D + og * 256, 256)
                        for j in range(2):
                            pg, Rpg = psum()
                            proj_fm(pg, Rpg, buf, Rw, j * 128, 128, XN)
                            g_, Rg_ = rB.get()
                            P.op("act", lambda e, pg=pg, g_=g_: e.activation(out=g_, in_=pg[:, 0:N], func=AF.Sigmoid), rd=[Rpg], wr=[Rg_])
                            gates.append((g_, Rg_))
                    accs = [rF.get() for _ in range(2)]
                    for b in range(3):
                        buf, Rw = wload("w_br%d" % b, l, 0, 1024, og * 256, 256)
                        for j in range(2):
                            pp, Rpp = psum()
                            proj_fm(pp, Rpp, buf, Rw, j * 128, 128, YB[b])
                            g_, Rg_ = gates[b * 2 + j]
                            acc, Racc = accs[j]
                            if b == 0:
                                P.op("dve", lambda e, pp=pp, g_=g_, acc=acc: e.tensor_tensor(out=acc, in0=pp[:, 0:N], in1=g_, op=ALU.mult), rd=[Rpp, Rg_], wr=[Racc])
                            else:
                                t1, Rt1 = rF.get()
                                P.op("dve", lambda e, pp=pp, g_=g_, t1=t1: e.tensor_tensor(out=t1, in0=pp[:, 0:N], in1=g_, op=ALU.mult), rd=[Rpp, Rg_], wr=[Rt1])
                                if b == 1:
                                    P.op("dve", lambda e, t1=t1, acc=acc: e.tensor_tensor(out=acc, in0=acc, in1=t1, op=ALU.add), rd=[Rt1, Racc], wr=[Racc])
                                else:
                                    oc = og * 2 + j
                                    P.op("dve", lambda e, t1=t1, acc=acc, oc=oc: e.tensor_tensor(out=MG[oc][0], in0=acc, in1=t1, op=ALU.add), rd=[Rt1, Racc], wr=[MG[oc][1]])
                for og in range(8):
                    buf, Rw = wload("w_out", l, 0, D, og * 256, 256)
                    for j in range(2):
                        oc = og * 2 + j
                        po, Rpo = psum()
                        proj_fm(po, Rpo, buf, Rw, j * 128, 128, MG)
                        resid_add(l, 2, oc, po, Rpo)

            def mlp(l):
                P.alias_after(hg_res, mix_res)
                for hg in range(8):
                    hb = HG[hg % 2]
                    for t in range(4):
                        buf, Rw = wload("w_ff1", l, 0, D, hg * 1024 + t * 256, 256)
                        for j in range(2):
                            hc = t * 2 + j
                            ph, Rph = psum()
                            proj_fm(ph, Rph, buf, Rw, j * 128, 128, XN)
                            r_, Rr_ = rF.get()
                            P.op("act", lambda e, ph=ph, r_=r_: e.activation(out=r_, in_=ph[:, 0:N], func=AF.Relu), rd=[Rph], wr=[Rr_])
                            P.op("dve", lambda e, r_=r_, hc=hc, hb=hb: e.tensor_tensor(out=hb[hc][0], in0=r_, in1=r_, op=ALU.mult), rd=[Rr_], wr=[hb[hc][1]])
                    for og in range(8):
                        buf, Rw = wload("w_ff2", l, hg * 1024, 1024, og * 256, 256)
                        for j in range(2):
                            oc = og * 2 + j
                            po, Rpo = psum()
                            proj_fm(po, Rpo, buf, Rw, j * 128, 128, hb)
                            resid_add(l, 5, oc, po, Rpo)
                P.alias_after(mix_res, hg_res)


            for ti in range(ntiles):
                src = i_xs if is_sample else i_xp[:, :, ti * N:(ti + 1) * N]
                for kc in range(KC):
                    P.dma("sp", lambda e, kc=kc, src=src: e.dma_start(out=X[kc][0], in_=src[:, kc, :]), wr=[X[kc][1]], own=X[kc][1])
                for l in range(NL):
                    norm_mod(l, 0)
                    lru_branch(l)
                    gla_branch(l, None if is_sample else Sst[l][0], None if is_sample else Sst[l][1])
                    mlstm_branch(l, None if is_sample else Cst[l][0], None if is_sample else Cst[l][1])
                    merge_out(l)
                    norm_mod(l, 1)
                    mlp(l)
                rs, Rrs = rmsnorm_rstd(X, rstdN)
                dst = o_ys if is_sample else o_yp[:, :, ti * N:(ti + 1) * N]
                for kc in range(KC):
                    yt, Ryt = rF.get()
                    P.op("dve", lambda e, kc=kc, yt=yt: e.scalar_tensor_tensor(out=yt, in0=X[kc][0], scalar=pc("gf", 0, kc, 1), in1=rs, op0=ALU.mult, op1=ALU.mult),
                         rd=[X[kc][1], Rprm, Rrs], wr=[Ryt])
                    P.dma("sp", lambda e, kc=kc, yt=yt, dst=dst: e.dma_start(out=dst[:, kc, :], in_=yt), rd=[Ryt], own=Ryt, store=True)
            P.dma("sp", lambda e: e.dma_start(out=o_hst[oi], in_=hst), rd=[Rhst], own=Rhst, store=True)
            P.dma("sp", lambda e: e.dma_start(out=o_hist[oi], in_=hist), rd=[Rhist], own=Rhist, store=True)
            P.dma("sp", lambda e: e.dma_start(out=o_m[oi], in_=mout), rd=[Rmout], own=Rmout, store=True)
            if not is_sample:
                for l in range(NL):
                    S, RS = Sst[l]
                    C, RC = Cst[l]
                    P.dma("sp", lambda e, l=l, S=S: e.dma_start(out=o_gla[0][l, 0].rearrange("h p e -> p h e"), in_=S), rd=[RS], own=RS, store=True)
                    for dc in range(2):
                        P.dma("sp", lambda e, l=l, dc=dc, C=C: e.dma_start(out=o_C[0][l, 0, :, dc * 128:(dc + 1) * 128, :].rearrange("h p e -> p h e"), in_=C[:, :, dc, 0:256]), rd=[RC], own=RC, store=True)
                    P.op("dve", lambda e, l=l, C=C: e.tensor_copy(out=nout[:, l, 0], in_=C[:, :, :, 256]), rd=[RC], wr=[Rnout])
            P.dma("sp", lambda e: e.dma_start(out=o_n[oi], in_=nout), rd=[Rnout], own=Rnout, store=True)
            return allres

        prev = setup_res
        if RUN_SAMPLE:
            prev = prev + run_phase(True, prev)
        if N_PROMPT_TILES > 0:
            run_phase(False, prev + [Rmod, RmodA])
        P.emit()
    return nc, WT_LIST


def _fm(a):
    a = np.asarray(a, dtype=np.float32)
    lead = a.shape[:-1]
    a = a.reshape(-1, KC, 128)
    a = np.transpose(a, (2, 1, 0))
    return np.ascontiguousarray(a.reshape(128, KC, *lead))


def _blockdiag(w):
    out = np.zeros((NL, 8, 128, 128), np.float32)
    for b in range(32):
        out[:, :, 4 * b:4 * b + 4, 4 * b:4 * b + 4] = w.reshape(NL, 8, 32, 4, 4)[:, :, b]
    return out


def _consts():
    c = np.zeros((128, CONST_COLS), np.float32)
    o = 0
    j = np.arange(128)
    c[:, o:o + 128] = (j[None, :] >= j[:, None]); o += 128
    c[:, o:o + 128] = 1.0; o += 128
    j64 = np.arange(64)
    m = (j64[None, :] >= j64[:, None]) & ((j64[None, :] // TS) == (j64[:, None] // TS))
    c[0:64, o:o + 64] = m; o += 64
    c[0:64, o:o + 16] = (j64[:, None] // TS) == np.arange(16)[None, :]; o += 16
    for h in range(4):
        c[32 * h, o:o + 128] = 1.0; o += 128
    for h in range(4):
        c[32 * h, o + h] = 1.0
    o += 4
    c[:, o:o + 128] = np.eye(128, dtype=np.float32); o += 128
    return c


_NC_CACHE = {}
_HOOK = {}


def kernel(x_prompt, x_sample, c_prompt, c_sample, state_lru_h, state_lru_conv, state_gla,
           state_mlstm_C, state_mlstm_n, state_mlstm_m, state_mlstm_conv,
           w_ada, b_ada, g_norm1, g_norm2, w_in, lru_conv_w, lru_conv_b, lru_w_a, lru_b_a,
           lru_w_x, lru_b_x, lru_lam, gla_w_g2, gla_b_g, gla_g_norm, ml_conv_w, ml_conv_b,
           ml_w_q, ml_w_k, ml_w_v, ml_b_if, ml_g_norm, w_br_lru, w_br_gla, w_br_ml, w_out,
           w_ff1, w_ff2, g_final):
    f = lambda a: np.ascontiguousarray(np.asarray(a, dtype=np.float32))
    poff, NPRM = prm_layout()
    prm = np.zeros((128, NPRM), np.float32)

    def put(name, l, arr2d):
        o = poff[(name, l)]
        prm[:, o:o + arr2d.shape[1]] = arr2d

    def cols(v, n):
        return np.asarray(v, np.float32).reshape(n, 128).T

    for l in range(NL):
        put("g1", l, cols(g_norm1[l], 16))
        put("g2", l, cols(g_norm2[l], 16))
        put("bada", l, cols(b_ada[l], 96))
        put("lcw", l, np.transpose(np.asarray(lru_conv_w[l], np.float32).reshape(4, 8, 128), (2, 1, 0)).reshape(128, 32))
        put("lcb", l, cols(lru_conv_b[l], 8))
        put("lba", l, cols(lru_b_a[l], 8))
        put("lbx", l, cols(lru_b_x[l], 8))
        put("lam", l, cols(lru_lam[l], 8))
        put("gbg", l, cols(gla_b_g[l], 4))
        put("ggn", l, cols(gla_g_norm[l], 2))
        put("mcw", l, np.transpose(np.asarray(ml_conv_w[l], np.float32).reshape(4, 8, 128), (2, 1, 0)).reshape(128, 32))
        put("mcb", l, cols(ml_conv_b[l], 8))
        bif = np.zeros((128, 2), np.float32)
        bif[:, 1] = 50.0
        for h in range(4):
            bif[32 * h, 0] = ml_b_if[l][h]
            bif[32 * h, 1] = ml_b_if[l][4 + h]
        put("mbif", l, bif)
        put("mgn", l, cols(ml_g_norm[l], 2))
    put("gf", 0, cols(g_final, 16))
    cst = _consts()
    w_in = f(w_in)
    wif = np.zeros((NL, D, 256), np.float32)
    for h in range(4):
        wif[:, :, 32 * h] = w_in[:, :, O_MIF + h]
        wif[:, :, 128 + 32 * h] = w_in[:, :, O_MIF + 4 + h]
    if "nc" not in _NC_CACHE:
        _NC_CACHE["nc"] = build_program()
    nc, wt_list = _NC_CACHE["nc"]
    srcs = {"w_ada": f(w_ada), "w_in": w_in, "w_if": wif, "w_br0": f(w_br_lru), "w_br1": f(w_br_gla), "w_br2": f(w_br_ml),
            "w_out": f(w_out), "w_ff1": f(w_ff1), "w_ff2": f(w_ff2)}
    w_all = np.zeros((NT_MAX, 128, 4096), np.float32)
    for idx, (name, l, r0, nr, c0, ncn) in enumerate(wt_list):
        kcn = nr // 128
        blk = srcs[name][l][r0:r0 + nr, c0:c0 + ncn]
        w_all[idx, :, :kcn * ncn] = blk.reshape(kcn, 128, ncn).transpose(1, 0, 2).reshape(128, kcn * ncn)
    del srcs
    shared = dict(prm=prm, cst=cst, w_all=w_all, lru_w_a=f(lru_w_a), lru_w_x=f(lru_w_x),
                  gla_w_g2=f(gla_w_g2), ml_wq=_blockdiag(f(ml_w_q)), ml_wk=_blockdiag(f(ml_w_k)), ml_wv=_blockdiag(f(ml_w_v)))
    x_prompt = f(x_prompt); x_sample = f(x_sample)
    in_maps = []
    xp_fm = [_fm(x_prompt[s]) for s in range(4)]
    for c in range(NCORE):
        s = c % 4
        sl = slice(GS * c, GS * (c + 1))
        m = dict(shared)
        m["xp"] = xp_fm[s]
        m["xs"] = _fm(x_sample[sl]).reshape(128, KC, GS * TS)
        ct = np.concatenate([np.asarray(c_prompt, np.float32)[s:s + 1], np.asarray(c_sample, np.float32)[sl]], 0)
        m["ct"] = _fm(ct)
        m["s_hst"] = np.ascontiguousarray(np.transpose(f(state_lru_h)[:, sl].reshape(NL, GS, 8, 128), (3, 0, 2, 1)))
        hl = np.transpose(f(state_lru_conv)[:, sl].reshape(NL, GS, 3, 8, 128), (4, 0, 3, 1, 2))
        hm = np.transpose(f(state_mlstm_conv)[:, sl].reshape(NL, GS, 3, 8, 128), (4, 0, 3, 1, 2))
        m["s_hist"] = np.ascontiguousarray(np.stack([hl, hm], axis=2))
        mm = np.zeros((128, NL, GS), np.float32)
        sm = f(state_mlstm_m)[:, sl]
        for h in range(4):
            mm[32 * h] = sm[:, :, h]
        m["s_m"] = mm
        m["s_n"] = np.ascontiguousarray(np.transpose(f(state_mlstm_n)[:, sl].reshape(NL, GS, 4, 2, 128), (4, 0, 1, 2, 3)))
        m["s_gla"] = np.ascontiguousarray(f(state_gla)[:, sl])
        m["s_C"] = np.ascontiguousarray(f(state_mlstm_C)[:, sl])
        in_maps.append(m)
    if _HOOK.get("in_maps_only"):
        return in_maps
    res = run_bass_kernel_spmd(nc, in_maps, core_ids=list(range(NCORE)))
    R = res.results

    def unfm(a):
        return np.ascontiguousarray(np.transpose(a, (2, 1, 0)).reshape(a.shape[2], D))

    y_prompt = np.stack([unfm(R[s]["o_yp"]) for s in range(4)], 0)
    y_sample = np.concatenate([unfm(R[c]["o_ys"]).reshape(GS, TS, D) for c in range(NCORE)], 0)

    def gather(key_idx, cores, fn):
        return np.concatenate([fn(R[c], key_idx) for c in cores], axis=1)

    def st_lru_h(r, i):
        return np.transpose(r["o_hst%d" % i], (1, 3, 2, 0)).reshape(NL, -1, 1024)

    def st_conv(cv):
        def fn(r, i):
            a = r["o_hist%d" % i][:, :, cv]
            return np.transpose(a, (1, 3, 4, 2, 0)).reshape(NL, a.shape[3], 3, 1024)
        return fn

    def st_m(r, i):
        return np.stack([r["o_m%d" % i][32 * h] for h in range(4)], axis=-1)

    def st_n(r, i):
        return np.transpose(r["o_n%d" % i], (1, 2, 3, 4, 0)).reshape(NL, -1, 4, 256)

    outs = [y_prompt, y_sample]
    for i, cores in ((0, range(4)), (1, range(NCORE))):
        outs += [gather(i, cores, st_lru_h), gather(i, cores, st_conv(0)),
                 gather(i, cores, lambda r, i: r["o_gla%d" % i]), gather(i, cores, lambda r, i: r["o_C%d" % i]),
                 gather(i, cores, st_n), gather(i, cores, st_m), gather(i, cores, st_conv(1))]
    return tuple(np.ascontiguousarray(o.astype(np.float32)) for o in outs)
```

```python
import types
import numpy as np
from contextlib import ExitStack
import concourse.bass as bass
import concourse.mybir as mybir
from concourse.bass_utils import run_bass_kernel_spmd

F32 = mybir.dt.float32
BF16 = mybir.dt.bfloat16
AF = mybir.ActivationFunctionType
ALU = mybir.AluOpType

D = 2048
KC = 16
DIN = 12312
NL = 2
EPS = 1e-6
NCORE = 8
GS = 16
TS = 4
O_LRUX, O_GQ, O_GK, O_GV, O_GLR, O_GG, O_MX, O_MO, O_MIF, O_MG = 0, 1024, 1536, 2048, 3072, 3088, 4112, 5136, 6160, 6168

N_PROMPT_TILES = 4
RUN_SAMPLE = True
NT_MAX = 460


def prm_layout():
    off = {}
    n = 0
    for l in range(NL):
        for name, w in (("g1", 16), ("g2", 16), ("bada", 96), ("lcw", 32), ("lcb", 8), ("lba", 8), ("lbx", 8),
                        ("lam", 8), ("gbg", 4), ("ggn", 2), ("mcw", 32), ("mcb", 8), ("mbif", 2), ("mgn", 2)):
            off[(name, l)] = n
            n += w
    off[("gf", 0)] = n
    n += 16
    return off, n


CONST_COLS = 128 + 128 + 64 + 16 + 4 * 128 + 4 + 128


def _freeze(fn):
    if fn.__closure__ is None:
        return fn
    cells = []
    for c in fn.__closure__:
        try:
            cells.append(types.CellType(c.cell_contents))
        except ValueError:
            cells.append(c)
    return types.FunctionType(fn.__code__, fn.__globals__, fn.__name__, fn.__defaults__, tuple(cells))


class Res:
    __slots__ = ("name", "w", "r", "sem", "cnt")

    def __init__(self, name):
        self.name = name
        self.w = None
        self.r = {}
        self.sem = None
        self.cnt = 0


class Prog:
    ENG = ("pe", "act", "dve", "pool", "sp")

    def __init__(self, nc, stack):
        self.nc = nc
        self.stack = stack
        self.ins = {e: [] for e in self.ENG}
        self.sems = {e: stack.enter_context(nc.semaphore("s_" + e)) for e in self.ENG}
        self.store_toks = []
        self.nsem = 0

    def _deps(self, rd, wr):
        deps = set()
        for r in rd:
            if r.w is not None:
                deps.add(r.w)
        for w in wr:
            if w.w is not None:
                deps.add(w.w)
            deps.update(w.r.values())
        return deps

    def op(self, eng, fn, rd=(), wr=()):
        i = len(self.ins[eng])
        tok = ("c", eng, i)
        deps = self._deps(rd, wr)
        if eng == "pe":
            deps = {d for d in deps if not (d[0] == "c" and d[1] == "pe")}
        self.ins[eng].append((_freeze(fn), deps, None))
        for r in rd:
            r.r[eng] = tok
        for w in wr:
            w.w = tok
            w.r = {}
        return tok

    def dma(self, q, fn, rd=(), wr=(), own=None, store=False):
        if own.sem is None:
            own.sem = self.stack.enter_context(self.nc.semaphore("d%d" % self.nsem))
            self.nsem += 1
        deps = self._deps(rd, wr)
        own.cnt += 1
        tok = ("d", own, own.cnt * 16)
        self.ins[q].append((_freeze(fn), deps, own))
        for r in rd:
            r.r["dma" + own.name] = tok
        for w in wr:
            w.w = tok
            w.r = {}
        if store:
            self.store_toks.append(tok)
        return tok

    def alias_after(self, new, old):
        acc = {}
        for o in old:
            toks = list(o.r.items())
            if o.w is not None:
                toks.append(("w" + o.name, o.w))
            for k, t in toks:
                if t[0] == "c":
                    key = t[1]
                    if key not in acc or acc[key][2] < t[2]:
                        acc[key] = t
                else:
                    key = "d" + t[1].name
                    if key not in acc or acc[key][2] < t[2]:
                        acc[key] = t
        for n_ in new:
            n_.w = None
            n_.r = dict(acc)

    def emit(self):
        nc = self.nc
        needed = {e: set() for e in self.ENG}
        for e in self.ENG:
            for (_, deps, _) in self.ins[e]:
                for d in deps:
                    if d[0] == "c":
                        needed[d[1]].add(d[2])
        mval = {}
        for e in self.ENG:
            for rank, i in enumerate(sorted(needed[e])):
                mval[(e, i)] = rank + 1
        final = list(self.store_toks)

        def run(e, eng):
            seen = {}

            def wait(sem, val):
                k = id(sem)
                if seen.get(k, 0) >= val:
                    return
                seen[k] = val
                eng.wait_ge(sem, val)

            def wait_all(deps):
                best = {}
                for d in deps:
                    sem, val = (self.sems[d[1]], mval[(d[1], d[2])]) if d[0] == "c" else (d[1].sem, d[2])
                    k = id(sem)
                    if k not in best or best[k][1] < val:
                        best[k] = (sem, val)
                for sem, val in best.values():
                    wait(sem, val)

            for i, (fn, deps, own) in enumerate(self.ins[e]):
                wait_all(deps)
                inst = fn(eng)
                if own is not None:
                    inst.then_inc(own.sem, 16)
                elif (e, i) in mval:
                    inst.then_inc(self.sems[e], 1)
            if e == "sp":
                wait_all(final)

        with nc.Block() as block:
            @block.tensor
            def _(eng):
                run("pe", eng)

            @block.scalar
            def _(eng):
                run("act", eng)

            @block.vector
            def _(eng):
                run("dve", eng)

            @block.gpsimd
            def _(eng):
                run("pool", eng)

            @block.sync
            def _(eng):
                run("sp", eng)


class Ring:
    def __init__(self, views):
        self.v = views
        self.i = 0

    def get(self):
        v = self.v[self.i % len(self.v)]
        self.i += 1
        return v


def build_program():
    nc = bass.Bass("TRN2", target_bir_lowering=False)
    poff, NPRM = prm_layout()

    def din(name, shape):
        return nc.dram_tensor(name, list(shape), F32, kind="ExternalInput").ap()

    def dout(name, shape):
        return nc.dram_tensor(name, list(shape), F32, kind="ExternalOutput").ap()

    NPT = 2048
    i_xp = din("xp", [128, KC, NPT])
    i_xs = din("xs", [128, KC, GS * TS])
    i_ct = din("ct", [128, KC, 1 + GS])
    i_prm = din("prm", [128, NPRM])
    i_cst = din("cst", [128, CONST_COLS])
    i_hst = din("s_hst", [128, NL, 8, GS])
    i_hist = din("s_hist", [128, NL, 2, 8, GS, 3])
    i_mc = din("s_m", [128, NL, GS])
    i_sn = din("s_n", [128, NL, GS, 4, 2])
    i_sgla = din("s_gla", [NL, GS, 4, 128, 256])
    i_sC = din("s_C", [NL, GS, 4, 256, 256])
    w_all = din("w_all", [NT_MAX, 128, 4096])
    w_la = din("lru_w_a", [NL, 8, 128, 128])
    w_lx = din("lru_w_x", [NL, 8, 128, 128])
    w_g2 = din("gla_w_g2", [NL, 16, 512])
    w_mq = din("ml_wq", [NL, 8, 128, 128])
    w_mk = din("ml_wk", [NL, 8, 128, 128])
    w_mv = din("ml_wv", [NL, 8, 128, 128])

    o_yp = dout("o_yp", [128, KC, NPT])
    o_ys = dout("o_ys", [128, KC, GS * TS])
    o_hst = [dout("o_hst%d" % i, [128, NL, 8, g]) for i, g in enumerate((1, GS))]
    o_hist = [dout("o_hist%d" % i, [128, NL, 2, 8, g, 3]) for i, g in enumerate((1, GS))]
    o_m = [dout("o_m%d" % i, [128, NL, g]) for i, g in enumerate((1, GS))]
    o_n = [dout("o_n%d" % i, [128, NL, g, 4, 2]) for i, g in enumerate((1, GS))]
    o_gla = [dout("o_gla%d" % i, [NL, g, 4, 128, 256]) for i, g in enumerate((1, GS))]
    o_C = [dout("o_C%d" % i, [NL, g, 4, 256, 256]) for i, g in enumerate((1, GS))]

    with ExitStack() as st:
        P = Prog(nc, st)
        ARENA_W = 52480
        arena = st.enter_context(nc.sbuf_tensor("arena", [128, ARENA_W], F32))
        ps_t = [st.enter_context(nc.psum_tensor("ps%d" % i, [128, 512], F32)) for i in range(8)]
        ps_res = [Res("ps%d" % i) for i in range(8)]
        ps_i = [0]

        def psum():
            i = ps_i[0] % 8
            ps_i[0] += 1
            return ps_t[i], ps_res[i]

        top = [0]
        uid = [0]

        def carve(words):
            a = top[0]
            top[0] += (words + 7) // 8 * 8
            assert top[0] <= ARENA_W, ("SBUF arena overflow", top[0])
            return a

        def view(shape, dt, name=None):
            n = int(np.prod(shape))
            words = n if dt == F32 else (n + 1) // 2
            a = carve(words)
            ap = arena[:, a:a + words]
            if dt != F32:
                ap = ap.bitcast(dt)[:, 0:n]
            if len(shape) >= 2:
                names = ["d%d" % i for i in range(len(shape))]
                pat = "p (" + " ".join(names) + ") -> p " + " ".join(names)
                ap = ap.rearrange(pat, **{names[i]: int(shape[i]) for i in range(1, len(shape))})
            uid[0] += 1
            return ap, Res(name or "v%d" % uid[0])

        def ring(cnt, shape, dt, name):
            return Ring([view(shape, dt, "%s%d" % (name, i)) for i in range(cnt)])

        cst, Rcst = view([CONST_COLS], F32, "cst")
        prm, Rprm = view([NPRM], F32, "prm")
        drv, Rdrv = view([NL, 40], F32, "drv")
        modT0, Rmod0 = view([NL, 96], F32, "modT0")
        modA0, RmodA0 = view([NL, 2, 16], F32, "modA0")
        c0_ = 0
        mask128 = cst[:, c0_:c0_ + 128]; c0_ += 128
        ones_f = cst[:, c0_:c0_ + 128]; c0_ += 128
        mask64 = cst[:, c0_:c0_ + 64]; c0_ += 64
        gmask = cst[:, c0_:c0_ + 16]; c0_ += 16
        sel = [cst[:, c0_ + 128 * h:c0_ + 128 * (h + 1)] for h in range(4)]; c0_ += 512
        E4 = cst[:, c0_:c0_ + 4]; c0_ += 4
        ident = cst[:, c0_:c0_ + 128]; c0_ += 128
        ones_b, Rob = view([128], BF16, "ones_b")
        wbuf = [view([KC, 256], BF16, "wbuf%d" % i) for i in range(3)]
        wring = Ring(wbuf)
        sring = ring(6, [128], BF16, "smallw")
        wg2, Rwg2 = view([512], BF16, "wg2")

        def pc(name, l, j=0, n=1):
            o = poff[(name, l)] + j
            return prm[:, o:o + n]

        P.dma("sp", lambda e: e.dma_start(out=cst, in_=i_cst), wr=[Rcst], own=Rcst)
        P.dma("sp", lambda e: e.dma_start(out=prm, in_=i_prm), wr=[Rprm], own=Rprm)
        P.op("dve", lambda e: e.memset(ones_b, 1.0), wr=[Rob])
        P.op("dve", lambda e: e.memset(drv, 0.0), wr=[Rdrv])
        for l in range(NL):
            P.op("act", lambda e, l=l: e.activation(out=drv[:, l, 0:8], in_=pc("lam", l, 0, 8), func=AF.Exp, scale=-1.0), rd=[Rprm, Rdrv], wr=[Rdrv])
            P.op("act", lambda e, l=l: e.activation(out=drv[:, l, 0:8], in_=drv[:, l, 0:8], func=AF.Ln, bias=1.0), rd=[Rdrv], wr=[Rdrv])
            P.op("dve", lambda e, l=l: e.tensor_scalar(out=drv[:, l, 8:16], in0=drv[:, l, 0:8], scalar1=-16.0, scalar2=None, op0=ALU.mult), rd=[Rdrv], wr=[Rdrv])
            P.op("dve", lambda e, l=l: e.tensor_scalar(out=drv[:, l, 0:8], in0=drv[:, l, 0:8], scalar1=-8.0, scalar2=None, op0=ALU.mult), rd=[Rdrv], wr=[Rdrv])
            P.op("dve", lambda e, l=l: e.tensor_scalar(out=drv[:, l, 16:20], in0=pc("gbg", l, 0, 4), scalar1=-1.0, scalar2=None, op0=ALU.mult), rd=[Rprm, Rdrv], wr=[Rdrv])
            P.op("dve", lambda e, l=l: e.tensor_scalar(out=drv[:, l, 20:21], in0=pc("mbif", l, 1, 1), scalar1=-1.0, scalar2=None, op0=ALU.mult), rd=[Rprm, Rdrv], wr=[Rdrv])

        WT = {}
        WT_LIST = []

        def wload(name, l, r0, nr, c0, ncn):
            key = (name, l, r0, nr, c0, ncn)
            if key not in WT:
                WT[key] = len(WT_LIST)
                WT_LIST.append(key)
            idx = WT[key]
            assert idx < NT_MAX
            kcn = nr // 128
            buf, R = wring.get()
            flat = buf.rearrange("p k n -> p (k n)")
            P.dma("pool", lambda e: e.dma_start(out=flat[:, 0:kcn * ncn], in_=w_all[idx][:, 0:kcn * ncn], max_dma_last_dim=8192), wr=[R], own=R)
            return flat[:, 0:kcn * ncn].rearrange("p (k n) -> p k n", n=ncn), R

        def sload(src2):
            buf, R = sring.get()
            P.dma("pool", lambda e: e.dma_start(out=buf, in_=src2), wr=[R], own=R)
            return buf, R

        top_setup = top[0]
        modT, Rmod = view([NL, 96, 1 + GS], F32, "modT")
        modA, RmodA = view([NL, 2, 16, 1 + GS], F32, "modA")
        top_mods = top[0]
        ctf, Rctf = view([KC, 1 + GS], F32, "ctf")
        ctb, Rctb = view([KC, 1 + GS], BF16, "ctb")
        P.dma("sp", lambda e: e.dma_start(out=ctf, in_=i_ct), wr=[Rctf], own=Rctf)
        P.op("act", lambda e: e.activation(out=ctb, in_=ctf, func=AF.Silu), rd=[Rctf], wr=[Rctb])
        NC_ = 1 + GS
        for l in range(NL):
            for t4 in range(48):
                buf, Rw = wload("w_ada", l, 0, D, t4 * 256, 256)
                for j in range(2):
                    oc = t4 * 2 + j
                    pt, Rp = psum()
                    for kc in range(KC):
                        P.op("pe", lambda e, kc=kc, j=j, pt=pt, buf=buf: e.matmul(pt[:, 0:NC_], lhsT=buf[:, kc, j * 128:(j + 1) * 128], rhs=ctb[:, kc, :],
                                                                                    start=(kc == 0), stop=(kc == KC - 1)), rd=[Rw, Rctb], wr=[Rp])
                    P.op("act", lambda e, oc=oc, l=l, pt=pt: e.activation(out=modT[:, l, oc, :], in_=pt[:, 0:NC_], func=AF.Identity, bias=pc("bada", l, oc, 1), scale=1.0),
                         rd=[Rp, Rprm], wr=[Rmod])
            for nrm, (gname, sck) in enumerate((("g1", 1), ("g2", 4))):
                P.op("dve", lambda e, l=l, nrm=nrm, sck=sck: e.tensor_scalar(out=modA[:, l, nrm], in0=modT[:, l, sck * 16:(sck + 1) * 16, :], scalar1=1.0, scalar2=None, op0=ALU.add),
                     rd=[Rmod], wr=[RmodA])
                P.op("dve", lambda e, l=l, nrm=nrm, gname=gname: e.tensor_tensor(out=modA[:, l, nrm], in0=modA[:, l, nrm],
                                                                                in1=pc(gname, l, 0, 16).unsqueeze(2).to_broadcast([128, 16, NC_]), op=ALU.mult),
                     rd=[RmodA, Rprm], wr=[RmodA])
        P.op("dve", lambda e: e.tensor_copy(out=modT0, in_=modT[:, :, :, 0]), rd=[Rmod], wr=[Rmod0])
        P.op("dve", lambda e: e.tensor_copy(out=modA0, in_=modA[:, :, :, :, 0]), rd=[RmodA], wr=[RmodA0])
        setup_res = [Rctf, Rctb]

        def run_phase(is_sample, prev_res):
            top[0] = top_mods if is_sample else top_setup
            allres = []
            if is_sample:
                N, G, T = GS * TS, GS, TS
                chunks = [(0, GS, TS)]
                ntiles = 1
                mcol0 = 1
                oi = 1
            else:
                N, G, T = 512, 1, 512
                chunks = [(128 * c, 1, 128) for c in range(4)]
                ntiles = N_PROMPT_TILES
                mcol0 = 0
                oi = 0
            NCH = len(chunks)
            CN = chunks[0][1] * chunks[0][2]

            def V(shape, dt, name):
                ap, R = view(shape, dt, name)
                allres.append(R)
                return ap, R

            def RING(cnt, shape, dt, name):
                r = ring(cnt, shape, dt, name)
                allres.extend(R for _, R in r.v)
                return r

            X = [V([N], F32, "x%d" % k) for k in range(KC)]
            XN = [V([N], BF16, "xn%d" % k) for k in range(KC)]
            ov0 = top[0]
            YB = [[V([N], BF16, "yb%d_%d" % (b, k)) for k in range(8)] for b in range(3)]
            MG = [V([N], BF16, "mg%d" % k) for k in range(KC)]
            ov1 = top[0]
            top[0] = ov0
            HG = [[V([N], BF16, "hg%d_%d" % (i, k)) for k in range(8)] for i in range(2)]
            top[0] = max(ov1, top[0])
            mix_res = [R for b in YB for _, R in b] + [R for _, R in MG]
            hg_res = [R for hb in HG for _, R in hb]
            hst, Rhst = V([NL, 8, G], F32, "hst")
            hist, Rhist = V([NL, 2, 8, G, 3], F32, "hist")
            Fc, RFc = V([NL, G], F32, "Fc")
            Mc, RMc = V([NL, G], F32, "Mc")
            mout, Rmout = V([NL, G], F32, "mout")
            nout, Rnout = V([NL, G, 4, 2], F32, "nout")
            if is_sample:
                Sst = [V([G, 256], F32, "Sst")]
                Cst = [V([G, 2, 257], F32, "Cst")]
                nin, Rnin = V([NL, G, 4, 2], F32, "nin")
            else:
                Sst = [V([4, 256], F32, "S%d" % l) for l in range(NL)]
                Cst = [V([4, 2, 257], F32, "C%d" % l) for l in range(NL)]
            Sbf, RSbf = V([G, 256], BF16, "Sbf")
            Cbf, RCbf = V([G, 2, 257], BF16, "Cbf")
            nbc, Rnbc = V([G, 2, 128], BF16, "nbc")
            rF = RING(6, [N], F32, "rF")
            rB = RING(12, [N], BF16, "rB")
            rf = RING(8, [CN], F32, "rf")
            rb = RING(4, [128], BF16, "rb")
            rcv = RING(2, [G, T + 3], F32, "rcv")
            vtm = [V([257], BF16, "vtm%d" % c) for c in range(NCH)]
            ktm = [V([256], BF16, "ktm%d" % c) for c in range(NCH)]
            mvt = [V([257], BF16, "mvt%d" % c) for c in range(NCH)]
            rbc = RING(2, [3, CN], F32, "rbc")
            rkm = RING(2, [chunks[0][1], 256], BF16, "rkm")
            rtc = RING(2, [257], F32, "rtc")
            rk256 = RING(2, [256], BF16, "rk256") if is_sample else None
            rcol_r = RING(2, [4], F32, "rcol")
            rRs = [V([3, CN], F32, "Rrow%d" % c) for c in range(NCH)]
            rcolv = [V([4], F32, "rcolv%d" % c) for c in range(NCH)]
            rstdN = V([N], F32, "rstdN")
            glrv = V([N], BF16, "glr")
            P.alias_after(allres, prev_res)

            if is_sample:
                P.dma("sp", lambda e: e.dma_start(out=hst, in_=i_hst), wr=[Rhst], own=Rhst)
                P.dma("sp", lambda e: e.dma_start(out=hist, in_=i_hist), wr=[Rhist], own=Rhist)
                P.dma("sp", lambda e: e.dma_start(out=Mc, in_=i_mc), wr=[RMc], own=RMc)
                P.dma("sp", lambda e: e.dma_start(out=nin, in_=i_sn), wr=[Rnin], own=Rnin)
            else:
                P.op("dve", lambda e: e.memset(hst, 0.0), wr=[Rhst])
                P.op("dve", lambda e: e.memset(hist, 0.0), wr=[Rhist])
                P.op("dve", lambda e: e.memset(Mc, 0.0), wr=[RMc])
                for l in range(NL):
                    P.op("dve", lambda e, l=l: e.memset(Sst[l][0], 0.0), wr=[Sst[l][1]])
                    P.op("dve", lambda e, l=l: e.memset(Cst[l][0], 0.0), wr=[Cst[l][1]])
            P.op("dve", lambda e: e.memset(Fc, 0.0), wr=[RFc])
            for c in range(NCH):
                P.op("dve", lambda e, c=c: e.memset(mvt[c][0][:, 256:257], 1.0), wr=[mvt[c][1]])

            def gt3(ap2):
                return ap2.rearrange("p (g t) -> p g t", t=T)

            def rmsnorm_rstd(srcs, dst=None):
                n = srcs[0][0].shape[-1]
                pt, Rp = psum()
                for i, (ap, R) in enumerate(srcs):
                    sq, Rsq = (rF.get() if n == N else rf.get())
                    P.op("act", lambda e, ap=ap, sq=sq: e.activation(out=sq[:, 0:n], in_=ap, func=AF.Square), rd=[R], wr=[Rsq])
                    P.op("pe", lambda e, sq=sq, i=i, pt=pt: e.matmul(pt[:, 0:n], lhsT=ones_f, rhs=sq[:, 0:n], start=(i == 0), stop=(i == len(srcs) - 1)),
                         rd=[Rsq, Rcst], wr=[Rp])
                rs, Rrs = dst if dst is not None else (rF.get() if n == N else rf.get())
                dim = 128.0 * len(srcs)
                P.op("act", lambda e, pt=pt, rs=rs: e.activation(out=rs[:, 0:n], in_=pt[:, 0:n], func=AF.Sqrt, scale=1.0 / dim, bias=EPS), rd=[Rp], wr=[Rrs])
                P.op("dve", lambda e, rs=rs: e.reciprocal(out=rs[:, 0:n], in_=rs[:, 0:n]), rd=[Rrs], wr=[Rrs])
                return rs[:, 0:n], Rrs

            def norm_mod(l, nrm):
                rs, Rrs = rmsnorm_rstd(X, rstdN)
                shk = 0 if nrm == 0 else 3
                for kc in range(KC):
                    t1, Rt1 = rF.get()
                    P.op("dve", lambda e, kc=kc, t1=t1: e.tensor_tensor(out=t1, in0=X[kc][0], in1=rs, op=ALU.mult), rd=[X[kc][1], Rrs], wr=[Rt1])
                    if G == 1:
                        P.op("act", lambda e, kc=kc, t1=t1: e.activation(out=XN[kc][0], in_=t1, func=AF.Identity, scale=modA0[:, l, nrm, kc:kc + 1],
                                                                          bias=modT0[:, l, shk * 16 + kc:shk * 16 + kc + 1]),
                             rd=[Rt1, RmodA0, Rmod0], wr=[XN[kc][1]])
                    else:
                        P.op("dve", lambda e, kc=kc, t1=t1: e.tensor_tensor(out=gt3(t1), in0=gt3(t1),
                                                                             in1=modA[:, l, nrm, kc, mcol0:mcol0 + G].unsqueeze(2).to_broadcast([128, G, T]), op=ALU.mult),
                             rd=[Rt1, RmodA], wr=[Rt1])
                        P.op("dve", lambda e, kc=kc, t1=t1: e.tensor_tensor(out=gt3(XN[kc][0]), in0=gt3(t1),
                                                                             in1=modT[:, l, shk * 16 + kc, mcol0:mcol0 + G].unsqueeze(2).to_broadcast([128, G, T]), op=ALU.add),
                             rd=[Rt1, Rmod], wr=[XN[kc][1]])

            def proj_fm(pt, Rp, buf, Rw, wc0, ncol, acts, n=None):
                n = n or N
                K = len(acts)
                for kc in range(K):
                    P.op("pe", lambda e, kc=kc: e.matmul(pt[0:ncol, 0:n], lhsT=buf[:, kc, wc0:wc0 + ncol], rhs=acts[kc][0], start=(kc == 0), stop=(kc == K - 1)),
                         rd=[Rw, acts[kc][1]], wr=[Rp])

            def proj_tm(pt, Rp, buf, Rw, wc0, ncol, acts, c0, n, pcol0=0):
                K = len(acts)
                for kc in range(K):
                    P.op("pe", lambda e, kc=kc: e.matmul(pt[0:n, pcol0:pcol0 + ncol], lhsT=acts[kc][0][:, c0:c0 + n], rhs=buf[:, kc, wc0:wc0 + ncol],
                                                         start=(kc == 0), stop=(kc == K - 1)), rd=[Rw, acts[kc][1]], wr=[Rp])

            def resid_add(l, kind, oc, pt, Rp):
                col = kind * 16 + oc
                if G == 1:
                    P.op("dve", lambda e: e.scalar_tensor_tensor(out=X[oc][0], in0=pt[:, 0:N], scalar=modT0[:, l, col:col + 1], in1=X[oc][0], op0=ALU.mult, op1=ALU.add),
                         rd=[Rp, Rmod0, X[oc][1]], wr=[X[oc][1]])
                else:
                    t1, Rt1 = rF.get()
                    P.op("dve", lambda e: e.tensor_tensor(out=gt3(t1), in0=gt3(pt[:, 0:N]), in1=modT[:, l, col, mcol0:mcol0 + G].unsqueeze(2).to_broadcast([128, G, T]), op=ALU.mult),
                         rd=[Rp, Rmod], wr=[Rt1])
                    P.op("dve", lambda e: e.tensor_tensor(out=X[oc][0], in0=X[oc][0], in1=t1, op=ALU.add), rd=[Rt1, X[oc][1]], wr=[X[oc][1]])

            def conv(l, cname, bname, ch, cb, Rcb, out, Rout):
                o3 = gt3(out)
                P.op("dve", lambda e: e.tensor_scalar(out=o3, in0=cb[:, :, 3:3 + T], scalar1=pc(cname, l, ch * 4 + 3, 1), scalar2=pc(bname, l, ch, 1), op0=ALU.mult, op1=ALU.add),
                     rd=[Rcb, Rprm], wr=[Rout])
                for j in (2, 1, 0):
                    P.op("dve", lambda e, j=j: e.scalar_tensor_tensor(out=o3, in0=cb[:, :, j:j + T], scalar=pc(cname, l, ch * 4 + j, 1), in1=o3, op0=ALU.mult, op1=ALU.add),
                         rd=[Rcb, Rprm, Rout], wr=[Rout])

            def conv_in(l, cv, ch, pt, Rp):
                cb, Rcb = rcv.get()
                P.op("act", lambda e: e.activation(out=cb[:, :, 0:3], in_=hist[:, l, cv, ch], func=AF.Copy), rd=[Rhist], wr=[Rcb])
                P.op("act", lambda e: e.activation(out=cb[:, :, 3:3 + T], in_=gt3(pt[:, 0:N]), func=AF.Copy), rd=[Rp], wr=[Rcb])
                P.op("act", lambda e: e.activation(out=hist[:, l, cv, ch], in_=cb[:, :, T:T + 3], func=AF.Copy), rd=[Rcb], wr=[Rhist])
                return cb, Rcb

            def lru_branch(l):
                for ch in range(8):
                    if ch % 2 == 0:
                        buf, Rw = wload("w_in", l, 0, D, O_LRUX + ch * 128, 256)
                    wa, Rwa = sload(w_la[l, ch])
                    wx, Rwx = sload(w_lx[l, ch])
                    pt, Rp = psum()
                    proj_fm(pt, Rp, buf, Rw, (ch % 2) * 128, 128, XN)
                    cb, Rcb = conv_in(l, 0, ch, pt, Rp)
                    u, Ru = rF.get()
                    conv(l, "lcw", "lcb", ch, cb, Rcb, u, Ru)
                    ub, Rub = rB.get()
                    P.op("act", lambda e: e.activation(out=ub, in_=u, func=AF.Copy), rd=[Ru], wr=[Rub])
                    pa, Rpa = psum()
                    P.op("pe", lambda e: e.matmul(pa[:, 0:N], lhsT=wa, rhs=ub, start=True, stop=True), rd=[Rwa, Rub], wr=[Rpa])
                    px, Rpx = psum()
                    P.op("pe", lambda e: e.matmul(px[:, 0:N], lhsT=wx, rhs=ub, start=True, stop=True), rd=[Rwx, Rub], wr=[Rpx])
                    r, Rr = rF.get()
                    P.op("act", lambda e: e.activation(out=r, in_=pa[:, 0:N], func=AF.Sigmoid, bias=pc("lba", l, ch, 1), scale=1.0), rd=[Rpa, Rprm], wr=[Rr])
                    ig, Rig = rF.get()
                    P.op("act", lambda e: e.activation(out=ig, in_=px[:, 0:N], func=AF.Sigmoid, bias=pc("lbx", l, ch, 1), scale=1.0), rd=[Rpx, Rprm], wr=[Rig])
                    a, Ra = rF.get()
                    P.op("act", lambda e: e.activation(out=a, in_=r, func=AF.Exp, scale=drv[:, l, ch:ch + 1]), rd=[Rr, Rdrv], wr=[Ra])
                    P.op("act", lambda e: e.activation(out=r, in_=r, func=AF.Exp, scale=drv[:, l, 8 + ch:9 + ch]), rd=[Rr, Rdrv], wr=[Rr])
                    P.op("dve", lambda e: e.tensor_scalar(out=r, in0=r, scalar1=1.0, scalar2=-1.0, op0=ALU.min, op1=ALU.mult), rd=[Rr], wr=[Rr])
                    P.op("act", lambda e: e.activation(out=r, in_=r, func=AF.Sqrt, bias=1.0, scale=1.0), rd=[Rr], wr=[Rr])
                    P.op("dve", lambda e: e.tensor_tensor(out=ig, in0=ig, in1=u, op=ALU.mult), rd=[Rig, Ru], wr=[Rig])
                    P.op("dve", lambda e: e.tensor_tensor(out=ig, in0=ig, in1=r, op=ALU.mult), rd=[Rig, Rr], wr=[Rig])
                    if G == 1:
                        P.op("dve", lambda e: e.tensor_tensor_scan(out=u, data0=a, data1=ig, initial=hst[:, l, ch, 0:1], op0=ALU.mult, op1=ALU.add),
                             rd=[Ra, Rig, Rhst], wr=[Ru])
                    else:
                        a3, g3, u3 = gt3(a), gt3(ig), gt3(u)
                        for t in range(T):
                            prev = hst[:, l, ch, :] if t == 0 else u3[:, :, t - 1]
                            P.op("dve", lambda e, t=t, prev=prev: e.tensor_tensor(out=u3[:, :, t], in0=a3[:, :, t], in1=prev, op=ALU.mult), rd=[Ra, Ru, Rhst], wr=[Ru])
                            P.op("dve", lambda e, t=t: e.tensor_tensor(out=u3[:, :, t], in0=u3[:, :, t], in1=g3[:, :, t], op=ALU.add), rd=[Ru, Rig], wr=[Ru])
                    P.op("act", lambda e: e.activation(out=hst[:, l, ch, :], in_=gt3(u)[:, :, T - 1], func=AF.Copy), rd=[Ru], wr=[Rhst])
                    P.op("act", lambda e: e.activation(out=YB[0][ch][0], in_=u, func=AF.Copy), rd=[Ru], wr=[YB[0][ch][1]])

            def gla_branch(l, Sfull, RS):
                buf, Rw = wload("w_in", l, 0, D, O_GLR, 16)
                pt, Rp = psum()
                proj_fm(pt, Rp, buf, Rw, 0, 16, XN)
                glr, Rglr = glrv
                P.op("act", lambda e: e.activation(out=glr[0:16, :], in_=pt[0:16, 0:N], func=AF.Copy), rd=[Rp], wr=[Rglr])
                P.dma("pool", lambda e: e.dma_start(out=wg2[0:16, :], in_=w_g2[l]), wr=[Rwg2], own=Rwg2)
                for h in range(4):
                    if is_sample:
                        S, RSh = Sst[0]
                        P.dma("sp", lambda e, h=h: e.dma_start(out=S, in_=i_sgla[l, :, h].rearrange("g p e -> p g e")), wr=[RSh], own=RSh)
                    else:
                        S, RSh = Sfull[:, h:h + 1, :], RS
                    P.op("act", lambda e, S=S: e.activation(out=Sbf, in_=S, func=AF.Copy), rd=[RSh], wr=[RSbf])
                    pz, Rpz = psum()
                    P.op("pe", lambda e, h=h: e.matmul(pz[:, 0:N], lhsT=wg2[0:16, h * 128:(h + 1) * 128], rhs=glr[0:16, :], start=True, stop=True), rd=[Rwg2, Rglr], wr=[Rpz])
                    sp_, Rsp = rF.get()
                    P.op("act", lambda e, h=h: e.activation(out=sp_, in_=pz[:, 0:N], func=AF.Exp, scale=-1.0, bias=drv[:, l, 16 + h:17 + h]), rd=[Rpz, Rdrv], wr=[Rsp])
                    P.op("act", lambda e: e.activation(out=sp_, in_=sp_, func=AF.Ln, bias=1.0), rd=[Rsp], wr=[Rsp])
                    cs, Rcs = rF.get()
                    if G == 1:
                        for (c0, ng, gl) in chunks:
                            P.op("dve", lambda e, c0=c0, gl=gl: e.tensor_tensor_scan(out=cs[:, c0:c0 + gl], data0=ones_f[:, 0:gl], data1=sp_[:, c0:c0 + gl], initial=0.0,
                                                                                     op0=ALU.mult, op1=ALU.add), rd=[Rsp, Rcst], wr=[Rcs])
                    else:
                        s3, c3 = gt3(sp_), gt3(cs)
                        P.op("dve", lambda e: e.tensor_copy(out=c3[:, :, 0], in_=s3[:, :, 0]), rd=[Rsp], wr=[Rcs])
                        for t in range(1, T):
                            P.op("dve", lambda e, t=t: e.tensor_tensor(out=c3[:, :, t], in0=c3[:, :, t - 1], in1=s3[:, :, t], op=ALU.add), rd=[Rsp, Rcs], wr=[Rcs])
                    eb, Reb = rF.get()
                    P.op("act", lambda e: e.activation(out=eb, in_=cs, func=AF.Exp, scale=-1.0 / 16.0), rd=[Rcs], wr=[Reb])
                    P.op("act", lambda e: e.activation(out=cs, in_=cs, func=AF.Exp, scale=1.0 / 16.0), rd=[Rcs], wr=[Rcs])
                    buf, Rw = wload("w_in", l, 0, D, O_GQ + h * 128, 128)
                    pq, Rpq = psum()
                    proj_fm(pq, Rpq, buf, Rw, 0, 128, XN)
                    qe, Rqe = rB.get()
                    P.op("dve", lambda e: e.scalar_tensor_tensor(out=qe, in0=pq[:, 0:N], scalar=128.0 ** -0.5, in1=eb, op0=ALU.mult, op1=ALU.mult), rd=[Rpq, Reb], wr=[Rqe])
                    buf, Rw = wload("w_in", l, 0, D, O_GK + h * 128, 128)
                    pk, Rpk = psum()
                    proj_fm(pk, Rpk, buf, Rw, 0, 128, XN)
                    ke, Rke = rF.get()
                    P.op("dve", lambda e: e.tensor_tensor(out=ke, in0=pk[:, 0:N], in1=cs, op=ALU.mult), rd=[Rpk, Rcs], wr=[Rke])
                    keb, Rkeb = rB.get()
                    P.op("act", lambda e: e.activation(out=keb, in_=ke, func=AF.Copy), rd=[Rke], wr=[Rkeb])
                    buf, Rw = wload("w_in", l, 0, D, O_GV + h * 256, 256)
                    for c, (c0, ng, gl) in enumerate(chunks):
                        n = ng * gl
                        pv, Rpv = psum()
                        proj_tm(pv, Rpv, buf, Rw, 0, 256, XN, c0, n)
                        P.op("act", lambda e, c=c, n=n, pv=pv: e.activation(out=vtm[c][0][0:n, 0:256], in_=pv[0:n, 0:256], func=AF.Copy), rd=[Rpv], wr=[vtm[c][1]])
                    buf, Rw = wload("w_in", l, 0, D, O_GG + h * 256, 256)
                    sg = []
                    for ec in range(2):
                        pg, Rpg = psum()
                        proj_fm(pg, Rpg, buf, Rw, ec * 128, 128, XN)
                        s_, Rs_ = rB.get()
                        P.op("act", lambda e, pg=pg, s_=s_: e.activation(out=s_, in_=pg[:, 0:N], func=AF.Silu), rd=[Rpg], wr=[Rs_])
                        sg.append((s_, Rs_))
                    pending = []
                    for c, (c0, ng, gl) in enumerate(chunks):
                        n = ng * gl
                        msk = mask128 if not is_sample else mask64
                        pa, Rpa = psum()
                        P.op("pe", lambda e, c0=c0, n=n, pa=pa: e.matmul(pa[0:n, 0:n], lhsT=keb[:, c0:c0 + n], rhs=qe[:, c0:c0 + n], start=True, stop=True), rd=[Rkeb, Rqe], wr=[Rpa])
                        AT, RAT = rb.get()
                        P.op("dve", lambda e, n=n, pa=pa, AT=AT, msk=msk: e.tensor_tensor(out=AT[0:n, 0:n], in0=pa[0:n, 0:n], in1=msk[0:n, 0:n], op=ALU.mult), rd=[Rpa, Rcst], wr=[RAT])
                        while pending:
                            pending.pop(0)()
                        osb = []
                        for ec in range(2):
                            po, Rpo = psum()
                            nmm = 1 + ng
                            P.op("pe", lambda e, c=c, n=n, ec=ec, po=po, AT=AT: e.matmul(po[:, 0:n], lhsT=vtm[c][0][0:n, ec * 128:(ec + 1) * 128], rhs=AT[0:n, 0:n], start=True, stop=False),
                                 rd=[vtm[c][1], RAT], wr=[Rpo])
                            for g in range(ng):
                                P.op("pe", lambda e, g=g, ec=ec, po=po, c0=c0, gl=gl, ng=ng: e.matmul(po[:, g * gl:(g + 1) * gl], lhsT=Sbf[:, g, ec * 128:(ec + 1) * 128],
                                                                                                    rhs=qe[:, c0 + g * gl:c0 + (g + 1) * gl], start=False, stop=(g == ng - 1)),
                                     rd=[RSbf, Rqe], wr=[Rpo])
                            o_, Ro_ = rf.get()
                            P.op("act", lambda e, n=n, po=po, o_=o_: e.activation(out=o_[:, 0:n], in_=po[:, 0:n], func=AF.Copy), rd=[Rpo], wr=[Ro_])
                            osb.append((o_[:, 0:n], Ro_))
                        kd, Rkd = rf.get()
                        kd3 = kd[:, 0:n].rearrange("p (g t) -> p g t", t=gl)
                        ke3 = ke[:, c0:c0 + n].rearrange("p (g t) -> p g t", t=gl)
                        eb3 = eb[:, c0:c0 + n].rearrange("p (g t) -> p g t", t=gl)
                        P.op("dve", lambda e, kd3=kd3, ke3=ke3, eb3=eb3, ng=ng, gl=gl: e.tensor_tensor(out=kd3, in0=ke3, in1=eb3[:, :, gl - 1:gl].to_broadcast([128, ng, gl]), op=ALU.mult),
                             rd=[Rke, Reb], wr=[Rkd])
                        ptr, Rptr = psum()
                        P.op("pe", lambda e, n=n, kd=kd, ptr=ptr: e.transpose(out=ptr[0:n, 0:128], in_=kd[:, 0:n], identity=ident), rd=[Rkd, Rcst], wr=[Rptr])
                        km, Rkm = rkm.get()
                        if ng == 1:
                            P.op("act", lambda e, n=n, ptr=ptr, km=km: e.activation(out=km[0:n, 0, 0:128], in_=ptr[0:n, 0:128], func=AF.Copy), rd=[Rptr], wr=[Rkm])
                        else:
                            kt, Rkt = rb.get()
                            P.op("act", lambda e, n=n, ptr=ptr, kt=kt: e.activation(out=kt[0:n, 0:128], in_=ptr[0:n, 0:128], func=AF.Copy), rd=[Rptr], wr=[Rkt])
                            P.op("dve", lambda e, n=n, kt=kt, km=km, ng=ng: e.tensor_tensor(out=km[0:n, :, 0:128], in0=kt[0:n, 0:128].unsqueeze(1).to_broadcast([n, ng, 128]),
                                                                                           in1=gmask[0:n, 0:ng].unsqueeze(2).to_broadcast([n, ng, 128]), op=ALU.mult),
                                 rd=[Rkt, Rcst], wr=[Rkm])
                        for g in range(ng):
                            pu, Rpu = psum()
                            P.op("pe", lambda e, g=g, n=n, c=c, pu=pu, km=km: e.matmul(pu[:, 0:256], lhsT=km[0:n, g, 0:128], rhs=vtm[c][0][0:n, 0:256], start=True, stop=True),
                                 rd=[Rkm, vtm[c][1]], wr=[Rpu])
                            ebl = eb[:, c0 + (g + 1) * gl - 1:c0 + (g + 1) * gl]
                            P.op("dve", lambda e, g=g, pu=pu, ebl=ebl, S=S: e.scalar_tensor_tensor(out=S[:, g, :], in0=S[:, g, :], scalar=ebl, in1=pu[:, 0:256], op0=ALU.mult, op1=ALU.add),
                                 rd=[RSh, Reb, Rpu], wr=[RSh])
                        if c < NCH - 1:
                            P.op("act", lambda e, S=S: e.activation(out=Sbf, in_=S, func=AF.Copy), rd=[RSh], wr=[RSbf])
                        def _epi(osb=osb, c0=c0, n=n):
                            rs, Rrs = rmsnorm_rstd(osb)
                            for ec in range(2):
                                o_, Ro_ = osb[ec]
                                P.op("dve", lambda e, ec=ec, o_=o_: e.scalar_tensor_tensor(out=o_, in0=o_, scalar=pc("ggn", l, ec, 1), in1=rs, op0=ALU.mult, op1=ALU.mult),
                                     rd=[Ro_, Rprm, Rrs], wr=[Ro_])
                                yb, Ryb = YB[1][2 * h + ec]
                                P.op("dve", lambda e, ec=ec, o_=o_, yb=yb, c0=c0, n=n: e.tensor_tensor(out=yb[:, c0:c0 + n], in0=o_, in1=sg[ec][0][:, c0:c0 + n], op=ALU.mult),
                                     rd=[Ro_, sg[ec][1]], wr=[Ryb])
                        if ng == 1:
                            pending.append(_epi)
                        else:
                            _epi()
                    while pending:
                        pending.pop(0)()
                    if is_sample:
                        P.dma("sp", lambda e, h=h, S=S: e.dma_start(out=o_gla[1][l, :, h].rearrange("g p e -> p g e"), in_=S), rd=[RSh], own=RSh, store=True)

            def mlstm_branch(l, Cfull, RC):
                buf, Rw = wload("w_if", l, 0, D, 0, 256)
                pi_, Rpi = psum()
                proj_fm(pi_, Rpi, buf, Rw, 0, 128, XN)
                pf_, Rpf = psum()
                proj_fm(pf_, Rpf, buf, Rw, 128, 128, XN)
                lfn, Rlfn = rF.get()
                P.op("act", lambda e: e.activation(out=lfn, in_=pf_[:, 0:N], func=AF.Exp, scale=-1.0, bias=drv[:, l, 20:21]), rd=[Rpf, Rdrv], wr=[Rlfn])
                P.op("act", lambda e: e.activation(out=lfn, in_=lfn, func=AF.Ln, bias=1.0), rd=[Rlfn], wr=[Rlfn])
                Fn, RFn = rF.get()
                gg, Rgg = rF.get()
                Mt, RMt = rF.get()
                if G == 1:
                    for (c0, ng, gl) in chunks:
                        ini = Fc[:, l, 0:1] if c0 == 0 else Fn[:, c0 - 1:c0]
                        P.op("dve", lambda e, c0=c0, gl=gl, ini=ini: e.tensor_tensor_scan(out=Fn[:, c0:c0 + gl], data0=ones_f[:, 0:gl], data1=lfn[:, c0:c0 + gl], initial=ini,
                                                                                          op0=ALU.mult, op1=ALU.add), rd=[Rlfn, RFc, Rcst, RFn], wr=[RFn])
                    P.op("dve", lambda e: e.scalar_tensor_tensor(out=gg, in0=pi_[:, 0:N], scalar=pc("mbif", l, 0, 1), in1=Fn, op0=ALU.add, op1=ALU.add), rd=[Rpi, Rprm, RFn], wr=[Rgg])
                    P.op("dve", lambda e: e.tensor_tensor_scan(out=Mt, data0=gg, data1=gg, initial=Mc[:, l, 0:1], op0=ALU.max, op1=ALU.max), rd=[Rgg, RMc], wr=[RMt])
                else:
                    l3, F3, g3, M3 = gt3(lfn), gt3(Fn), gt3(gg), gt3(Mt)
                    P.op("dve", lambda e: e.tensor_copy(out=F3[:, :, 0], in_=l3[:, :, 0]), rd=[Rlfn], wr=[RFn])
                    for t in range(1, T):
                        P.op("dve", lambda e, t=t: e.tensor_tensor(out=F3[:, :, t], in0=F3[:, :, t - 1], in1=l3[:, :, t], op=ALU.add), rd=[Rlfn, RFn], wr=[RFn])
                    P.op("dve", lambda e: e.scalar_tensor_tensor(out=gg, in0=pi_[:, 0:N], scalar=pc("mbif", l, 0, 1), in1=Fn, op0=ALU.add, op1=ALU.add), rd=[Rpi, Rprm, RFn], wr=[Rgg])
                    for t in range(T):
                        prev = Mc[:, l, :] if t == 0 else M3[:, :, t - 1]
                        P.op("dve", lambda e, t=t, prev=prev: e.tensor_tensor(out=M3[:, :, t], in0=g3[:, :, t], in1=prev, op=ALU.max), rd=[Rgg, RMt, RMc], wr=[RMt])
                Rrows = []
                for c, (c0, ng, gl) in enumerate(chunks):
                    n = ng * gl
                    R_, RR_ = rRs[c]
                    if G == 1:
                        mref = Mc[:, l, 0:1] if c == 0 else Mt[:, c0 - 1:c0]
                        mrd = RMc if c == 0 else RMt
                        P.op("dve", lambda e, c0=c0, n=n, R_=R_, mref=mref: e.tensor_scalar(out=R_[:, 0, 0:n], in0=gg[:, c0:c0 + n], scalar1=mref, scalar2=None, op0=ALU.subtract),
                             rd=[Rgg, mrd], wr=[RR_])
                        P.op("dve", lambda e, c0=c0, n=n, R_=R_, mref=mref: e.tensor_scalar(out=R_[:, 1, 0:n], in0=Mt[:, c0:c0 + n], scalar1=-1.0, scalar2=mref, op0=ALU.mult, op1=ALU.add),
                             rd=[RMt, mrd], wr=[RR_])
                    else:
                        mb = Mc[:, l, :].unsqueeze(2).to_broadcast([128, G, T])
                        P.op("dve", lambda e, R_=R_, mb=mb: e.tensor_tensor(out=gt3(R_[:, 0, 0:N]), in0=gt3(gg), in1=mb, op=ALU.subtract), rd=[Rgg, RMc], wr=[RR_])
                        P.op("dve", lambda e, R_=R_, mb=mb: e.tensor_tensor(out=gt3(R_[:, 1, 0:N]), in0=mb, in1=gt3(Mt), op=ALU.subtract), rd=[RMt, RMc], wr=[RR_])
                    if G == 1:
                        P.op("dve", lambda e, c0=c0, n=n, R_=R_, mref=mref: e.tensor_scalar(out=R_[:, 2, 0:n], in0=Fn[:, c0:c0 + n], scalar1=mref, scalar2=None, op0=ALU.subtract),
                             rd=[RFn, mrd], wr=[RR_])
                    else:
                        P.op("dve", lambda e, R_=R_, mb=mb: e.tensor_tensor(out=gt3(R_[:, 2, 0:N]), in0=gt3(Fn), in1=mb, op=ALU.subtract), rd=[RFn, RMc], wr=[RR_])
                    P.op("act", lambda e, R_=R_: e.activation(out=R_, in_=R_, func=AF.Exp), rd=[RR_], wr=[RR_])
                    Rrows.append((R_, RR_))
                P.op("dve", lambda e: e.tensor_tensor(out=mout[:, l, :], in0=gt3(Mt)[:, :, T - 1], in1=gt3(Fn)[:, :, T - 1], op=ALU.subtract), rd=[RMt, RFn], wr=[Rmout])
                if not is_sample:
                    P.op("dve", lambda e: e.tensor_copy(out=Fc[:, l, :], in_=Fn[:, N - 1:N]), rd=[RFn], wr=[RFc])
                    P.op("dve", lambda e: e.tensor_copy(out=Mc[:, l, :], in_=Mt[:, N - 1:N]), rd=[RMt], wr=[RMc])
                rcols = []
                for c, (c0, ng, gl) in enumerate(chunks):
                    n = ng * gl
                    R_, RR_ = Rrows[c]
                    pr, Rpr = psum()
                    P.op("pe", lambda e, n=n, R_=R_, pr=pr: e.matmul(pr[0:n, 0:4], lhsT=R_[:, 0, 0:n], rhs=E4, start=True, stop=True), rd=[RR_, Rcst], wr=[Rpr])
                    rc_, Rrc = rcolv[c]
                    P.op("act", lambda e, n=n, pr=pr, rc_=rc_: e.activation(out=rc_[0:n, :], in_=pr[0:n, 0:4], func=AF.Copy), rd=[Rpr], wr=[Rrc])
                    rcols.append((rc_, Rrc))
                for h in range(4):
                    if is_sample:
                        C, RCh = Cst[0]
                        for dc in range(2):
                            P.dma("sp", lambda e, h=h, dc=dc, C=C: e.dma_start(out=C[:, :, dc, 0:256], in_=i_sC[l, :, h, dc * 128:(dc + 1) * 128, :].rearrange("g p e -> p g e")), wr=[RCh], own=RCh)
                        P.op("dve", lambda e, h=h, C=C: e.tensor_copy(out=C[:, :, :, 256], in_=nin[:, l, :, h, :]), rd=[Rnin], wr=[RCh])
                    else:
                        C, RCh = Cfull[:, h:h + 1], RC

                    def refresh_shadow(C=C, RCh=RCh):
                        P.op("act", lambda e: e.activation(out=Cbf, in_=C, func=AF.Copy), rd=[RCh], wr=[RCbf])
                        P.op("dve", lambda e: e.tensor_copy(out=nbc, in_=C[:, :, :, 256:257].to_broadcast([128, C.shape[1], 2, 128])), rd=[RCh], wr=[Rnbc])
                    refresh_shadow()
                    buf, Rw = wload("w_in", l, 0, D, O_MX + h * 256, 256)
                    mqT, mkT, mxb_l, mc_l = [], [], [], []
                    for dc in range(2):
                        ch = 2 * h + dc
                        pt, Rp = psum()
                        proj_fm(pt, Rp, buf, Rw, dc * 128, 128, XN)
                        cb, Rcb = conv_in(l, 1, ch, pt, Rp)
                        mxb, Rmxb = rB.get()
                        P.op("act", lambda e, cb=cb, mxb=mxb: e.activation(out=gt3(mxb), in_=cb[:, :, 3:3 + T], func=AF.Copy), rd=[Rcb], wr=[Rmxb])
                        cv, Rcv = rF.get()
                        conv(l, "mcw", "mcb", ch, cb, Rcb, cv, Rcv)
                        mc_, Rmc_ = rB.get()
                        P.op("act", lambda e, cv=cv, mc_=mc_: e.activation(out=mc_, in_=cv, func=AF.Silu), rd=[Rcv], wr=[Rmc_])
                        wq, Rwq = sload(w_mq[l, ch])
                        wk, Rwk = sload(w_mk[l, ch])
                        wv, Rwv = sload(w_mv[l, ch])
                        pq, Rpq = psum()
                        P.op("pe", lambda e, pq=pq, wq=wq, mc_=mc_: e.matmul(pq[:, 0:N], lhsT=wq, rhs=mc_, start=True, stop=True), rd=[Rwq, Rmc_], wr=[Rpq])
                        q_, Rq_ = rB.get()
                        P.op("act", lambda e, pq=pq, q_=q_: e.activation(out=q_, in_=pq[:, 0:N], func=AF.Copy), rd=[Rpq], wr=[Rq_])
                        pk, Rpk = psum()
                        P.op("pe", lambda e, pk=pk, wk=wk, mc_=mc_: e.matmul(pk[:, 0:N], lhsT=wk, rhs=mc_, start=True, stop=True), rd=[Rwk, Rmc_], wr=[Rpk])
                        k_, Rk_ = rB.get()
                        P.op("act", lambda e, pk=pk, k_=k_: e.activation(out=k_, in_=pk[:, 0:N], func=AF.Copy, scale=1.0 / 16.0), rd=[Rpk], wr=[Rk_])
                        mqT.append((q_, Rq_)); mkT.append((k_, Rk_))
                        for c, (c0, ng, gl) in enumerate(chunks):
                            n = ng * gl
                            pkt, Rpkt = psum()
                            P.op("pe", lambda e, c0=c0, n=n, pkt=pkt, mc_=mc_, wk=wk: e.matmul(pkt[0:n, 0:128], lhsT=mc_[:, c0:c0 + n], rhs=wk, start=True, stop=True), rd=[Rmc_, Rwk], wr=[Rpkt])
                            P.op("act", lambda e, c=c, n=n, dc=dc, pkt=pkt: e.activation(out=ktm[c][0][0:n, dc * 128:(dc + 1) * 128], in_=pkt[0:n, 0:128], func=AF.Copy, scale=1.0 / 16.0),
                                 rd=[Rpkt], wr=[ktm[c][1]])
                            pvt, Rpvt = psum()
                            P.op("pe", lambda e, c0=c0, n=n, pvt=pvt, mxb=mxb, wv=wv: e.matmul(pvt[0:n, 0:128], lhsT=mxb[:, c0:c0 + n], rhs=wv, start=True, stop=True), rd=[Rmxb, Rwv], wr=[Rpvt])
                            P.op("act", lambda e, c=c, n=n, dc=dc, pvt=pvt: e.activation(out=mvt[c][0][0:n, dc * 128:(dc + 1) * 128], in_=pvt[0:n, 0:128], func=AF.Copy),
                                 rd=[Rpvt], wr=[mvt[c][1]])
                    buf, Rw = wload("w_in", l, 0, D, O_MO + h * 256, 256)
                    so = []
                    for ec in range(2):
                        pg, Rpg = psum()
                        proj_fm(pg, Rpg, buf, Rw, ec * 128, 128, XN)
                        s_, Rs_ = rB.get()
                        P.op("act", lambda e, pg=pg, s_=s_: e.activation(out=s_, in_=pg[:, 0:N], func=AF.Sigmoid), rd=[Rpg], wr=[Rs_])
                        so.append((s_, Rs_))
                    pending = []
                    for c, (c0, ng, gl) in enumerate(chunks):
                        n = ng * gl
                        msk = mask128 if not is_sample else mask64
                        R_, RR_ = Rrows[c]
                        pb, Rpb = psum()
                        P.op("pe", lambda e, h=h, n=n, R_=R_, pb=pb: e.matmul(pb[:, 0:3 * CN], lhsT=sel[h], rhs=R_.rearrange("p a b -> p (a b)"), start=True, stop=True), rd=[RR_, Rcst], wr=[Rpb])
                        bc, Rbc = rbc.get()
                        P.op("act", lambda e, pb=pb, bc=bc: e.activation(out=bc.rearrange("p a b -> p (a b)"), in_=pb[:, 0:3 * CN], func=AF.Copy), rd=[Rpb], wr=[Rbc])
                        pst, Rpst = psum()
                        for dc in range(2):
                            kr, Rkr = rb.get()
                            P.op("dve", lambda e, dc=dc, c0=c0, n=n, kr=kr, bc=bc: e.tensor_tensor(out=kr[:, 0:n], in0=mkT[dc][0][:, c0:c0 + n], in1=bc[:, 0, 0:n], op=ALU.mult),
                                 rd=[mkT[dc][1], Rbc], wr=[Rkr])
                            P.op("pe", lambda e, dc=dc, c0=c0, n=n, kr=kr, pst=pst: e.matmul(pst[0:n, 0:n], lhsT=kr[:, 0:n], rhs=mqT[dc][0][:, c0:c0 + n], start=(dc == 0), stop=(dc == 1)),
                                 rd=[Rkr, mqT[dc][1]], wr=[Rpst])
                        ST, RST = rb.get()
                        P.op("dve", lambda e, n=n, pst=pst, ST=ST, msk=msk: e.tensor_tensor(out=ST[0:n, 0:n], in0=pst[0:n, 0:n], in1=msk[0:n, 0:n], op=ALU.mult), rd=[Rpst, Rcst], wr=[RST])
                        while pending:
                            pending.pop(0)()
                        pd, Rpd = psum()
                        P.op("pe", lambda e, n=n, pd=pd, ST=ST: e.matmul(pd[:, 0:n], lhsT=ones_b[0:n, :], rhs=ST[0:n, 0:n], start=True, stop=False), rd=[Rob, RST], wr=[Rpd])
                        for g in range(ng):
                            for dc in range(2):
                                P.op("pe", lambda e, g=g, dc=dc, pd=pd, c0=c0, gl=gl, ng=ng: e.matmul(pd[:, g * gl:(g + 1) * gl], lhsT=nbc[:, g, dc, :], rhs=mqT[dc][0][:, c0 + g * gl:c0 + (g + 1) * gl],
                                                                                                    start=False, stop=(g == ng - 1 and dc == 1)), rd=[Rnbc, mqT[dc][1]], wr=[Rpd])
                        pns = []
                        for ec in range(2):
                            pn, Rpn = psum()
                            P.op("pe", lambda e, c=c, n=n, ec=ec, pn=pn, ST=ST: e.matmul(pn[:, 0:n], lhsT=mvt[c][0][0:n, ec * 128:(ec + 1) * 128], rhs=ST[0:n, 0:n], start=True, stop=False),
                                 rd=[mvt[c][1], RST], wr=[Rpn])
                            for g in range(ng):
                                for dc in range(2):
                                    P.op("pe", lambda e, g=g, dc=dc, ec=ec, pn=pn, c0=c0, gl=gl, ng=ng: e.matmul(pn[:, g * gl:(g + 1) * gl], lhsT=Cbf[:, g, dc, ec * 128:(ec + 1) * 128],
                                                                                                               rhs=mqT[dc][0][:, c0 + g * gl:c0 + (g + 1) * gl], start=False,
                                                                                                               stop=(g == ng - 1 and dc == 1)), rd=[RCbf, mqT[dc][1]], wr=[Rpn])
                            pns.append((pn, Rpn))
                        if ng == 1:
                            km, Rkm = rkm.get()
                            rc_, Rrc = rcols[c]
                            if ng == 1:
                                P.op("dve", lambda e, h=h, c=c, n=n, km=km, rc_=rc_: e.tensor_scalar(out=km[0:n, 0, :], in0=ktm[c][0][0:n, :], scalar1=rc_[0:n, h:h + 1], scalar2=None, op0=ALU.mult),
                                     rd=[ktm[c][1], Rrc], wr=[Rkm])
                            else:
                                kt, Rkt = rk256.get()
                                P.op("dve", lambda e, h=h, c=c, n=n, kt=kt, rc_=rc_: e.tensor_scalar(out=kt[0:n, 0:256], in0=ktm[c][0][0:n, :], scalar1=rc_[0:n, h:h + 1], scalar2=None, op0=ALU.mult),
                                     rd=[ktm[c][1], Rrc], wr=[Rkt])
                                P.op("dve", lambda e, n=n, kt=kt, km=km, ng=ng: e.tensor_tensor(out=km[0:n, :, :], in0=kt[0:n, 0:256].unsqueeze(1).to_broadcast([n, ng, 256]),
                                                                                               in1=gmask[0:n, 0:ng].unsqueeze(2).to_broadcast([n, ng, 256]), op=ALU.mult),
                                     rd=[Rkt, Rcst], wr=[Rkm])
                            for g in range(ng):
                                csc = bc[:, 1, (g + 1) * gl - 1:(g + 1) * gl]
                                for dc in range(2):
                                    pu, Rpu = psum()
                                    P.op("pe", lambda e, g=g, dc=dc, n=n, c=c, pu=pu, km=km: e.matmul(pu[:, 0:257], lhsT=km[0:n, g, dc * 128:(dc + 1) * 128], rhs=mvt[c][0][0:n, 0:257], start=True, stop=True),
                                         rd=[Rkm, mvt[c][1]], wr=[Rpu])
                                    tc_, Rtc = rtc.get()
                                    P.op("dve", lambda e, g=g, dc=dc, pu=pu, tc_=tc_, C=C: e.tensor_tensor(out=tc_, in0=C[:, g, dc, :], in1=pu[:, 0:257], op=ALU.add), rd=[RCh, Rpu], wr=[Rtc])
                                    P.op("act", lambda e, g=g, dc=dc, tc_=tc_, C=C, csc=csc: e.activation(out=C[:, g, dc, :], in_=tc_, func=AF.Copy, scale=csc), rd=[Rtc, Rbc], wr=[RCh])
                            if c < NCH - 1:
                                refresh_shadow()
                        def _epi(pd=pd, Rpd=Rpd, pns=pns, bc=bc, Rbc=Rbc, n=n, c0=c0):
                            t1, Rt1 = rf.get()
                            P.op("act", lambda e, n=n, pd=pd, t1=t1: e.activation(out=t1[:, 0:n], in_=pd[:, 0:n], func=AF.Abs), rd=[Rpd], wr=[Rt1])
                            P.op("dve", lambda e, n=n, t1=t1, bc=bc: e.tensor_tensor(out=t1[:, 0:n], in0=t1[:, 0:n], in1=bc[:, 2, 0:n], op=ALU.max), rd=[Rt1, Rbc], wr=[Rt1])
                            P.op("dve", lambda e, n=n, t1=t1: e.reciprocal(out=t1[:, 0:n], in_=t1[:, 0:n]), rd=[Rt1], wr=[Rt1])
                            hsb = []
                            for ec in range(2):
                                pn, Rpn = pns[ec]
                                h_, Rh_ = rf.get()
                                P.op("dve", lambda e, n=n, pn=pn, h_=h_, t1=t1: e.tensor_tensor(out=h_[:, 0:n], in0=pn[:, 0:n], in1=t1[:, 0:n], op=ALU.mult), rd=[Rpn, Rt1], wr=[Rh_])
                                hsb.append((h_[:, 0:n], Rh_))
                            rs, Rrs = rmsnorm_rstd(hsb)
                            for ec in range(2):
                                h_, Rh_ = hsb[ec]
                                P.op("dve", lambda e, ec=ec, h_=h_: e.scalar_tensor_tensor(out=h_, in0=h_, scalar=pc("mgn", l, ec, 1), in1=rs, op0=ALU.mult, op1=ALU.mult),
                                     rd=[Rh_, Rprm, Rrs], wr=[Rh_])
                                yb, Ryb = YB[2][2 * h + ec]
                                P.op("dve", lambda e, ec=ec, h_=h_, yb=yb, c0=c0, n=n: e.tensor_tensor(out=yb[:, c0:c0 + n], in0=h_, in1=so[ec][0][:, c0:c0 + n], op=ALU.mult),
                                     rd=[Rh_, so[ec][1]], wr=[Ryb])
                        if ng == 1:
                            pending.append(_epi)
                        else:
                            _epi()
                        if ng != 1:
                            km, Rkm = rkm.get()
                            rc_, Rrc = rcols[c]
                            if ng == 1:
                                P.op("dve", lambda e, h=h, c=c, n=n, km=km, rc_=rc_: e.tensor_scalar(out=km[0:n, 0, :], in0=ktm[c][0][0:n, :], scalar1=rc_[0:n, h:h + 1], scalar2=None, op0=ALU.mult),
                                     rd=[ktm[c][1], Rrc], wr=[Rkm])
                            else:
                                kt, Rkt = rk256.get()
                                P.op("dve", lambda e, h=h, c=c, n=n, kt=kt, rc_=rc_: e.tensor_scalar(out=kt[0:n, 0:256], in0=ktm[c][0][0:n, :], scalar1=rc_[0:n, h:h + 1], scalar2=None, op0=ALU.mult),
                                     rd=[ktm[c][1], Rrc], wr=[Rkt])
                                P.op("dve", lambda e, n=n, kt=kt, km=km, ng=ng: e.tensor_tensor(out=km[0:n, :, :], in0=kt[0:n, 0:256].unsqueeze(1).to_broadcast([n, ng, 256]),
                                                                                               in1=gmask[0:n, 0:ng].unsqueeze(2).to_broadcast([n, ng, 256]), op=ALU.mult),
                                     rd=[Rkt, Rcst], wr=[Rkm])
                            for g in range(ng):
                                csc = bc[:, 1, (g + 1) * gl - 1:(g + 1) * gl]
                                for dc in range(2):
                                    pu, Rpu = psum()
                                    P.op("pe", lambda e, g=g, dc=dc, n=n, c=c, pu=pu, km=km: e.matmul(pu[:, 0:257], lhsT=km[0:n, g, dc * 128:(dc + 1) * 128], rhs=mvt[c][0][0:n, 0:257], start=True, stop=True),
                                         rd=[Rkm, mvt[c][1]], wr=[Rpu])
                                    tc_, Rtc = rtc.get()
                                    P.op("dve", lambda e, g=g, dc=dc, pu=pu, tc_=tc_, C=C: e.tensor_tensor(out=tc_, in0=C[:, g, dc, :], in1=pu[:, 0:257], op=ALU.add), rd=[RCh, Rpu], wr=[Rtc])
                                    P.op("act", lambda e, g=g, dc=dc, tc_=tc_, C=C, csc=csc: e.activation(out=C[:, g, dc, :], in_=tc_, func=AF.Copy, scale=csc), rd=[Rtc, Rbc], wr=[RCh])
                            if c < NCH - 1:
                                refresh_shadow()
                    while pending:
                        pending.pop(0)()
                    if is_sample:
                        for dc in range(2):
                            P.dma("sp", lambda e, h=h, dc=dc, C=C: e.dma_start(out=o_C[1][l, :, h, dc * 128:(dc + 1) * 128, :].rearrange("g p e -> p g e"), in_=C[:, :, dc, 0:256]), rd=[RCh], own=RCh, store=True)
                        P.op("dve", lambda e, h=h, C=C: e.tensor_copy(out=nout[:, l, :, h, :], in_=C[:, :, :, 256]), rd=[RCh], wr=[Rnout])

            def merge_out(l):
                for og in range(8):
                    gates = []
                    for b in range(3):
                        buf, Rw = wload("w_in", l, 0, D, O_MG + b * D + og * 256, 256)
                        for j in range(2):
                            pg, Rpg = psum()
                            proj_fm(pg, Rpg, buf, Rw, j * 128, 128, XN)
                            g_, Rg_ = rB.get()
                            P.op("act", lambda e, pg=pg, g_=g_: e.activation(out=g_, in_=pg[:, 0:N], func=AF.Sigmoid), rd=[Rpg], wr=[Rg_])
                            gates.append((g_, Rg_))
                    accs = [rF.get() for _ in range(2)]
                    for b in range(3):
                        buf, Rw = wload("w_br%d" % b, l, 0, 1024, og * 256, 256)
                        for j in range(2):
                            pp, Rpp = psum()
                            proj_fm(pp, Rpp, buf, Rw, j * 128, 128, YB[b])
                            g_, Rg_ = gates[b * 2 + j]
                            acc, Racc = accs[j]
                            if b == 0:
                                P.op("dve", lambda e, pp=pp, g_=g_, acc=acc: e.tensor_tensor(out=acc, in0=pp[:, 0:N], in1=g_, op=ALU.mult), rd=[Rpp, Rg_], wr=[Racc])
                            else:
                                t1, Rt1 = rF.get()
                                P.op("dve", lambda e, pp=pp, g_=g_, t1=t1: e.tensor_tensor(out=t1, in0=pp[:, 0:N], in1=g_, op=ALU.mult), rd=[Rpp, Rg_], wr=[Rt1])
                                if b == 1:
                                    P.op("dve", lambda e, t1=t1, acc=acc: e.tensor_tensor(out=acc, in0=acc, in1=t1, op=ALU.add), rd=[Rt1, Racc], wr=[Racc])
                                else:
                                    oc = og * 2 + j
                                    P.op("dve", lambda e, t1=t1, acc=acc, oc=oc: e.tensor_tensor(out=MG[oc][0], in0=acc, in1=t1, op=ALU.add), rd=[Rt1, Racc], wr=[MG[oc][1]])
                for og in range(8):
                    buf, Rw = wload("w_out", l, 0, D, og * 256, 256)
                    for j in range(2):
                        oc = og * 2 + j
                        po, Rpo = psum()
                        proj_fm(po, Rpo, buf, Rw, j * 128, 128, MG)
                        resid_add(l, 2, oc, po, Rpo)

            def mlp(l):
                P.alias_after(hg_res, mix_res)
                for hg in range(8):
                    hb = HG[hg % 2]
                    for t in range(4):
                        buf, Rw = wload("w_ff1", l, 0, D, hg * 1024 + t * 256, 256)
                        for j in range(2):
                            hc = t * 2 + j
                            ph, Rph = psum()
                            proj_fm(ph, Rph, buf, Rw, j * 128, 128, XN)
                            r_, Rr_ = rF.get()
                            P.op("act", lambda e, ph=ph, r_=r_: e.activation(out=r_, in_=ph[:, 0:N], func=AF.Relu), rd=[Rph], wr=[Rr_])
                            P.op("dve", lambda e, r_=r_, hc=hc, hb=hb: e.tensor_tensor(out=hb[hc][0], in0=r_, in1=r_, op=ALU.mult), rd=[Rr_], wr=[hb[hc][1]])
                    for og in range(8):
                        buf, Rw = wload("w_ff2", l, hg * 1024, 1024, og * 256, 256)
                        for j in range(2):
                            oc = og * 2 + j
                            po, Rpo = psum()
                            proj_fm(po, Rpo, buf, Rw, j * 128, 128, hb)
                            resid_add(l, 5, oc, po, Rpo)
                P.alias_after(mix_res, hg_res)


            for ti in range(ntiles):
                src = i_xs if is_sample else i_xp[:, :, ti * N:(ti + 1) * N]
                for kc in range(KC):
                    P.dma("sp", lambda e, kc=kc, src=src: e.dma_start(out=X[kc][0], in_=src[:, kc, :]), wr=[X[kc][1]], own=X[kc][1])
                for l in range(NL):
                    norm_mod(l, 0)
                    lru_branch(l)
                    gla_branch(l, None if is_sample else Sst[l][0], None if is_sample else Sst[l][1])
                    mlstm_branch(l, None if is_sample else Cst[l][0], None if is_sample else Cst[l][1])
                    merge_out(l)
                    norm_mod(l, 1)
                    mlp(l)
                rs, Rrs = rmsnorm_rstd(X, rstdN)
                dst = o_ys if is_sample else o_yp[:, :, ti * N:(ti + 1) * N]
                for kc in range(KC):
                    yt, Ryt = rF.get()
                    P.op("dve", lambda e, kc=kc, yt=yt: e.scalar_tensor_tensor(out=yt, in0=X[kc][0], scalar=pc("gf", 0, kc, 1), in1=rs, op0=ALU.mult, op1=ALU.mult),
                         rd=[X[kc][1], Rprm, Rrs], wr=[Ryt])
                    P.dma("sp", lambda e, kc=kc, yt=yt, dst=dst: e.dma_start(out=dst[:, kc, :], in_=yt), rd=[Ryt], own=Ryt, store=True)
            P.dma("sp", lambda e: e.dma_start(out=o_hst[oi], in_=hst), rd=[Rhst], own=Rhst, store=True)
            P.dma("sp", lambda e: e.dma_start(out=o_hist[oi], in_=hist), rd=[Rhist], own=Rhist, store=True)
            P.dma("sp", lambda e: e.dma_start(out=o_m[oi], in_=mout), rd=[Rmout], own=Rmout, store=True)
            if not is_sample:
                for l in range(NL):
                    S, RS = Sst[l]
                    C, RC = Cst[l]
                    P.dma("sp", lambda e, l=l, S=S: e.dma_start(out=o_gla[0][l, 0].rearrange("h p e -> p h e"), in_=S), rd=[RS], own=RS, store=True)
                    for dc in range(2):
                        P.dma("sp", lambda e, l=l, dc=dc, C=C: e.dma_start(out=o_C[0][l, 0, :, dc * 128:(dc + 1) * 128, :].rearrange("h p e -> p h e"), in_=C[:, :, dc, 0:256]), rd=[RC], own=RC, store=True)
                    P.op("dve", lambda e, l=l, C=C: e.tensor_copy(out=nout[:, l, 0], in_=C[:, :, :, 256]), rd=[RC], wr=[Rnout])
            P.dma("sp", lambda e: e.dma_start(out=o_n[oi], in_=nout), rd=[Rnout], own=Rnout, store=True)
            return allres

        prev = setup_res
        if RUN_SAMPLE:
            prev = prev + run_phase(True, prev)
        if N_PROMPT_TILES > 0:
            run_phase(False, prev + [Rmod, RmodA])
        P.emit()
    return nc, WT_LIST


def _fm(a):
    a = np.asarray(a, dtype=np.float32)
    lead = a.shape[:-1]
    a = a.reshape(-1, KC, 128)
    a = np.transpose(a, (2, 1, 0))
    return np.ascontiguousarray(a.reshape(128, KC, *lead))


def _blockdiag(w):
    out = np.zeros((NL, 8, 128, 128), np.float32)
    for b in range(32):
        out[:, :, 4 * b:4 * b + 4, 4 * b:4 * b + 4] = w.reshape(NL, 8, 32, 4, 4)[:, :, b]
    return out


def _consts():
    c = np.zeros((128, CONST_COLS), np.float32)
    o = 0
    j = np.arange(128)
    c[:, o:o + 128] = (j[None, :] >= j[:, None]); o += 128
    c[:, o:o + 128] = 1.0; o += 128
    j64 = np.arange(64)
    m = (j64[None, :] >= j64[:, None]) & ((j64[None, :] // TS) == (j64[:, None] // TS))
    c[0:64, o:o + 64] = m; o += 64
    c[0:64, o:o + 16] = (j64[:, None] // TS) == np.arange(16)[None, :]; o += 16
    for h in range(4):
        c[32 * h, o:o + 128] = 1.0; o += 128
    for h in range(4):
        c[32 * h, o + h] = 1.0
    o += 4
    c[:, o:o + 128] = np.eye(128, dtype=np.float32); o += 128
    return c


_NC_CACHE = {}
_HOOK = {}


def kernel(x_prompt, x_sample, c_prompt, c_sample, state_lru_h, state_lru_conv, state_gla,
           state_mlstm_C, state_mlstm_n, state_mlstm_m, state_mlstm_conv,
           w_ada, b_ada, g_norm1, g_norm2, w_in, lru_conv_w, lru_conv_b, lru_w_a, lru_b_a,
           lru_w_x, lru_b_x, lru_lam, gla_w_g2, gla_b_g, gla_g_norm, ml_conv_w, ml_conv_b,
           ml_w_q, ml_w_k, ml_w_v, ml_b_if, ml_g_norm, w_br_lru, w_br_gla, w_br_ml, w_out,
           w_ff1, w_ff2, g_final):
    f = lambda a: np.ascontiguousarray(np.asarray(a, dtype=np.float32))
    poff, NPRM = prm_layout()
    prm = np.zeros((128, NPRM), np.float32)

    def put(name, l, arr2d):
        o = poff[(name, l)]
        prm[:, o:o + arr2d.shape[1]] = arr2d

    def cols(v, n):
        return np.asarray(v, np.float32).reshape(n, 128).T

    for l in range(NL):
        put("g1", l, cols(g_norm1[l], 16))
        put("g2", l, cols(g_norm2[l], 16))
        put("bada", l, cols(b_ada[l], 96))
        put("lcw", l, np.transpose(np.asarray(lru_conv_w[l], np.float32).reshape(4, 8, 128), (2, 1, 0)).reshape(128, 32))
        put("lcb", l, cols(lru_conv_b[l], 8))
        put("lba", l, cols(lru_b_a[l], 8))
        put("lbx", l, cols(lru_b_x[l], 8))
        put("lam", l, cols(lru_lam[l], 8))
        put("gbg", l, cols(gla_b_g[l], 4))
        put("ggn", l, cols(gla_g_norm[l], 2))
        put("mcw", l, np.transpose(np.asarray(ml_conv_w[l], np.float32).reshape(4, 8, 128), (2, 1, 0)).reshape(128, 32))
        put("mcb", l, cols(ml_conv_b[l], 8))
        bif = np.zeros((128, 2), np.float32)
        bif[:, 1] = 50.0
        for h in range(4):
            bif[32 * h, 0] = ml_b_if[l][h]
            bif[32 * h, 1] = ml_b_if[l][4 + h]
        put("mbif", l, bif)
        put("mgn", l, cols(ml_g_norm[l], 2))
    put("gf", 0, cols(g_final, 16))
    cst = _consts()
    w_in = f(w_in)
    wif = np.zeros((NL, D, 256), np.float32)
    for h in range(4):
        wif[:, :, 32 * h] = w_in[:, :, O_MIF + h]
        wif[:, :, 128 + 32 * h] = w_in[:, :, O_MIF + 4 + h]
    if "nc" not in _NC_CACHE:
        _NC_CACHE["nc"] = build_program()
    nc, wt_list = _NC_CACHE["nc"]
    srcs = {"w_ada": f(w_ada), "w_in": w_in, "w_if": wif, "w_br0": f(w_br_lru), "w_br1": f(w_br_gla), "w_br2": f(w_br_ml),
            "w_out": f(w_out), "w_ff1": f(w_ff1), "w_ff2": f(w_ff2)}
    w_all = np.zeros((NT_MAX, 128, 4096), np.float32)
    for idx, (name, l, r0, nr, c0, ncn) in enumerate(wt_list):
        kcn = nr // 128
        blk = srcs[name][l][r0:r0 + nr, c0:c0 + ncn]
        w_all[idx, :, :kcn * ncn] = blk.reshape(kcn, 128, ncn).transpose(1, 0, 2).reshape(128, kcn * ncn)
    del srcs
    shared = dict(prm=prm, cst=cst, w_all=w_all, lru_w_a=f(lru_w_a), lru_w_x=f(lru_w_x),
                  gla_w_g2=f(gla_w_g2), ml_wq=_blockdiag(f(ml_w_q)), ml_wk=_blockdiag(f(ml_w_k)), ml_wv=_blockdiag(f(ml_w_v)))
    x_prompt = f(x_prompt); x_sample = f(x_sample)
    in_maps = []
    xp_fm = [_fm(x_prompt[s]) for s in range(4)]
    for c in range(NCORE):
        s = c % 4
        sl = slice(GS * c, GS * (c + 1))
        m = dict(shared)
        m["xp"] = xp_fm[s]
        m["xs"] = _fm(x_sample[sl]).reshape(128, KC, GS * TS)
        ct = np.concatenate([np.asarray(c_prompt, np.float32)[s:s + 1], np.asarray(c_sample, np.float32)[sl]], 0)
        m["ct"] = _fm(ct)
        m["s_hst"] = np.ascontiguousarray(np.transpose(f(state_lru_h)[:, sl].reshape(NL, GS, 8, 128), (3, 0, 2, 1)))
        hl = np.transpose(f(state_lru_conv)[:, sl].reshape(NL, GS, 3, 8, 128), (4, 0, 3, 1, 2))
        hm = np.transpose(f(state_mlstm_conv)[:, sl].reshape(NL, GS, 3, 8, 128), (4, 0, 3, 1, 2))
        m["s_hist"] = np.ascontiguousarray(np.stack([hl, hm], axis=2))
        mm = np.zeros((128, NL, GS), np.float32)
        sm = f(state_mlstm_m)[:, sl]
        for h in range(4):
            mm[32 * h] = sm[:, :, h]
        m["s_m"] = mm
        m["s_n"] = np.ascontiguousarray(np.transpose(f(state_mlstm_n)[:, sl].reshape(NL, GS, 4, 2, 128), (4, 0, 1, 2, 3)))
        m["s_gla"] = np.ascontiguousarray(f(state_gla)[:, sl])
        m["s_C"] = np.ascontiguousarray(f(state_mlstm_C)[:, sl])
        in_maps.append(m)
    if _HOOK.get("in_maps_only"):
        return in_maps
    res = run_bass_kernel_spmd(nc, in_maps, core_ids=list(range(NCORE)))
    R = res.results

    def unfm(a):
        return np.ascontiguousarray(np.transpose(a, (2, 1, 0)).reshape(a.shape[2], D))

    y_prompt = np.stack([unfm(R[s]["o_yp"]) for s in range(4)], 0)
    y_sample = np.concatenate([unfm(R[c]["o_ys"]).reshape(GS, TS, D) for c in range(NCORE)], 0)

    def gather(key_idx, cores, fn):
        return np.concatenate([fn(R[c], key_idx) for c in cores], axis=1)

    def st_lru_h(r, i):
        return np.transpose(r["o_hst%d" % i], (1, 3, 2, 0)).reshape(NL, -1, 1024)

    def st_conv(cv):
        def fn(r, i):
            a = r["o_hist%d" % i][:, :, cv]
            return np.transpose(a, (1, 3, 4, 2, 0)).reshape(NL, a.shape[3], 3, 1024)
        return fn

    def st_m(r, i):
        return np.stack([r["o_m%d" % i][32 * h] for h in range(4)], axis=-1)

    def st_n(r, i):
        return np.transpose(r["o_n%d" % i], (1, 2, 3, 4, 0)).reshape(NL, -1, 4, 256)

    outs = [y_prompt, y_sample]
    for i, cores in ((0, range(4)), (1, range(NCORE))):
        outs += [gather(i, cores, st_lru_h), gather(i, cores, st_conv(0)),
                 gather(i, cores, lambda r, i: r["o_gla%d" % i]), gather(i, cores, lambda r, i: r["o_C%d" % i]),
                 gather(i, cores, st_n), gather(i, cores, st_m), gather(i, cores, st_conv(1))]
    return tuple(np.ascontiguousarray(o.astype(np.float32)) for o in outs)
```

```python
import types
import numpy as np
from contextlib import ExitStack
import concourse.bass as bass
import concourse.mybir as mybir
from concourse.bass_utils import run_bass_kernel_spmd

F32 = mybir.dt.float32
BF16 = mybir.dt.bfloat16
AF = mybir.ActivationFunctionType
ALU = mybir.AluOpType

D = 2048
KC = 16
DIN = 12312
NL = 2
EPS = 1e-6
NCORE = 8
GS = 16
TS = 4
O_LRUX, O_GQ, O_GK, O_GV, O_GLR, O_GG, O_MX, O_MO, O_MIF, O_MG = 0, 1024, 1536, 2048, 3072, 3088, 4112, 5136, 6160, 6168

N_PROMPT_TILES = 4
RUN_SAMPLE = True
NT_MAX = 460


def prm_layout():
    off = {}
    n = 0
    for l in range(NL):
        for name, w in (("g1", 16), ("g2", 16), ("bada", 96), ("lcw", 32), ("lcb", 8), ("lba", 8), ("lbx", 8),
                        ("lam", 8), ("gbg", 4), ("ggn", 2), ("mcw", 32), ("mcb", 8), ("mbif", 2), ("mgn", 2)):
            off[(name, l)] = n
            n += w
    off[("gf", 0)] = n
    n += 16
    return off, n


CONST_COLS = 128 + 128 + 64 + 16 + 4 * 128 + 4 + 128


def _freeze(fn):
    if fn.__closure__ is None:
        return fn
    cells = []
    for c in fn.__closure__:
        try:
            cells.append(types.CellType(c.cell_contents))
        except ValueError:
            cells.append(c)
    return types.FunctionType(fn.__code__, fn.__globals__, fn.__name__, fn.__defaults__, tuple(cells))


class Res:
    __slots__ = ("name", "w", "r", "sem", "cnt")

    def __init__(self, name):
        self.name = name
        self.w = None
        self.r = {}
        self.sem = None
        self.cnt = 0


class Prog:
    ENG = ("pe", "act", "dve", "pool", "sp")

    def __init__(self, nc, stack):
        self.nc = nc
        self.stack = stack
        self.ins = {e: [] for e in self.ENG}
        self.sems = {e: stack.enter_context(nc.semaphore("s_" + e)) for e in self.ENG}
        self.store_toks = []
        self.nsem = 0

    def _deps(self, rd, wr):
        deps = set()
        for r in rd:
            if r.w is not None:
                deps.add(r.w)
        for w in wr:
            if w.w is not None:
                deps.add(w.w)
            deps.update(w.r.values())
        return deps

    def op(self, eng, fn, rd=(), wr=()):
        i = len(self.ins[eng])
        tok = ("c", eng, i)
        deps = self._deps(rd, wr)
        if eng == "pe":
            deps = {d for d in deps if not (d[0] == "c" and d[1] == "pe")}
        self.ins[eng].append((_freeze(fn), deps, None))
        for r in rd:
            r.r[eng] = tok
        for w in wr:
            w.w = tok
            w.r = {}
        return tok

    def dma(self, q, fn, rd=(), wr=(), own=None, store=False):
        if own.sem is None:
            own.sem = self.stack.enter_context(self.nc.semaphore("d%d" % self.nsem))
            self.nsem += 1
        deps = self._deps(rd, wr)
        own.cnt += 1
        tok = ("d", own, own.cnt * 16)
        self.ins[q].append((_freeze(fn), deps, own))
        for r in rd:
            r.r["dma" + own.name] = tok
        for w in wr:
            w.w = tok
            w.r = {}
        if store:
            self.store_toks.append(tok)
        return tok

    def alias_after(self, new, old):
        acc = {}
        for o in old:
            toks = list(o.r.items())
            if o.w is not None:
                toks.append(("w" + o.name, o.w))
            for k, t in toks:
                if t[0] == "c":
                    key = t[1]
                    if key not in acc or acc[key][2] < t[2]:
                        acc[key] = t
                else:
                    key = "d" + t[1].name
                    if key not in acc or acc[key][2] < t[2]:
                        acc[key] = t
        for n_ in new:
            n_.w = None
            n_.r = dict(acc)

    def emit(self):
        nc = self.nc
        needed = {e: set() for e in self.ENG}
        for e in self.ENG:
            for (_, deps, _) in self.ins[e]:
                for d in deps:
                    if d[0] == "c":
                        needed[d[1]].add(d[2])
        mval = {}
        for e in self.ENG:
            for rank, i in enumerate(sorted(needed[e])):
                mval[(e, i)] = rank + 1
        final = list(self.store_toks)

        def run(e, eng):
            seen = {}

            def wait(sem, val):
                k = id(sem)
                if seen.get(k, 0) >= val:
                    return
                seen[k] = val
                eng.wait_ge(sem, val)

            def wait_all(deps):
                best = {}
                for d in deps:
                    sem, val = (self.sems[d[1]], mval[(d[1], d[2])]) if d[0] == "c" else (d[1].sem, d[2])
                    k = id(sem)
                    if k not in best or best[k][1] < val:
                        best[k] = (sem, val)
                for sem, val in best.values():
                    wait(sem, val)

            for i, (fn, deps, own) in enumerate(self.ins[e]):
                wait_all(deps)
                inst = fn(eng)
                if own is not None:
                    inst.then_inc(own.sem, 16)
                elif (e, i) in mval:
                    inst.then_inc(self.sems[e], 1)
            if e == "sp":
                wait_all(final)

        with nc.Block() as block:
            @block.tensor
            def _(eng):
                run("pe", eng)

            @block.scalar
            def _(eng):
                run("act", eng)

            @block.vector
            def _(eng):
                run("dve", eng)

            @block.gpsimd
            def _(eng):
                run("pool", eng)

            @block.sync
            def _(eng):
                run("sp", eng)


class Ring:
    def __init__(self, views):
        self.v = views
        self.i = 0

    def get(self):
        v = self.v[self.i % len(self.v)]
        self.i += 1
        return v


def build_program():
    nc = bass.Bass("TRN2", target_bir_lowering=False)
    poff, NPRM = prm_layout()

    def din(name, shape):
        return nc.dram_tensor(name, list(shape), F32, kind="ExternalInput").ap()

    def dout(name, shape):
        return nc.dram_tensor(name, list(shape), F32, kind="ExternalOutput").ap()

    NPT = 2048
    i_xp = din("xp", [128, KC, NPT])
    i_xs = din("xs", [128, KC, GS * TS])
    i_ct = din("ct", [128, KC, 1 + GS])
    i_prm = din("prm", [128, NPRM])
    i_cst = din("cst", [128, CONST_COLS])
    i_hst = din("s_hst", [128, NL, 8, GS])
    i_hist = din("s_hist", [128, NL, 2, 8, GS, 3])
    i_mc = din("s_m", [128, NL, GS])
    i_sn = din("s_n", [128, NL, GS, 4, 2])
    i_sgla = din("s_gla", [NL, GS, 4, 128, 256])
    i_sC = din("s_C", [NL, GS, 4, 256, 256])
    w_all = din("w_all", [NT_MAX, 128, 4096])
    w_la = din("lru_w_a", [NL, 8, 128, 128])
    w_lx = din("lru_w_x", [NL, 8, 128, 128])
    w_g2 = din("gla_w_g2", [NL, 16, 512])
    w_mq = din("ml_wq", [NL, 8, 128, 128])
    w_mk = din("ml_wk", [NL, 8, 128, 128])
    w_mv = din("ml_wv", [NL, 8, 128, 128])

    o_yp = dout("o_yp", [128, KC, NPT])
    o_ys = dout("o_ys", [128, KC, GS * TS])
    o_hst = [dout("o_hst%d" % i, [128, NL, 8, g]) for i, g in enumerate((1, GS))]
    o_hist = [dout("o_hist%d" % i, [128, NL, 2, 8, g, 3]) for i, g in enumerate((1, GS))]
    o_m = [dout("o_m%d" % i, [128, NL, g]) for i, g in enumerate((1, GS))]
    o_n = [dout("o_n%d" % i, [128, NL, g, 4, 2]) for i, g in enumerate((1, GS))]
    o_gla = [dout("o_gla%d" % i, [NL, g, 4, 128, 256]) for i, g in enumerate((1, GS))]
    o_C = [dout("o_C%d" % i, [NL, g, 4, 256, 256]) for i, g in enumerate((1, GS))]

    with ExitStack() as st:
        P = Prog(nc, st)
        ARENA_W = 52480
        arena = st.enter_context(nc.sbuf_tensor("arena", [128, ARENA_W], F32))
        ps_t = [st.enter_context(nc.psum_tensor("ps%d" % i, [128, 512], F32)) for i in range(8)]
        ps_res = [Res("ps%d" % i) for i in range(8)]
        ps_i = [0]

        def psum():
            i = ps_i[0] % 8
            ps_i[0] += 1
            return ps_t[i], ps_res[i]

        top = [0]
        uid = [0]

        def carve(words):
            a = top[0]
            top[0] += (words + 7) // 8 * 8
            assert top[0] <= ARENA_W, ("SBUF arena overflow", top[0])
            return a

        def view(shape, dt, name=None):
            n = int(np.prod(shape))
            words = n if dt == F32 else (n + 1) // 2
            a = carve(words)
            ap = arena[:, a:a + words]
            if dt != F32:
                ap = ap.bitcast(dt)[:, 0:n]
            if len(shape) >= 2:
                names = ["d%d" % i for i in range(len(shape))]
                pat = "p (" + " ".join(names) + ") -> p " + " ".join(names)
                ap = ap.rearrange(pat, **{names[i]: int(shape[i]) for i in range(1, len(shape))})
            uid[0] += 1
            return ap, Res(name or "v%d" % uid[0])

        def ring(cnt, shape, dt, name):
            return Ring([view(shape, dt, "%s%d" % (name, i)) for i in range(cnt)])

        cst, Rcst = view([CONST_COLS], F32, "cst")
        prm, Rprm = view([NPRM], F32, "prm")
        drv, Rdrv = view([NL, 40], F32, "drv")
        modT0, Rmod0 = view([NL, 96], F32, "modT0")
        modA0, RmodA0 = view([NL, 2, 16], F32, "modA0")
        c0_ = 0
        mask128 = cst[:, c0_:c0_ + 128]; c0_ += 128
        ones_f = cst[:, c0_:c0_ + 128]; c0_ += 128
        mask64 = cst[:, c0_:c0_ + 64]; c0_ += 64
        gmask = cst[:, c0_:c0_ + 16]; c0_ += 16
        sel = [cst[:, c0_ + 128 * h:c0_ + 128 * (h + 1)] for h in range(4)]; c0_ += 512
        E4 = cst[:, c0_:c0_ + 4]; c0_ += 4
        ident = cst[:, c0_:c0_ + 128]; c0_ += 128
        ones_b, Rob = view([128], BF16, "ones_b")
        wbuf = [view([KC, 256], BF16, "wbuf%d" % i) for i in range(3)]
        wring = Ring(wbuf)
        sring = ring(6, [128], BF16, "smallw")
        wg2, Rwg2 = view([512], BF16, "wg2")

        def pc(name, l, j=0, n=1):
            o = poff[(name, l)] + j
            return prm[:, o:o + n]

        P.dma("sp", lambda e: e.dma_start(out=cst, in_=i_cst), wr=[Rcst], own=Rcst)
        P.dma("sp", lambda e: e.dma_start(out=prm, in_=i_prm), wr=[Rprm], own=Rprm)
        P.op("dve", lambda e: e.memset(ones_b, 1.0), wr=[Rob])
        P.op("dve", lambda e: e.memset(drv, 0.0), wr=[Rdrv])
        for l in range(NL):
            P.op("act", lambda e, l=l: e.activation(out=drv[:, l, 0:8], in_=pc("lam", l, 0, 8), func=AF.Exp, scale=-1.0), rd=[Rprm, Rdrv], wr=[Rdrv])
            P.op("act", lambda e, l=l: e.activation(out=drv[:, l, 0:8], in_=drv[:, l, 0:8], func=AF.Ln, bias=1.0), rd=[Rdrv], wr=[Rdrv])
            P.op("dve", lambda e, l=l: e.tensor_scalar(out=drv[:, l, 8:16], in0=drv[:, l, 0:8], scalar1=-16.0, scalar2=None, op0=ALU.mult), rd=[Rdrv], wr=[Rdrv])
            P.op("dve", lambda e, l=l: e.tensor_scalar(out=drv[:, l, 0:8], in0=drv[:, l, 0:8], scalar1=-8.0, scalar2=None, op0=ALU.mult), rd=[Rdrv], wr=[Rdrv])
            P.op("dve", lambda e, l=l: e.tensor_scalar(out=drv[:, l, 16:20], in0=pc("gbg", l, 0, 4), scalar1=-1.0, scalar2=None, op0=ALU.mult), rd=[Rprm, Rdrv], wr=[Rdrv])
            P.op("dve", lambda e, l=l: e.tensor_scalar(out=drv[:, l, 20:21], in0=pc("mbif", l, 1, 1), scalar1=-1.0, scalar2=None, op0=ALU.mult), rd=[Rprm, Rdrv], wr=[Rdrv])

        WT = {}
        WT_LIST = []

        def wload(name, l, r0, nr, c0, ncn):
            key = (name, l, r0, nr, c0, ncn)
            if key not in WT:
                WT[key] = len(WT_LIST)
                WT_LIST.append(key)
            idx = WT[key]
            assert idx < NT_MAX
            kcn = nr // 128
            buf, R = wring.get()
            flat = buf.rearrange("p k n -> p (k n)")
            P.dma("pool", lambda e: e.dma_start(out=flat[:, 0:kcn * ncn], in_=w_all[idx][:, 0:kcn * ncn], max_dma_last_dim=8192), wr=[R], own=R)
            return flat[:, 0:kcn * ncn].rearrange("p (k n) -> p k n", n=ncn), R

        def sload(src2):
            buf, R = sring.get()
            P.dma("pool", lambda e: e.dma_start(out=buf, in_=src2), wr=[R], own=R)
            return buf, R

        top_setup = top[0]
        modT, Rmod = view([NL, 96, 1 + GS], F32, "modT")
        modA, RmodA = view([NL, 2, 16, 1 + GS], F32, "modA")
        ctb, Rctb = view([KC, 1 + GS], BF16, "ctb")
        top_mods = top[0]
        ctf, Rctf = view([KC, 1 + GS], F32, "ctf")
        P.dma("sp", lambda e: e.dma_start(out=ctf, in_=i_ct), wr=[Rctf], own=Rctf)
        P.op("act", lambda e: e.activation(out=ctb, in_=ctf, func=AF.Silu), rd=[Rctf], wr=[Rctb])
        NC_ = 1 + GS

        def ada_gen(l):
            for t4 in range(48):
                buf, Rw = wload("w_ada", l, 0, D, t4 * 256, 256)
                for j in range(2):
                    oc = t4 * 2 + j
                    pt, Rp = psum()
                    for kc in range(KC):
                        P.op("pe", lambda e, kc=kc, j=j, pt=pt, buf=buf: e.matmul(pt[:, 0:NC_], lhsT=buf[:, kc, j * 128:(j + 1) * 128], rhs=ctb[:, kc, :],
                                                                                    start=(kc == 0), stop=(kc == KC - 1)), rd=[Rw, Rctb], wr=[Rp])
                    P.op("act", lambda e, oc=oc, l=l, pt=pt: e.activation(out=modT[:, l, oc, :], in_=pt[:, 0:NC_], func=AF.Identity, bias=pc("bada", l, oc, 1), scale=1.0),
                         rd=[Rp, Rprm], wr=[Rmod])
                yield
            for nrm, (gname, sck) in enumerate((("g1", 1), ("g2", 4))):
                P.op("dve", lambda e, l=l, nrm=nrm, sck=sck: e.tensor_scalar(out=modA[:, l, nrm], in0=modT[:, l, sck * 16:(sck + 1) * 16, :], scalar1=1.0, scalar2=None, op0=ALU.add),
                     rd=[Rmod], wr=[RmodA])
                P.op("dve", lambda e, l=l, nrm=nrm, gname=gname: e.tensor_tensor(out=modA[:, l, nrm], in0=modA[:, l, nrm],
                                                                                in1=pc(gname, l, 0, 16).unsqueeze(2).to_broadcast([128, 16, NC_]), op=ALU.mult),
                     rd=[RmodA, Rprm], wr=[RmodA])
            P.op("dve", lambda e, l=l: e.tensor_copy(out=modT0[:, l], in_=modT[:, l, :, 0]), rd=[Rmod], wr=[Rmod0])
            P.op("dve", lambda e, l=l: e.tensor_copy(out=modA0[:, l], in_=modA[:, l, :, :, 0]), rd=[RmodA], wr=[RmodA0])
            yield

        for _ in ada_gen(0):
            pass
        ada1 = ada_gen(1)

        def ada_pump(k):
            n_ = 0
            for _ in ada1:
                n_ += 1
                if k is not None and n_ >= k:
                    break

        if not RUN_SAMPLE:
            ada_pump(None)
        setup_res = [Rctf]

        def run_phase(is_sample, prev_res):
            top[0] = top_mods if is_sample else top_setup
            allres = []
            if is_sample:
                N, G, T = GS * TS, GS, TS
                chunks = [(0, GS, TS)]
                ntiles = 1
                mcol0 = 1
                oi = 1
            else:
                N, G, T = 512, 1, 512
                chunks = [(128 * c, 1, 128) for c in range(4)]
                ntiles = N_PROMPT_TILES
                mcol0 = 0
                oi = 0
            NCH = len(chunks)
            CN = chunks[0][1] * chunks[0][2]

            def V(shape, dt, name):
                ap, R = view(shape, dt, name)
                allres.append(R)
                return ap, R

            def RING(cnt, shape, dt, name):
                r = ring(cnt, shape, dt, name)
                allres.extend(R for _, R in r.v)
                return r

            X = [V([N], F32, "x%d" % k) for k in range(KC)]
            XN = [V([N], BF16, "xn%d" % k) for k in range(KC)]
            ov0 = top[0]
            YB = [[V([N], BF16, "yb%d_%d" % (b, k)) for k in range(8)] for b in range(3)]
            MG = [V([N], BF16, "mg%d" % k) for k in range(KC)]
            ov1 = top[0]
            top[0] = ov0
            HG = [[V([N], BF16, "hg%d_%d" % (i, k)) for k in range(8)] for i in range(2)]
            top[0] = max(ov1, top[0])
            mix_res = [R for b in YB for _, R in b] + [R for _, R in MG]
            hg_res = [R for hb in HG for _, R in hb]
            hst, Rhst = V([NL, 8, G], F32, "hst")
            hist, Rhist = V([NL, 2, 8, G, 3], F32, "hist")
            Fc, RFc = V([NL, G], F32, "Fc")
            Mc, RMc = V([NL, G], F32, "Mc")
            mout, Rmout = V([NL, G], F32, "mout")
            nout, Rnout = V([NL, G, 4, 2], F32, "nout")
            if is_sample:
                Sst = [V([G, 256], F32, "Sst")]
                Cst = [V([G, 2, 257], F32, "Cst")]
                nin, Rnin = V([NL, G, 4, 2], F32, "nin")
            else:
                Sst = [V([4, 256], F32, "S%d" % l) for l in range(NL)]
                Cst = [V([4, 2, 257], F32, "C%d" % l) for l in range(NL)]
            Sbf, RSbf = V([G, 256], BF16, "Sbf")
            Cbf, RCbf = V([G, 2, 257], BF16, "Cbf")
            nbc, Rnbc = V([G, 2, 128], BF16, "nbc")
            rF = RING(6, [N], F32, "rF")
            rB = RING(12, [N], BF16, "rB")
            rf = RING(8, [CN], F32, "rf")
            rb = RING(4, [128], BF16, "rb")
            rcv = RING(2, [G, T + 3], F32, "rcv")
            vtm = [V([257], BF16, "vtm%d" % c) for c in range(NCH)]
            ktm = [V([256], BF16, "ktm%d" % c) for c in range(NCH)]
            mvt = [V([257], BF16, "mvt%d" % c) for c in range(NCH)]
            rbc = RING(2, [3, CN], F32, "rbc")
            rkm = RING(2, [chunks[0][1], 256], BF16, "rkm")
            rtc = RING(2, [257], F32, "rtc")
            rk256 = RING(2, [256], BF16, "rk256")
            rcol_r = RING(2, [4], F32, "rcol")
            rRs = [V([3, CN], F32, "Rrow%d" % c) for c in range(NCH)]
            rcolv = [V([4], F32, "rcolv%d" % c) for c in range(NCH)]
            rstdN = V([N], F32, "rstdN")
            glrv = V([N], BF16, "glr")
            P.alias_after(allres, prev_res)

            if is_sample:
                P.dma("sp", lambda e: e.dma_start(out=hst, in_=i_hst), wr=[Rhst], own=Rhst)
                P.dma("sp", lambda e: e.dma_start(out=hist, in_=i_hist), wr=[Rhist], own=Rhist)
                P.dma("sp", lambda e: e.dma_start(out=Mc, in_=i_mc), wr=[RMc], own=RMc)
                P.dma("sp", lambda e: e.dma_start(out=nin, in_=i_sn), wr=[Rnin], own=Rnin)
            else:
                P.op("dve", lambda e: e.memset(hst, 0.0), wr=[Rhst])
                P.op("dve", lambda e: e.memset(hist, 0.0), wr=[Rhist])
                P.op("dve", lambda e: e.memset(Mc, 0.0), wr=[RMc])
                for l in range(NL):
                    P.op("dve", lambda e, l=l: e.memset(Sst[l][0], 0.0), wr=[Sst[l][1]])
                    P.op("dve", lambda e, l=l: e.memset(Cst[l][0], 0.0), wr=[Cst[l][1]])
            P.op("dve", lambda e: e.memset(Fc, 0.0), wr=[RFc])
            for c in range(NCH):
                P.op("dve", lambda e, c=c: e.memset(mvt[c][0][:, 256:257], 1.0), wr=[mvt[c][1]])

            def gt3(ap2):
                return ap2.rearrange("p (g t) -> p g t", t=T)

            def rmsnorm_rstd(srcs, dst=None):
                n = srcs[0][0].shape[-1]
                pt, Rp = psum()
                for i, (ap, R) in enumerate(srcs):
                    sq, Rsq = (rF.get() if n == N else rf.get())
                    P.op("act", lambda e, ap=ap, sq=sq: e.activation(out=sq[:, 0:n], in_=ap, func=AF.Square), rd=[R], wr=[Rsq])
                    P.op("pe", lambda e, sq=sq, i=i, pt=pt: e.matmul(pt[:, 0:n], lhsT=ones_f, rhs=sq[:, 0:n], start=(i == 0), stop=(i == len(srcs) - 1)),
                         rd=[Rsq, Rcst], wr=[Rp])
                rs, Rrs = dst if dst is not None else (rF.get() if n == N else rf.get())
                dim = 128.0 * len(srcs)
                P.op("act", lambda e, pt=pt, rs=rs: e.activation(out=rs[:, 0:n], in_=pt[:, 0:n], func=AF.Sqrt, scale=1.0 / dim, bias=EPS), rd=[Rp], wr=[Rrs])
                P.op("dve", lambda e, rs=rs: e.reciprocal(out=rs[:, 0:n], in_=rs[:, 0:n]), rd=[Rrs], wr=[Rrs])
                return rs[:, 0:n], Rrs

            def norm_mod(l, nrm):
                rs, Rrs = rmsnorm_rstd(X, rstdN)
                shk = 0 if nrm == 0 else 3
                for kc in range(KC):
                    t1, Rt1 = rF.get()
                    P.op("dve", lambda e, kc=kc, t1=t1: e.tensor_tensor(out=t1, in0=X[kc][0], in1=rs, op=ALU.mult), rd=[X[kc][1], Rrs], wr=[Rt1])
                    if G == 1:
                        P.op("act", lambda e, kc=kc, t1=t1: e.activation(out=XN[kc][0], in_=t1, func=AF.Identity, scale=modA0[:, l, nrm, kc:kc + 1],
                                                                          bias=modT0[:, l, shk * 16 + kc:shk * 16 + kc + 1]),
                             rd=[Rt1, RmodA0, Rmod0], wr=[XN[kc][1]])
                    else:
                        P.op("dve", lambda e, kc=kc, t1=t1: e.tensor_tensor(out=gt3(t1), in0=gt3(t1),
                                                                             in1=modA[:, l, nrm, kc, mcol0:mcol0 + G].unsqueeze(2).to_broadcast([128, G, T]), op=ALU.mult),
                             rd=[Rt1, RmodA], wr=[Rt1])
                        P.op("dve", lambda e, kc=kc, t1=t1: e.tensor_tensor(out=gt3(XN[kc][0]), in0=gt3(t1),
                                                                             in1=modT[:, l, shk * 16 + kc, mcol0:mcol0 + G].unsqueeze(2).to_broadcast([128, G, T]), op=ALU.add),
                             rd=[Rt1, Rmod], wr=[XN[kc][1]])

            def proj_fm(pt, Rp, buf, Rw, wc0, ncol, acts, n=None):
                n = n or N
                K = len(acts)
                for kc in range(K):
                    P.op("pe", lambda e, kc=kc: e.matmul(pt[0:ncol, 0:n], lhsT=buf[:, kc, wc0:wc0 + ncol], rhs=acts[kc][0], start=(kc == 0), stop=(kc == K - 1)),
                         rd=[Rw, acts[kc][1]], wr=[Rp])

            def proj_tm(pt, Rp, buf, Rw, wc0, ncol, acts, c0, n, pcol0=0):
                K = len(acts)
                for kc in range(K):
                    P.op("pe", lambda e, kc=kc: e.matmul(pt[0:n, pcol0:pcol0 + ncol], lhsT=acts[kc][0][:, c0:c0 + n], rhs=buf[:, kc, wc0:wc0 + ncol],
                                                         start=(kc == 0), stop=(kc == K - 1)), rd=[Rw, acts[kc][1]], wr=[Rp])

            def resid_add(l, kind, oc, pt, Rp):
                col = kind * 16 + oc
                if G == 1:
                    P.op("dve", lambda e: e.scalar_tensor_tensor(out=X[oc][0], in0=pt[:, 0:N], scalar=modT0[:, l, col:col + 1], in1=X[oc][0], op0=ALU.mult, op1=ALU.add),
                         rd=[Rp, Rmod0, X[oc][1]], wr=[X[oc][1]])
                else:
                    t1, Rt1 = rF.get()
                    P.op("dve", lambda e: e.tensor_tensor(out=gt3(t1), in0=gt3(pt[:, 0:N]), in1=modT[:, l, col, mcol0:mcol0 + G].unsqueeze(2).to_broadcast([128, G, T]), op=ALU.mult),
                         rd=[Rp, Rmod], wr=[Rt1])
                    P.op("dve", lambda e: e.tensor_tensor(out=X[oc][0], in0=X[oc][0], in1=t1, op=ALU.add), rd=[Rt1, X[oc][1]], wr=[X[oc][1]])

            def conv(l, cname, bname, ch, cb, Rcb, out, Rout):
                o3 = gt3(out)
                P.op("dve", lambda e: e.tensor_scalar(out=o3, in0=cb[:, :, 3:3 + T], scalar1=pc(cname, l, ch * 4 + 3, 1), scalar2=pc(bname, l, ch, 1), op0=ALU.mult, op1=ALU.add),
                     rd=[Rcb, Rprm], wr=[Rout])
                for j in (2, 1, 0):
                    P.op("dve", lambda e, j=j: e.scalar_tensor_tensor(out=o3, in0=cb[:, :, j:j + T], scalar=pc(cname, l, ch * 4 + j, 1), in1=o3, op0=ALU.mult, op1=ALU.add),
                         rd=[Rcb, Rprm, Rout], wr=[Rout])

            def conv_in(l, cv, ch, pt, Rp):
                cb, Rcb = rcv.get()
                P.op("act", lambda e: e.activation(out=cb[:, :, 0:3], in_=hist[:, l, cv, ch], func=AF.Copy), rd=[Rhist], wr=[Rcb])
                P.op("act", lambda e: e.activation(out=cb[:, :, 3:3 + T], in_=gt3(pt[:, 0:N]), func=AF.Copy), rd=[Rp], wr=[Rcb])
                P.op("act", lambda e: e.activation(out=hist[:, l, cv, ch], in_=cb[:, :, T:T + 3], func=AF.Copy), rd=[Rcb], wr=[Rhist])
                return cb, Rcb

            def lru_branch(l):
                for ch in range(8):
                    if ch % 2 == 0:
                        buf, Rw = wload("w_in", l, 0, D, O_LRUX + ch * 128, 256)
                    wa, Rwa = sload(w_la[l, ch])
                    wx, Rwx = sload(w_lx[l, ch])
                    pt, Rp = psum()
                    proj_fm(pt, Rp, buf, Rw, (ch % 2) * 128, 128, XN)
                    cb, Rcb = conv_in(l, 0, ch, pt, Rp)
                    u, Ru = rF.get()
                    conv(l, "lcw", "lcb", ch, cb, Rcb, u, Ru)
                    ub, Rub = rB.get()
                    P.op("dve", lambda e: e.tensor_copy(out=ub, in_=u), rd=[Ru], wr=[Rub])
                    pa, Rpa = psum()
                    P.op("pe", lambda e: e.matmul(pa[:, 0:N], lhsT=wa, rhs=ub, start=True, stop=True), rd=[Rwa, Rub], wr=[Rpa])
                    px, Rpx = psum()
                    P.op("pe", lambda e: e.matmul(px[:, 0:N], lhsT=wx, rhs=ub, start=True, stop=True), rd=[Rwx, Rub], wr=[Rpx])
                    r, Rr = rF.get()
                    P.op("act", lambda e: e.activation(out=r, in_=pa[:, 0:N], func=AF.Sigmoid, bias=pc("lba", l, ch, 1), scale=1.0), rd=[Rpa, Rprm], wr=[Rr])
                    ig, Rig = rF.get()
                    P.op("act", lambda e: e.activation(out=ig, in_=px[:, 0:N], func=AF.Sigmoid, bias=pc("lbx", l, ch, 1), scale=1.0), rd=[Rpx, Rprm], wr=[Rig])
                    a, Ra = rF.get()
                    P.op("act", lambda e: e.activation(out=a, in_=r, func=AF.Exp, scale=drv[:, l, ch:ch + 1]), rd=[Rr, Rdrv], wr=[Ra])
                    P.op("act", lambda e: e.activation(out=r, in_=r, func=AF.Exp, scale=drv[:, l, 8 + ch:9 + ch]), rd=[Rr, Rdrv], wr=[Rr])
                    P.op("dve", lambda e: e.tensor_scalar(out=r, in0=r, scalar1=1.0, scalar2=-1.0, op0=ALU.min, op1=ALU.mult), rd=[Rr], wr=[Rr])
                    P.op("act", lambda e: e.activation(out=r, in_=r, func=AF.Sqrt, bias=1.0, scale=1.0), rd=[Rr], wr=[Rr])
                    P.op("dve", lambda e: e.tensor_tensor(out=ig, in0=ig, in1=u, op=ALU.mult), rd=[Rig, Ru], wr=[Rig])
                    P.op("dve", lambda e: e.tensor_tensor(out=ig, in0=ig, in1=r, op=ALU.mult), rd=[Rig, Rr], wr=[Rig])
                    if G == 1:
                        P.op("dve", lambda e: e.tensor_tensor_scan(out=u, data0=a, data1=ig, initial=hst[:, l, ch, 0:1], op0=ALU.mult, op1=ALU.add),
                             rd=[Ra, Rig, Rhst], wr=[Ru])
                    else:
                        a3, g3, u3 = gt3(a), gt3(ig), gt3(u)
                        for t in range(T):
                            prev = hst[:, l, ch, :] if t == 0 else u3[:, :, t - 1]
                            P.op("dve", lambda e, t=t, prev=prev: e.tensor_tensor(out=u3[:, :, t], in0=a3[:, :, t], in1=prev, op=ALU.mult), rd=[Ra, Ru, Rhst], wr=[Ru])
                            P.op("dve", lambda e, t=t: e.tensor_tensor(out=u3[:, :, t], in0=u3[:, :, t], in1=g3[:, :, t], op=ALU.add), rd=[Ru, Rig], wr=[Ru])
                    P.op("act", lambda e: e.activation(out=hst[:, l, ch, :], in_=gt3(u)[:, :, T - 1], func=AF.Copy), rd=[Ru], wr=[Rhst])
                    P.op("dve", lambda e: e.tensor_copy(out=YB[0][ch][0], in_=u), rd=[Ru], wr=[YB[0][ch][1]])

            def gla_branch(l, Sfull, RS):
                buf, Rw = wload("w_in", l, 0, D, O_GLR, 16)
                pt, Rp = psum()
                proj_fm(pt, Rp, buf, Rw, 0, 16, XN)
                glr, Rglr = glrv
                P.op("act", lambda e: e.activation(out=glr[0:16, :], in_=pt[0:16, 0:N], func=AF.Copy), rd=[Rp], wr=[Rglr])
                P.dma("pool", lambda e: e.dma_start(out=wg2[0:16, :], in_=w_g2[l]), wr=[Rwg2], own=Rwg2)
                for h in range(4):
                    if is_sample:
                        S, RSh = Sst[0]
                        P.dma("sp", lambda e, h=h: e.dma_start(out=S, in_=i_sgla[l, :, h].rearrange("g p e -> p g e")), wr=[RSh], own=RSh)
                    else:
                        S, RSh = Sfull[:, h:h + 1, :], RS
                    P.op("act", lambda e, S=S: e.activation(out=Sbf, in_=S, func=AF.Copy), rd=[RSh], wr=[RSbf])
                    pz, Rpz = psum()
                    P.op("pe", lambda e, h=h: e.matmul(pz[:, 0:N], lhsT=wg2[0:16, h * 128:(h + 1) * 128], rhs=glr[0:16, :], start=True, stop=True), rd=[Rwg2, Rglr], wr=[Rpz])
                    sp_, Rsp = rF.get()
                    P.op("act", lambda e, h=h: e.activation(out=sp_, in_=pz[:, 0:N], func=AF.Exp, scale=-1.0, bias=drv[:, l, 16 + h:17 + h]), rd=[Rpz, Rdrv], wr=[Rsp])
                    P.op("act", lambda e: e.activation(out=sp_, in_=sp_, func=AF.Ln, bias=1.0), rd=[Rsp], wr=[Rsp])
                    cs, Rcs = rF.get()
                    if G == 1:
                        for (c0, ng, gl) in chunks:
                            P.op("dve", lambda e, c0=c0, gl=gl: e.tensor_tensor_scan(out=cs[:, c0:c0 + gl], data0=ones_f[:, 0:gl], data1=sp_[:, c0:c0 + gl], initial=0.0,
                                                                                     op0=ALU.mult, op1=ALU.add), rd=[Rsp, Rcst], wr=[Rcs])
                    else:
                        s3, c3 = gt3(sp_), gt3(cs)
                        P.op("dve", lambda e: e.tensor_copy(out=c3[:, :, 0], in_=s3[:, :, 0]), rd=[Rsp], wr=[Rcs])
                        for t in range(1, T):
                            P.op("dve", lambda e, t=t: e.tensor_tensor(out=c3[:, :, t], in0=c3[:, :, t - 1], in1=s3[:, :, t], op=ALU.add), rd=[Rsp, Rcs], wr=[Rcs])
                    eb, Reb = rF.get()
                    P.op("act", lambda e: e.activation(out=eb, in_=cs, func=AF.Exp, scale=-1.0 / 16.0), rd=[Rcs], wr=[Reb])
                    P.op("act", lambda e: e.activation(out=cs, in_=cs, func=AF.Exp, scale=1.0 / 16.0), rd=[Rcs], wr=[Rcs])
                    buf, Rw = wload("w_in", l, 0, D, O_GQ + h * 128, 128)
                    pq, Rpq = psum()
                    proj_fm(pq, Rpq, buf, Rw, 0, 128, XN)
                    qe, Rqe = rB.get()
                    P.op("dve", lambda e: e.scalar_tensor_tensor(out=qe, in0=pq[:, 0:N], scalar=128.0 ** -0.5, in1=eb, op0=ALU.mult, op1=ALU.mult), rd=[Rpq, Reb], wr=[Rqe])
                    buf, Rw = wload("w_in", l, 0, D, O_GK + h * 128, 128)
                    pk, Rpk = psum()
                    proj_fm(pk, Rpk, buf, Rw, 0, 128, XN)
                    ke, Rke = rF.get()
                    P.op("dve", lambda e: e.tensor_tensor(out=ke, in0=pk[:, 0:N], in1=cs, op=ALU.mult), rd=[Rpk, Rcs], wr=[Rke])
                    keb, Rkeb = rB.get()
                    P.op("act", lambda e: e.activation(out=keb, in_=ke, func=AF.Copy), rd=[Rke], wr=[Rkeb])
                    buf, Rw = wload("w_in", l, 0, D, O_GV + h * 256, 256)
                    for c, (c0, ng, gl) in enumerate(chunks):
                        n = ng * gl
                        pv, Rpv = psum()
                        proj_tm(pv, Rpv, buf, Rw, 0, 256, XN, c0, n)
                        P.op("act", lambda e, c=c, n=n, pv=pv: e.activation(out=vtm[c][0][0:n, 0:256], in_=pv[0:n, 0:256], func=AF.Copy), rd=[Rpv], wr=[vtm[c][1]])
                    buf, Rw = wload("w_in", l, 0, D, O_GG + h * 256, 256)
                    sg = []
                    for ec in range(2):
                        pg, Rpg = psum()
                        proj_fm(pg, Rpg, buf, Rw, ec * 128, 128, XN)
                        s_, Rs_ = rB.get()
                        P.op("act", lambda e, pg=pg, s_=s_: e.activation(out=s_, in_=pg[:, 0:N], func=AF.Silu), rd=[Rpg], wr=[Rs_])
                        sg.append((s_, Rs_))
                    for c, (c0, ng, gl) in enumerate(chunks):
                        n = ng * gl
                        msk = mask128 if not is_sample else mask64
                        pa, Rpa = psum()
                        P.op("pe", lambda e, c0=c0, n=n, pa=pa: e.matmul(pa[0:n, 0:n], lhsT=keb[:, c0:c0 + n], rhs=qe[:, c0:c0 + n], start=True, stop=True), rd=[Rkeb, Rqe], wr=[Rpa])
                        AT, RAT = rb.get()
                        P.op("dve", lambda e, n=n, pa=pa, AT=AT, msk=msk: e.tensor_tensor(out=AT[0:n, 0:n], in0=pa[0:n, 0:n], in1=msk[0:n, 0:n], op=ALU.mult), rd=[Rpa, Rcst], wr=[RAT])
                        osb = []
                        for ec in range(2):
                            po, Rpo = psum()
                            nmm = 1 + ng
                            P.op("pe", lambda e, c=c, n=n, ec=ec, po=po, AT=AT: e.matmul(po[:, 0:n], lhsT=vtm[c][0][0:n, ec * 128:(ec + 1) * 128], rhs=AT[0:n, 0:n], start=True, stop=False),
                                 rd=[vtm[c][1], RAT], wr=[Rpo])
                            for g in range(ng):
                                P.op("pe", lambda e, g=g, ec=ec, po=po, c0=c0, gl=gl, ng=ng: e.matmul(po[:, g * gl:(g + 1) * gl], lhsT=Sbf[:, g, ec * 128:(ec + 1) * 128],
                                                                                                    rhs=qe[:, c0 + g * gl:c0 + (g + 1) * gl], start=False, stop=(g == ng - 1)),
                                     rd=[RSbf, Rqe], wr=[Rpo])
                            o_, Ro_ = rf.get()
                            P.op("act", lambda e, n=n, po=po, o_=o_: e.activation(out=o_[:, 0:n], in_=po[:, 0:n], func=AF.Copy), rd=[Rpo], wr=[Ro_])
                            osb.append((o_[:, 0:n], Ro_))
                        kd, Rkd = rf.get()
                        kd3 = kd[:, 0:n].rearrange("p (g t) -> p g t", t=gl)
                        ke3 = ke[:, c0:c0 + n].rearrange("p (g t) -> p g t", t=gl)
                        eb3 = eb[:, c0:c0 + n].rearrange("p (g t) -> p g t", t=gl)
                        P.op("dve", lambda e, kd3=kd3, ke3=ke3, eb3=eb3, ng=ng, gl=gl: e.tensor_tensor(out=kd3, in0=ke3, in1=eb3[:, :, gl - 1:gl].to_broadcast([128, ng, gl]), op=ALU.mult),
                             rd=[Rke, Reb], wr=[Rkd])
                        ptr, Rptr = psum()
                        P.op("pe", lambda e, n=n, kd=kd, ptr=ptr: e.transpose(out=ptr[0:n, 0:128], in_=kd[:, 0:n], identity=ident), rd=[Rkd, Rcst], wr=[Rptr])
                        km, Rkm = rkm.get()
                        if ng == 1:
                            P.op("act", lambda e, n=n, ptr=ptr, km=km: e.activation(out=km[0:n, 0, 0:128], in_=ptr[0:n, 0:128], func=AF.Copy), rd=[Rptr], wr=[Rkm])
                        else:
                            kt, Rkt = rb.get()
                            P.op("act", lambda e, n=n, ptr=ptr, kt=kt: e.activation(out=kt[0:n, 0:128], in_=ptr[0:n, 0:128], func=AF.Copy), rd=[Rptr], wr=[Rkt])
                            P.op("dve", lambda e, n=n, kt=kt, km=km, ng=ng: e.tensor_tensor(out=km[0:n, :, 0:128], in0=kt[0:n, 0:128].unsqueeze(1).to_broadcast([n, ng, 128]),
                                                                                           in1=gmask[0:n, 0:ng].unsqueeze(2).to_broadcast([n, ng, 128]), op=ALU.mult),
                                 rd=[Rkt, Rcst], wr=[Rkm])
                        for g in range(ng):
                            pu, Rpu = psum()
                            P.op("pe", lambda e, g=g, n=n, c=c, pu=pu, km=km: e.matmul(pu[:, 0:256], lhsT=km[0:n, g, 0:128], rhs=vtm[c][0][0:n, 0:256], start=True, stop=True),
                                 rd=[Rkm, vtm[c][1]], wr=[Rpu])
                            ebl = eb[:, c0 + (g + 1) * gl - 1:c0 + (g + 1) * gl]
                            P.op("dve", lambda e, g=g, pu=pu, ebl=ebl, S=S: e.scalar_tensor_tensor(out=S[:, g, :], in0=S[:, g, :], scalar=ebl, in1=pu[:, 0:256], op0=ALU.mult, op1=ALU.add),
                                 rd=[RSh, Reb, Rpu], wr=[RSh])
                        if c < NCH - 1:
                            P.op("act", lambda e, S=S: e.activation(out=Sbf, in_=S, func=AF.Copy), rd=[RSh], wr=[RSbf])
                        rs, Rrs = rmsnorm_rstd(osb)
                        for ec in range(2):
                            o_, Ro_ = osb[ec]
                            P.op("dve", lambda e, ec=ec, o_=o_: e.scalar_tensor_tensor(out=o_, in0=o_, scalar=pc("ggn", l, ec, 1), in1=rs, op0=ALU.mult, op1=ALU.mult),
                                 rd=[Ro_, Rprm, Rrs], wr=[Ro_])
                            yb, Ryb = YB[1][2 * h + ec]
                            P.op("dve", lambda e, ec=ec, o_=o_, yb=yb, c0=c0, n=n: e.tensor_tensor(out=yb[:, c0:c0 + n], in0=o_, in1=sg[ec][0][:, c0:c0 + n], op=ALU.mult),
                                 rd=[Ro_, sg[ec][1]], wr=[Ryb])
                    if is_sample:
                        P.dma("sp", lambda e, h=h, S=S: e.dma_start(out=o_gla[1][l, :, h].rearrange("g p e -> p g e"), in_=S), rd=[RSh], own=RSh, store=True)
                        if l == 0:
                            ada_pump(6)

            def mlstm_branch(l, Cfull, RC):
                buf, Rw = wload("w_if", l, 0, D, 0, 256)
                pi_, Rpi = psum()
                proj_fm(pi_, Rpi, buf, Rw, 0, 128, XN)
                pf_, Rpf = psum()
                proj_fm(pf_, Rpf, buf, Rw, 128, 128, XN)
                lfn, Rlfn = rF.get()
                P.op("act", lambda e: e.activation(out=lfn, in_=pf_[:, 0:N], func=AF.Exp, scale=-1.0, bias=drv[:, l, 20:21]), rd=[Rpf, Rdrv], wr=[Rlfn])
                P.op("act", lambda e: e.activation(out=lfn, in_=lfn, func=AF.Ln, bias=1.0), rd=[Rlfn], wr=[Rlfn])
                Fn, RFn = rF.get()
                gg, Rgg = rF.get()
                Mt, RMt = rF.get()
                if G == 1:
                    for (c0, ng, gl) in chunks:
                        ini = Fc[:, l, 0:1] if c0 == 0 else Fn[:, c0 - 1:c0]
                        P.op("dve", lambda e, c0=c0, gl=gl, ini=ini: e.tensor_tensor_scan(out=Fn[:, c0:c0 + gl], data0=ones_f[:, 0:gl], data1=lfn[:, c0:c0 + gl], initial=ini,
                                                                                          op0=ALU.mult, op1=ALU.add), rd=[Rlfn, RFc, Rcst, RFn], wr=[RFn])
                    P.op("dve", lambda e: e.scalar_tensor_tensor(out=gg, in0=pi_[:, 0:N], scalar=pc("mbif", l, 0, 1), in1=Fn, op0=ALU.add, op1=ALU.add), rd=[Rpi, Rprm, RFn], wr=[Rgg])
                    P.op("dve", lambda e: e.tensor_tensor_scan(out=Mt, data0=gg, data1=gg, initial=Mc[:, l, 0:1], op0=ALU.max, op1=ALU.max), rd=[Rgg, RMc], wr=[RMt])
                else:
                    l3, F3, g3, M3 = gt3(lfn), gt3(Fn), gt3(gg), gt3(Mt)
                    P.op("dve", lambda e: e.tensor_copy(out=F3[:, :, 0], in_=l3[:, :, 0]), rd=[Rlfn], wr=[RFn])
                    for t in range(1, T):
                        P.op("dve", lambda e, t=t: e.tensor_tensor(out=F3[:, :, t], in0=F3[:, :, t - 1], in1=l3[:, :, t], op=ALU.add), rd=[Rlfn, RFn], wr=[RFn])
                    P.op("dve", lambda e: e.scalar_tensor_tensor(out=gg, in0=pi_[:, 0:N], scalar=pc("mbif", l, 0, 1), in1=Fn, op0=ALU.add, op1=ALU.add), rd=[Rpi, Rprm, RFn], wr=[Rgg])
                    for t in range(T):
                        prev = Mc[:, l, :] if t == 0 else M3[:, :, t - 1]
                        P.op("dve", lambda e, t=t, prev=prev: e.tensor_tensor(out=M3[:, :, t], in0=g3[:, :, t], in1=prev, op=ALU.max), rd=[Rgg, RMt, RMc], wr=[RMt])
                Rrows = []
                for c, (c0, ng, gl) in enumerate(chunks):
                    n = ng * gl
                    R_, RR_ = rRs[c]
                    if G == 1:
                        mref = Mc[:, l, 0:1] if c == 0 else Mt[:, c0 - 1:c0]
                        mrd = RMc if c == 0 else RMt
                        P.op("dve", lambda e, c0=c0, n=n, R_=R_, mref=mref: e.tensor_scalar(out=R_[:, 0, 0:n], in0=gg[:, c0:c0 + n], scalar1=mref, scalar2=None, op0=ALU.subtract),
                             rd=[Rgg, mrd], wr=[RR_])
                        P.op("dve", lambda e, c0=c0, n=n, R_=R_, mref=mref: e.tensor_scalar(out=R_[:, 1, 0:n], in0=Mt[:, c0:c0 + n], scalar1=-1.0, scalar2=mref, op0=ALU.mult, op1=ALU.add),
                             rd=[RMt, mrd], wr=[RR_])
                    else:
                        mb = Mc[:, l, :].unsqueeze(2).to_broadcast([128, G, T])
                        P.op("dve", lambda e, R_=R_, mb=mb: e.tensor_tensor(out=gt3(R_[:, 0, 0:N]), in0=gt3(gg), in1=mb, op=ALU.subtract), rd=[Rgg, RMc], wr=[RR_])
                        P.op("dve", lambda e, R_=R_, mb=mb: e.tensor_tensor(out=gt3(R_[:, 1, 0:N]), in0=mb, in1=gt3(Mt), op=ALU.subtract), rd=[RMt, RMc], wr=[RR_])
                    if G == 1:
                        P.op("dve", lambda e, c0=c0, n=n, R_=R_, mref=mref: e.tensor_scalar(out=R_[:, 2, 0:n], in0=Fn[:, c0:c0 + n], scalar1=mref, scalar2=None, op0=ALU.subtract),
                             rd=[RFn, mrd], wr=[RR_])
                    else:
                        P.op("dve", lambda e, R_=R_, mb=mb: e.tensor_tensor(out=gt3(R_[:, 2, 0:N]), in0=gt3(Fn), in1=mb, op=ALU.subtract), rd=[RFn, RMc], wr=[RR_])
                    P.op("act", lambda e, R_=R_: e.activation(out=R_, in_=R_, func=AF.Exp), rd=[RR_], wr=[RR_])
                    Rrows.append((R_, RR_))
                P.op("dve", lambda e: e.tensor_tensor(out=mout[:, l, :], in0=gt3(Mt)[:, :, T - 1], in1=gt3(Fn)[:, :, T - 1], op=ALU.subtract), rd=[RMt, RFn], wr=[Rmout])
                if not is_sample:
                    P.op("dve", lambda e: e.tensor_copy(out=Fc[:, l, :], in_=Fn[:, N - 1:N]), rd=[RFn], wr=[RFc])
                    P.op("dve", lambda e: e.tensor_copy(out=Mc[:, l, :], in_=Mt[:, N - 1:N]), rd=[RMt], wr=[RMc])
                rcols = []
                for c, (c0, ng, gl) in enumerate(chunks):
                    n = ng * gl
                    R_, RR_ = Rrows[c]
                    pr, Rpr = psum()
                    P.op("pe", lambda e, n=n, R_=R_, pr=pr: e.matmul(pr[0:n, 0:4], lhsT=R_[:, 0, 0:n], rhs=E4, start=True, stop=True), rd=[RR_, Rcst], wr=[Rpr])
                    rc_, Rrc = rcolv[c]
                    P.op("act", lambda e, n=n, pr=pr, rc_=rc_: e.activation(out=rc_[0:n, :], in_=pr[0:n, 0:4], func=AF.Copy), rd=[Rpr], wr=[Rrc])
                    rcols.append((rc_, Rrc))
                for h in range(4):
                    if is_sample:
                        C, RCh = Cst[0]
                        for dc in range(2):
                            P.dma("sp", lambda e, h=h, dc=dc, C=C: e.dma_start(out=C[:, :, dc, 0:256], in_=i_sC[l, :, h, dc * 128:(dc + 1) * 128, :].rearrange("g p e -> p g e")), wr=[RCh], own=RCh)
                        P.op("dve", lambda e, h=h, C=C: e.tensor_copy(out=C[:, :, :, 256], in_=nin[:, l, :, h, :]), rd=[Rnin], wr=[RCh])
                    else:
                        C, RCh = Cfull[:, h:h + 1], RC

                    def refresh_shadow(C=C, RCh=RCh):
                        P.op("act", lambda e: e.activation(out=Cbf, in_=C, func=AF.Copy), rd=[RCh], wr=[RCbf])
                        P.op("dve", lambda e: e.tensor_copy(out=nbc, in_=C[:, :, :, 256:257].to_broadcast([128, C.shape[1], 2, 128])), rd=[RCh], wr=[Rnbc])
                    refresh_shadow()
                    buf, Rw = wload("w_in", l, 0, D, O_MX + h * 256, 256)
                    mqT, mkT, mxb_l, mc_l = [], [], [], []
                    for dc in range(2):
                        ch = 2 * h + dc
                        pt, Rp = psum()
                        proj_fm(pt, Rp, buf, Rw, dc * 128, 128, XN)
                        cb, Rcb = conv_in(l, 1, ch, pt, Rp)
                        mxb, Rmxb = rB.get()
                        P.op("act", lambda e, cb=cb, mxb=mxb: e.activation(out=gt3(mxb), in_=cb[:, :, 3:3 + T], func=AF.Copy), rd=[Rcb], wr=[Rmxb])
                        cv, Rcv = rF.get()
                        conv(l, "mcw", "mcb", ch, cb, Rcb, cv, Rcv)
                        mc_, Rmc_ = rB.get()
                        P.op("act", lambda e, cv=cv, mc_=mc_: e.activation(out=mc_, in_=cv, func=AF.Silu), rd=[Rcv], wr=[Rmc_])
                        wq, Rwq = sload(w_mq[l, ch])
                        wk, Rwk = sload(w_mk[l, ch])
                        wv, Rwv = sload(w_mv[l, ch])
                        pq, Rpq = psum()
                        P.op("pe", lambda e, pq=pq, wq=wq, mc_=mc_: e.matmul(pq[:, 0:N], lhsT=wq, rhs=mc_, start=True, stop=True), rd=[Rwq, Rmc_], wr=[Rpq])
                        q_, Rq_ = rB.get()
                        P.op("act", lambda e, pq=pq, q_=q_: e.activation(out=q_, in_=pq[:, 0:N], func=AF.Copy), rd=[Rpq], wr=[Rq_])
                        pk, Rpk = psum()
                        P.op("pe", lambda e, pk=pk, wk=wk, mc_=mc_: e.matmul(pk[:, 0:N], lhsT=wk, rhs=mc_, start=True, stop=True), rd=[Rwk, Rmc_], wr=[Rpk])
                        k_, Rk_ = rB.get()
                        P.op("act", lambda e, pk=pk, k_=k_: e.activation(out=k_, in_=pk[:, 0:N], func=AF.Copy, scale=1.0 / 16.0), rd=[Rpk], wr=[Rk_])
                        mqT.append((q_, Rq_)); mkT.append((k_, Rk_))
                        for c, (c0, ng, gl) in enumerate(chunks):
                            n = ng * gl
                            pkt, Rpkt = psum()
                            P.op("pe", lambda e, c0=c0, n=n, pkt=pkt, mc_=mc_, wk=wk: e.matmul(pkt[0:n, 0:128], lhsT=mc_[:, c0:c0 + n], rhs=wk, start=True, stop=True), rd=[Rmc_, Rwk], wr=[Rpkt])
                            P.op("act", lambda e, c=c, n=n, dc=dc, pkt=pkt: e.activation(out=ktm[c][0][0:n, dc * 128:(dc + 1) * 128], in_=pkt[0:n, 0:128], func=AF.Copy, scale=1.0 / 16.0),
                                 rd=[Rpkt], wr=[ktm[c][1]])
                            pvt, Rpvt = psum()
                            P.op("pe", lambda e, c0=c0, n=n, pvt=pvt, mxb=mxb, wv=wv: e.matmul(pvt[0:n, 0:128], lhsT=mxb[:, c0:c0 + n], rhs=wv, start=True, stop=True), rd=[Rmxb, Rwv], wr=[Rpvt])
                            P.op("act", lambda e, c=c, n=n, dc=dc, pvt=pvt: e.activation(out=mvt[c][0][0:n, dc * 128:(dc + 1) * 128], in_=pvt[0:n, 0:128], func=AF.Copy),
                                 rd=[Rpvt], wr=[mvt[c][1]])
                    buf, Rw = wload("w_in", l, 0, D, O_MO + h * 256, 256)
                    so = []
                    for ec in range(2):
                        pg, Rpg = psum()
                        proj_fm(pg, Rpg, buf, Rw, ec * 128, 128, XN)
                        s_, Rs_ = rB.get()
                        P.op("act", lambda e, pg=pg, s_=s_: e.activation(out=s_, in_=pg[:, 0:N], func=AF.Sigmoid), rd=[Rpg], wr=[Rs_])
                        so.append((s_, Rs_))
                    for c, (c0, ng, gl) in enumerate(chunks):
                        n = ng * gl
                        msk = mask128 if not is_sample else mask64
                        R_, RR_ = Rrows[c]
                        pb, Rpb = psum()
                        P.op("pe", lambda e, h=h, n=n, R_=R_, pb=pb: e.matmul(pb[:, 0:3 * CN], lhsT=sel[h], rhs=R_.rearrange("p a b -> p (a b)"), start=True, stop=True), rd=[RR_, Rcst], wr=[Rpb])
                        bc, Rbc = rbc.get()
                        P.op("act", lambda e, pb=pb, bc=bc: e.activation(out=bc.rearrange("p a b -> p (a b)"), in_=pb[:, 0:3 * CN], func=AF.Copy), rd=[Rpb], wr=[Rbc])
                        pst, Rpst = psum()
                        for dc in range(2):
                            kr, Rkr = rb.get()
                            P.op("dve", lambda e, dc=dc, c0=c0, n=n, kr=kr, bc=bc: e.tensor_tensor(out=kr[:, 0:n], in0=mkT[dc][0][:, c0:c0 + n], in1=bc[:, 0, 0:n], op=ALU.mult),
                                 rd=[mkT[dc][1], Rbc], wr=[Rkr])
                            P.op("pe", lambda e, dc=dc, c0=c0, n=n, kr=kr, pst=pst: e.matmul(pst[0:n, 0:n], lhsT=kr[:, 0:n], rhs=mqT[dc][0][:, c0:c0 + n], start=(dc == 0), stop=(dc == 1)),
                                 rd=[Rkr, mqT[dc][1]], wr=[Rpst])
                        ST, RST = rb.get()
                        P.op("dve", lambda e, n=n, pst=pst, ST=ST, msk=msk: e.tensor_tensor(out=ST[0:n, 0:n], in0=pst[0:n, 0:n], in1=msk[0:n, 0:n], op=ALU.mult), rd=[Rpst, Rcst], wr=[RST])
                        pd, Rpd = psum()
                        P.op("pe", lambda e, n=n, pd=pd, ST=ST: e.matmul(pd[:, 0:n], lhsT=ones_b[0:n, :], rhs=ST[0:n, 0:n], start=True, stop=False), rd=[Rob, RST], wr=[Rpd])
                        for g in range(ng):
                            for dc in range(2):
                                P.op("pe", lambda e, g=g, dc=dc, pd=pd, c0=c0, gl=gl, ng=ng: e.matmul(pd[:, g * gl:(g + 1) * gl], lhsT=nbc[:, g, dc, :], rhs=mqT[dc][0][:, c0 + g * gl:c0 + (g + 1) * gl],
                                                                                                    start=False, stop=(g == ng - 1 and dc == 1)), rd=[Rnbc, mqT[dc][1]], wr=[Rpd])
                        pns = []
                        for ec in range(2):
                            pn, Rpn = psum()
                            P.op("pe", lambda e, c=c, n=n, ec=ec, pn=pn, ST=ST: e.matmul(pn[:, 0:n], lhsT=mvt[c][0][0:n, ec * 128:(ec + 1) * 128], rhs=ST[0:n, 0:n], start=True, stop=False),
                                 rd=[mvt[c][1], RST], wr=[Rpn])
                            for g in range(ng):
                                for dc in range(2):
                                    P.op("pe", lambda e, g=g, dc=dc, ec=ec, pn=pn, c0=c0, gl=gl, ng=ng: e.matmul(pn[:, g * gl:(g + 1) * gl], lhsT=Cbf[:, g, dc, ec * 128:(ec + 1) * 128],
                                                                                                               rhs=mqT[dc][0][:, c0 + g * gl:c0 + (g + 1) * gl], start=False,
                                                                                                               stop=(g == ng - 1 and dc == 1)), rd=[RCbf, mqT[dc][1]], wr=[Rpn])
                            pns.append((pn, Rpn))
                        if ng == 1:
                            km, Rkm = rkm.get()
                            rc_, Rrc = rcols[c]
                            if ng == 1:
                                P.op("dve", lambda e, h=h, c=c, n=n, km=km, rc_=rc_: e.tensor_scalar(out=km[0:n, 0, :], in0=ktm[c][0][0:n, :], scalar1=rc_[0:n, h:h + 1], scalar2=None, op0=ALU.mult),
                                     rd=[ktm[c][1], Rrc], wr=[Rkm])
                            else:
                                kt, Rkt = rk256.get()
                                P.op("dve", lambda e, h=h, c=c, n=n, kt=kt, rc_=rc_: e.tensor_scalar(out=kt[0:n, 0:256], in0=ktm[c][0][0:n, :], scalar1=rc_[0:n, h:h + 1], scalar2=None, op0=ALU.mult),
                                     rd=[ktm[c][1], Rrc], wr=[Rkt])
                                P.op("dve", lambda e, n=n, kt=kt, km=km, ng=ng: e.tensor_tensor(out=km[0:n, :, :], in0=kt[0:n, 0:256].unsqueeze(1).to_broadcast([n, ng, 256]),
                                                                                               in1=gmask[0:n, 0:ng].unsqueeze(2).to_broadcast([n, ng, 256]), op=ALU.mult),
                                     rd=[Rkt, Rcst], wr=[Rkm])
                            for g in range(ng):
                                csc = bc[:, 1, (g + 1) * gl - 1:(g + 1) * gl]
                                for dc in range(2):
                                    pu, Rpu = psum()
                                    P.op("pe", lambda e, g=g, dc=dc, n=n, c=c, pu=pu, km=km: e.matmul(pu[:, 0:257], lhsT=km[0:n, g, dc * 128:(dc + 1) * 128], rhs=mvt[c][0][0:n, 0:257], start=True, stop=True),
                                         rd=[Rkm, mvt[c][1]], wr=[Rpu])
                                    tc_, Rtc = rtc.get()
                                    P.op("dve", lambda e, g=g, dc=dc, pu=pu, tc_=tc_, C=C: e.tensor_tensor(out=tc_, in0=C[:, g, dc, :], in1=pu[:, 0:257], op=ALU.add), rd=[RCh, Rpu], wr=[Rtc])
                                    P.op("act", lambda e, g=g, dc=dc, tc_=tc_, C=C, csc=csc: e.activation(out=C[:, g, dc, :], in_=tc_, func=AF.Copy, scale=csc), rd=[Rtc, Rbc], wr=[RCh])
                            if c < NCH - 1:
                                refresh_shadow()
                        t1, Rt1 = rf.get()
                        P.op("act", lambda e, n=n, pd=pd, t1=t1: e.activation(out=t1[:, 0:n], in_=pd[:, 0:n], func=AF.Abs), rd=[Rpd], wr=[Rt1])
                        P.op("dve", lambda e, n=n, t1=t1, bc=bc: e.tensor_tensor(out=t1[:, 0:n], in0=t1[:, 0:n], in1=bc[:, 2, 0:n], op=ALU.max), rd=[Rt1, Rbc], wr=[Rt1])
                        P.op("dve", lambda e, n=n, t1=t1: e.reciprocal(out=t1[:, 0:n], in_=t1[:, 0:n]), rd=[Rt1], wr=[Rt1])
                        hsb = []
                        for ec in range(2):
                            pn, Rpn = pns[ec]
                            h_, Rh_ = rf.get()
                            P.op("dve", lambda e, n=n, pn=pn, h_=h_, t1=t1: e.tensor_tensor(out=h_[:, 0:n], in0=pn[:, 0:n], in1=t1[:, 0:n], op=ALU.mult), rd=[Rpn, Rt1], wr=[Rh_])
                            hsb.append((h_[:, 0:n], Rh_))
                        rs, Rrs = rmsnorm_rstd(hsb)
                        for ec in range(2):
                            h_, Rh_ = hsb[ec]
                            P.op("dve", lambda e, ec=ec, h_=h_: e.scalar_tensor_tensor(out=h_, in0=h_, scalar=pc("mgn", l, ec, 1), in1=rs, op0=ALU.mult, op1=ALU.mult),
                                 rd=[Rh_, Rprm, Rrs], wr=[Rh_])
                            yb, Ryb = YB[2][2 * h + ec]
                            P.op("dve", lambda e, ec=ec, h_=h_, yb=yb, c0=c0, n=n: e.tensor_tensor(out=yb[:, c0:c0 + n], in0=h_, in1=so[ec][0][:, c0:c0 + n], op=ALU.mult),
                                 rd=[Rh_, so[ec][1]], wr=[Ryb])
                        if ng != 1:
                            km, Rkm = rkm.get()
                            rc_, Rrc = rcols[c]
                            if ng == 1:
                                P.op("dve", lambda e, h=h, c=c, n=n, km=km, rc_=rc_: e.tensor_scalar(out=km[0:n, 0, :], in0=ktm[c][0][0:n, :], scalar1=rc_[0:n, h:h + 1], scalar2=None, op0=ALU.mult),
                                     rd=[ktm[c][1], Rrc], wr=[Rkm])
                            else:
                                kt, Rkt = rk256.get()
                                P.op("dve", lambda e, h=h, c=c, n=n, kt=kt, rc_=rc_: e.tensor_scalar(out=kt[0:n, 0:256], in0=ktm[c][0][0:n, :], scalar1=rc_[0:n, h:h + 1], scalar2=None, op0=ALU.mult),
                                     rd=[ktm[c][1], Rrc], wr=[Rkt])
                                P.op("dve", lambda e, n=n, kt=kt, km=km, ng=ng: e.tensor_tensor(out=km[0:n, :, :], in0=kt[0:n, 0:256].unsqueeze(1).to_broadcast([n, ng, 256]),
                                                                                               in1=gmask[0:n, 0:ng].unsqueeze(2).to_broadcast([n, ng, 256]), op=ALU.mult),
                                     rd=[Rkt, Rcst], wr=[Rkm])
                            for g in range(ng):
                                csc = bc[:, 1, (g + 1) * gl - 1:(g + 1) * gl]
                                for dc in range(2):
                                    pu, Rpu = psum()
                                    P.op("pe", lambda e, g=g, dc=dc, n=n, c=c, pu=pu, km=km: e.matmul(pu[:, 0:257], lhsT=km[0:n, g, dc * 128:(dc + 1) * 128], rhs=mvt[c][0][0:n, 0:257], start=True, stop=True),
                                         rd=[Rkm, mvt[c][1]], wr=[Rpu])
                                    tc_, Rtc = rtc.get()
                                    P.op("dve", lambda e, g=g, dc=dc, pu=pu, tc_=tc_, C=C: e.tensor_tensor(out=tc_, in0=C[:, g, dc, :], in1=pu[:, 0:257], op=ALU.add), rd=[RCh, Rpu], wr=[Rtc])
                                    P.op("act", lambda e, g=g, dc=dc, tc_=tc_, C=C, csc=csc: e.activation(out=C[:, g, dc, :], in_=tc_, func=AF.Copy, scale=csc), rd=[Rtc, Rbc], wr=[RCh])
                            if c < NCH - 1:
                                refresh_shadow()
                    if is_sample:
                        for dc in range(2):
                            P.dma("sp", lambda e, h=h, dc=dc, C=C: e.dma_start(out=o_C[1][l, :, h, dc * 128:(dc + 1) * 128, :].rearrange("g p e -> p g e"), in_=C[:, :, dc, 0:256]), rd=[RCh], own=RCh, store=True)
                        P.op("dve", lambda e, h=h, C=C: e.tensor_copy(out=nout[:, l, :, h, :], in_=C[:, :, :, 256]), rd=[RCh], wr=[Rnout])
                        if l == 0:
                            ada_pump(6)

            def merge_out(l):
                for og in range(8):
                    gates = []
                    for b in range(3):
                        buf, Rw = wload("w_in", l, 0, D, O_MG + b * D + og * 256, 256)
                        for j in range(2):
                            pg, Rpg = psum()
                            proj_fm(pg, Rpg, buf, Rw, j * 128, 128, XN)
                            g_, Rg_ = rB.get()
                            P.op("act", lambda e, pg=pg, g_=g_: e.activation(out=g_, in_=pg[:, 0:N], func=AF.Sigmoid), rd=[Rpg], wr=[Rg_])
                            gates.append((g_, Rg_))
                    accs = [rF.get() for _ in range(2)]
                    for b in range(3):
                        buf, Rw = wload("w_br%d" % b, l, 0, 1024, og * 256, 256)
                        for j in range(2):
                            pp, Rpp = psum()
                            proj_fm(pp, Rpp, buf, Rw, j * 128, 128, YB[b])
                            g_, Rg_ = gates[b * 2 + j]
                            acc, Racc = accs[j]
                            if b == 0:
                                P.op("dve", lambda e, pp=pp, g_=g_, acc=acc: e.tensor_tensor(out=acc, in0=pp[:, 0:N], in1=g_, op=ALU.mult), rd=[Rpp, Rg_], wr=[Racc])
                            else:
                                t1, Rt1 = rF.get()
                                P.op("dve", lambda e, pp=pp, g_=g_, t1=t1: e.tensor_tensor(out=t1, in0=pp[:, 0:N], in1=g_, op=ALU.mult), rd=[Rpp, Rg_], wr=[Rt1])
                                if b == 1:
                                    P.op("dve", lambda e, t1=t1, acc=acc: e.tensor_tensor(out=acc, in0=acc, in1=t1, op=ALU.add), rd=[Rt1, Racc], wr=[Racc])
                                else:
                                    oc = og * 2 + j
                                    P.op("dve", lambda e, t1=t1, acc=acc, oc=oc: e.tensor_tensor(out=MG[oc][0], in0=acc, in1=t1, op=ALU.add), rd=[Rt1, Racc], wr=[MG[oc][1]])
                for og in range(8):
                    buf, Rw = wload("w_out", l, 0, D, og * 256, 256)
                    for j in range(2):
                        oc = og * 2 + j
                        po, Rpo = psum()
                        proj_fm(po, Rpo, buf, Rw, j * 128, 128, MG)
                        resid_add(l, 2, oc, po, Rpo)

            def mlp(l):
                P.alias_after(hg_res, mix_res)
                for hg in range(8):
                    hb = HG[hg % 2]
                    for t in range(4):
                        buf, Rw = wload("w_ff1", l, 0, D, hg * 1024 + t * 256, 256)
                        for j in range(2):
                            hc = t * 2 + j
                            ph, Rph = psum()
                            proj_fm(ph, Rph, buf, Rw, j * 128, 128, XN)
                            r_, Rr_ = rF.get()
                            P.op("act", lambda e, ph=ph, r_=r_: e.activation(out=r_, in_=ph[:, 0:N], func=AF.Relu), rd=[Rph], wr=[Rr_])
                            P.op("dve", lambda e, r_=r_, hc=hc, hb=hb: e.tensor_tensor(out=hb[hc][0], in0=r_, in1=r_, op=ALU.mult), rd=[Rr_], wr=[hb[hc][1]])
                    for og in range(8):
                        buf, Rw = wload("w_ff2", l, hg * 1024, 1024, og * 256, 256)
                        for j in range(2):
                            oc = og * 2 + j
                            po, Rpo = psum()
                            proj_fm(po, Rpo, buf, Rw, j * 128, 128, hb)
                            resid_add(l, 5, oc, po, Rpo)
                P.alias_after(mix_res, hg_res)


            for ti in range(ntiles):
                src = i_xs if is_sample else i_xp[:, :, ti * N:(ti + 1) * N]
                for kc in range(KC):
                    P.dma("sp", lambda e, kc=kc, src=src: e.dma_start(out=X[kc][0], in_=src[:, kc, :]), wr=[X[kc][1]], own=X[kc][1])
                for l in range(NL):
                    if l == 1 and is_sample:
                        ada_pump(None)
                    norm_mod(l, 0)
                    lru_branch(l)
                    gla_branch(l, None if is_sample else Sst[l][0], None if is_sample else Sst[l][1])
                    mlstm_branch(l, None if is_sample else Cst[l][0], None if is_sample else Cst[l][1])
                    merge_out(l)
                    norm_mod(l, 1)
                    mlp(l)
                rs, Rrs = rmsnorm_rstd(X, rstdN)
                dst = o_ys if is_sample else o_yp[:, :, ti * N:(ti + 1) * N]
                for kc in range(KC):
                    yt, Ryt = rF.get()
                    P.op("dve", lambda e, kc=kc, yt=yt: e.scalar_tensor_tensor(out=yt, in0=X[kc][0], scalar=pc("gf", 0, kc, 1), in1=rs, op0=ALU.mult, op1=ALU.mult),
                         rd=[X[kc][1], Rprm, Rrs], wr=[Ryt])
                    P.dma("sp", lambda e, kc=kc, yt=yt, dst=dst: e.dma_start(out=dst[:, kc, :], in_=yt), rd=[Ryt], own=Ryt, store=True)
            P.dma("sp", lambda e: e.dma_start(out=o_hst[oi], in_=hst), rd=[Rhst], own=Rhst, store=True)
            P.dma("sp", lambda e: e.dma_start(out=o_hist[oi], in_=hist), rd=[Rhist], own=Rhist, store=True)
            P.dma("sp", lambda e: e.dma_start(out=o_m[oi], in_=mout), rd=[Rmout], own=Rmout, store=True)
            if not is_sample:
                for l in range(NL):
                    S, RS = Sst[l]
                    C, RC = Cst[l]
                    P.dma("sp", lambda e, l=l, S=S: e.dma_start(out=o_gla[0][l, 0].rearrange("h p e -> p h e"), in_=S), rd=[RS], own=RS, store=True)
                    for dc in range(2):
                        P.dma("sp", lambda e, l=l, dc=dc, C=C: e.dma_start(out=o_C[0][l, 0, :, dc * 128:(dc + 1) * 128, :].rearrange("h p e -> p h e"), in_=C[:, :, dc, 0:256]), rd=[RC], own=RC, store=True)
                    P.op("dve", lambda e, l=l, C=C: e.tensor_copy(out=nout[:, l, 0], in_=C[:, :, :, 256]), rd=[RC], wr=[Rnout])
            P.dma("sp", lambda e: e.dma_start(out=o_n[oi], in_=nout), rd=[Rnout], own=Rnout, store=True)
            return allres

        prev = setup_res
        if RUN_SAMPLE:
            prev = prev + run_phase(True, prev)
        if N_PROMPT_TILES > 0:
            run_phase(False, prev + [Rmod, RmodA, Rctb])
        P.emit()
    return nc, WT_LIST


def _fm(a):
    a = np.asarray(a, dtype=np.float32)
    lead = a.shape[:-1]
    a = a.reshape(-1, KC, 128)
    a = np.transpose(a, (2, 1, 0))
    return np.ascontiguousarray(a.reshape(128, KC, *lead))


def _blockdiag(w):
    out = np.zeros((NL, 8, 128, 128), np.float32)
    for b in range(32):
        out[:, :, 4 * b:4 * b + 4, 4 * b:4 * b + 4] = w.reshape(NL, 8, 32, 4, 4)[:, :, b]
    return out


def _consts():
    c = np.zeros((128, CONST_COLS), np.float32)
    o = 0
    j = np.arange(128)
    c[:, o:o + 128] = (j[None, :] >= j[:, None]); o += 128
    c[:, o:o + 128] = 1.0; o += 128
    j64 = np.arange(64)
    m = (j64[None, :] >= j64[:, None]) & ((j64[None, :] // TS) == (j64[:, None] // TS))
    c[0:64, o:o + 64] = m; o += 64
    c[0:64, o:o + 16] = (j64[:, None] // TS) == np.arange(16)[None, :]; o += 16
    for h in range(4):
        c[32 * h, o:o + 128] = 1.0; o += 128
    for h in range(4):
        c[32 * h, o + h] = 1.0
    o += 4
    c[:, o:o + 128] = np.eye(128, dtype=np.float32); o += 128
    return c


_NC_CACHE = {}
_HOOK = {}


def kernel(x_prompt, x_sample, c_prompt, c_sample, state_lru_h, state_lru_conv, state_gla,
           state_mlstm_C, state_mlstm_n, state_mlstm_m, state_mlstm_conv,
           w_ada, b_ada, g_norm1, g_norm2, w_in, lru_conv_w, lru_conv_b, lru_w_a, lru_b_a,
           lru_w_x, lru_b_x, lru_lam, gla_w_g2, gla_b_g, gla_g_norm, ml_conv_w, ml_conv_b,
           ml_w_q, ml_w_k, ml_w_v, ml_b_if, ml_g_norm, w_br_lru, w_br_gla, w_br_ml, w_out,
           w_ff1, w_ff2, g_final):
    f = lambda a: np.ascontiguousarray(np.asarray(a, dtype=np.float32))
    poff, NPRM = prm_layout()
    prm = np.zeros((128, NPRM), np.float32)

    def put(name, l, arr2d):
        o = poff[(name, l)]
        prm[:, o:o + arr2d.shape[1]] = arr2d

    def cols(v, n):
        return np.asarray(v, np.float32).reshape(n, 128).T

    for l in range(NL):
        put("g1", l, cols(g_norm1[l], 16))
        put("g2", l, cols(g_norm2[l], 16))
        put("bada", l, cols(b_ada[l], 96))
        put("lcw", l, np.transpose(np.asarray(lru_conv_w[l], np.float32).reshape(4, 8, 128), (2, 1, 0)).reshape(128, 32))
        put("lcb", l, cols(lru_conv_b[l], 8))
        put("lba", l, cols(lru_b_a[l], 8))
        put("lbx", l, cols(lru_b_x[l], 8))
        put("lam", l, cols(lru_lam[l], 8))
        put("gbg", l, cols(gla_b_g[l], 4))
        put("ggn", l, cols(gla_g_norm[l], 2))
        put("mcw", l, np.transpose(np.asarray(ml_conv_w[l], np.float32).reshape(4, 8, 128), (2, 1, 0)).reshape(128, 32))
        put("mcb", l, cols(ml_conv_b[l], 8))
        bif = np.zeros((128, 2), np.float32)
        bif[:, 1] = 50.0
        for h in range(4):
            bif[32 * h, 0] = ml_b_if[l][h]
            bif[32 * h, 1] = ml_b_if[l][4 + h]
        put("mbif", l, bif)
        put("mgn", l, cols(ml_g_norm[l], 2))
    put("gf", 0, cols(g_final, 16))
    cst = _consts()
    w_in = f(w_in)
    wif = np.zeros((NL, D, 256), np.float32)
    for h in range(4):
        wif[:, :, 32 * h] = w_in[:, :, O_MIF + h]
        wif[:, :, 128 + 32 * h] = w_in[:, :, O_MIF + 4 + h]
    if "nc" not in _NC_CACHE:
        _NC_CACHE["nc"] = build_program()
    nc, wt_list = _NC_CACHE["nc"]
    srcs = {"w_ada": f(w_ada), "w_in": w_in, "w_if": wif, "w_br0": f(w_br_lru), "w_br1": f(w_br_gla), "w_br2": f(w_br_ml),
            "w_out": f(w_out), "w_ff1": f(w_ff1), "w_ff2": f(w_ff2)}
    w_all = np.zeros((NT_MAX, 128, 4096), np.float32)
    for idx, (name, l, r0, nr, c0, ncn) in enumerate(wt_list):
        kcn = nr // 128
        blk = srcs[name][l][r0:r0 + nr, c0:c0 + ncn]
        w_all[idx, :, :kcn * ncn] = blk.reshape(kcn, 128, ncn).transpose(1, 0, 2).reshape(128, kcn * ncn)
    del srcs
    shared = dict(prm=prm, cst=cst, w_all=w_all, lru_w_a=f(lru_w_a), lru_w_x=f(lru_w_x),
                  gla_w_g2=f(gla_w_g2), ml_wq=_blockdiag(f(ml_w_q)), ml_wk=_blockdiag(f(ml_w_k)), ml_wv=_blockdiag(f(ml_w_v)))
    x_prompt = f(x_prompt); x_sample = f(x_sample)
    in_maps = []
    xp_fm = [_fm(x_prompt[s]) for s in range(4)]
    for c in range(NCORE):
        s = c % 4
        sl = slice(GS * c, GS * (c + 1))
        m = dict(shared)
        m["xp"] = xp_fm[s]
        m["xs"] = _fm(x_sample[sl]).reshape(128, KC, GS * TS)
        ct = np.concatenate([np.asarray(c_prompt, np.float32)[s:s + 1], np.asarray(c_sample, np.float32)[sl]], 0)
        m["ct"] = _fm(ct)
        m["s_hst"] = np.ascontiguousarray(np.transpose(f(state_lru_h)[:, sl].reshape(NL, GS, 8, 128), (3, 0, 2, 1)))
        hl = np.transpose(f(state_lru_conv)[:, sl].reshape(NL, GS, 3, 8, 128), (4, 0, 3, 1, 2))
        hm = np.transpose(f(state_mlstm_conv)[:, sl].reshape(NL, GS, 3, 8, 128), (4, 0, 3, 1, 2))
        m["s_hist"] = np.ascontiguousarray(np.stack([hl, hm], axis=2))
        mm = np.zeros((128, NL, GS), np.float32)
        sm = f(state_mlstm_m)[:, sl]
        for h in range(4):
            mm[32 * h] = sm[:, :, h]
        m["s_m"] = mm
        m["s_n"] = np.ascontiguousarray(np.transpose(f(state_mlstm_n)[:, sl].reshape(NL, GS, 4, 2, 128), (4, 0, 1, 2, 3)))
        m["s_gla"] = np.ascontiguousarray(f(state_gla)[:, sl])
        m["s_C"] = np.ascontiguousarray(f(state_mlstm_C)[:, sl])
        in_maps.append(m)
    if _HOOK.get("in_maps_only"):
        return in_maps
    res = run_bass_kernel_spmd(nc, in_maps, core_ids=list(range(NCORE)))
    R = res.results

    def unfm(a):
        return np.ascontiguousarray(np.transpose(a, (2, 1, 0)).reshape(a.shape[2], D))

    y_prompt = np.stack([unfm(R[s]["o_yp"]) for s in range(4)], 0)
    y_sample = np.concatenate([unfm(R[c]["o_ys"]).reshape(GS, TS, D) for c in range(NCORE)], 0)

    def gather(key_idx, cores, fn):
        return np.concatenate([fn(R[c], key_idx) for c in cores], axis=1)

    def st_lru_h(r, i):
        return np.transpose(r["o_hst%d" % i], (1, 3, 2, 0)).reshape(NL, -1, 1024)

    def st_conv(cv):
        def fn(r, i):
            a = r["o_hist%d" % i][:, :, cv]
            return np.transpose(a, (1, 3, 4, 2, 0)).reshape(NL, a.shape[3], 3, 1024)
        return fn

    def st_m(r, i):
        return np.stack([r["o_m%d" % i][32 * h] for h in range(4)], axis=-1)

    def st_n(r, i):
        return np.transpose(r["o_n%d" % i], (1, 2, 3, 4, 0)).reshape(NL, -1, 4, 256)

    outs = [y_prompt, y_sample]
    for i, cores in ((0, range(4)), (1, range(NCORE))):
        outs += [gather(i, cores, st_lru_h), gather(i, cores, st_conv(0)),
                 gather(i, cores, lambda r, i: r["o_gla%d" % i]), gather(i, cores, lambda r, i: r["o_C%d" % i]),
                 gather(i, cores, st_n), gather(i, cores, st_m), gather(i, cores, st_conv(1))]
    return tuple(np.ascontiguousarray(o.astype(np.float32)) for o in outs)
```
